# Optimizing a Trainium2 kernel written in Bass

```python
import math
import numpy as np
import jax
import jax.numpy as jnp
from jax import lax

D_MODEL = 1024
BATCH = 16
SEQ = 4096
DEPTH = 4

CTX_LEN = 256
GRID_W = 64
EPS = 1e-6

A_WIDTH = 512
A_CONV = 3
DN_HEADS = 4
DN_DK = 128
DN_DV = 128
DN_CONV = 3
DN_CHUNK = 64
GLA_HEADS = 4
GLA_DK = 64
GLA_DV = 128
GLA_RANK = 16
GLA_GATE_NORM = 16.0
GLA_CHUNK = 16

N_BRANCH = 3
N_DIR = 2

IN_SPLITS = (
    A_WIDTH, A_WIDTH, A_WIDTH, A_WIDTH,
    DN_HEADS * (2 * DN_DK + DN_DV), DN_HEADS * DN_DV,
    N_DIR * DN_HEADS, N_DIR * DN_HEADS,
    GLA_HEADS * GLA_DK, GLA_HEADS * GLA_DK,
    GLA_HEADS * GLA_DV, GLA_HEADS * GLA_DV,
    N_DIR * GLA_RANK,
    N_BRANCH * D_MODEL,
)
IN_WIDTH = int(sum(IN_SPLITS))
SPLIT_IDX = tuple(int(i) for i in np.cumsum(IN_SPLITS)[:-1])
F32 = jnp.float32

kernel_name = "hybrid_conv_deltanet_gla_prefix_dit"


def rms_norm(x, w):
    xf = x.astype(F32)
    y = xf * lax.rsqrt(jnp.mean(xf * xf, axis=-1, keepdims=True) + EPS)
    return (y * w.astype(F32)).astype(x.dtype)


def l2_normalize(x):
    xf = x.astype(F32)
    return (xf * lax.rsqrt(jnp.sum(xf * xf, axis=-1, keepdims=True) + EPS)).astype(x.dtype)


def dw_conv_centred(x, w):
    k_w, ch = w.shape
    rhs = w[:, None, :].astype(x.dtype)
    return lax.conv_general_dilated(x, rhs, window_strides=(1,), padding=[(k_w // 2, k_w // 2)],
                                    dimension_numbers=("NWC", "WIO", "NWC"), feature_group_count=ch)


def row_conv(u, w):
    b_, l_, ch = u.shape
    rows = l_ // GRID_W
    return dw_conv_centred(u.reshape(b_ * rows, GRID_W, ch), w).reshape(b_, l_, ch)


def to_col_major(t):
    b_, l_ = t.shape[:2]
    rows = l_ // GRID_W
    return jnp.swapaxes(t.reshape((b_, rows, GRID_W) + t.shape[2:]), 1, 2).reshape(t.shape)


def to_row_major(t):
    b_, l_ = t.shape[:2]
    rows = l_ // GRID_W
    return jnp.swapaxes(t.reshape((b_, GRID_W, rows) + t.shape[2:]), 1, 2).reshape(t.shape)


def to_chunks(t, chunk):
    b_, l_ = t.shape[:2]
    t = t.astype(F32).reshape((b_, l_ // chunk, chunk) + t.shape[2:])
    return jnp.swapaxes(t, 2, 3)


def from_chunks(t):
    t = jnp.swapaxes(t, 2, 3)
    return t.reshape((t.shape[0], t.shape[1] * t.shape[2]) + t.shape[3:])


def gated_delta_chunked(q, k, v, g, beta, s0):
    out_dtype = v.dtype
    c_ = DN_CHUNK
    q, k, v, g, beta = (to_chunks(t, c_) for t in (q, k, v, g, beta))
    gc = jnp.cumsum(g, axis=3)
    tri = jnp.tril(jnp.ones((c_, c_), bool))
    tri_strict = jnp.tril(jnp.ones((c_, c_), bool), -1)
    decay = jnp.exp(jnp.where(tri, gc[..., :, None] - gc[..., None, :], -jnp.inf))
    kb = k * beta[..., None]
    vb = v * beta[..., None]
    a_kk = jnp.where(tri_strict, jnp.einsum("bnhid,bnhjd->bnhij", kb, k) * decay, 0.0)
    t_mat = jnp.eye(c_, dtype=F32) + a_kk
    u = lax.linalg.triangular_solve(t_mat, vb, left_side=True, lower=True, unit_diagonal=True)
    w = lax.linalg.triangular_solve(t_mat, kb * jnp.exp(gc)[..., None], left_side=True, lower=True,
                                    unit_diagonal=True)
    a_qk = jnp.einsum("bnhid,bnhjd->bnhij", q, k) * decay
    q_g = q * jnp.exp(gc)[..., None]
    k_g = k * jnp.exp(gc[..., -1:] - gc)[..., None]
    g_last = jnp.exp(gc[..., -1])

    def step(s, inp):
        u_n, w_n, aqk_n, qg_n, kg_n, gl_n = inp
        v_new = u_n - jnp.einsum("bhcd,bhde->bhce", w_n, s)
        o_n = jnp.einsum("bhcd,bhde->bhce", qg_n, s) + jnp.einsum("bhij,bhje->bhie", aqk_n, v_new)
        s = s * gl_n[..., None, None] + jnp.einsum("bhcd,bhce->bhde", kg_n, v_new)
        return s, o_n

    xs = tuple(jnp.moveaxis(t, 1, 0) for t in (u, w, a_qk, q_g, k_g, g_last))
    s_fin, o = lax.scan(step, s0.astype(F32), xs)
    o = from_chunks(jnp.moveaxis(o, 0, 1))
    return o.astype(out_dtype), s_fin


def gla_chunked(q, k, v, gk, s0):
    out_dtype = v.dtype
    c_ = GLA_CHUNK
    q, k, v, gk = (to_chunks(t, c_) for t in (q, k, v, gk))
    gc = jnp.cumsum(gk, axis=3)
    tri = jnp.tril(jnp.ones((c_, c_), bool))
    q_g = q * jnp.exp(gc)
    k_inv = k * jnp.exp(-gc)
    a = jnp.where(tri, jnp.einsum("bnhid,bnhjd->bnhij", q_g, k_inv), 0.0)
    o_intra = jnp.einsum("bnhij,bnhje->bnhie", a, v)
    k_g = k * jnp.exp(gc[..., -1:, :] - gc)
    g_last = jnp.exp(gc[..., -1, :])

    def step(s, inp):
        qg_n, kg_n, v_n, gl_n = inp
        o_n = jnp.einsum("bhcd,bhde->bhce", qg_n, s)
        s = s * gl_n[..., None] + jnp.einsum("bhcd,bhce->bhde", kg_n, v_n)
        return s, o_n

    xs = tuple(jnp.moveaxis(t, 1, 0) for t in (q_g, k_g, v, g_last))
    s_fin, o_inter = lax.scan(step, s0.astype(F32), xs)
    o = from_chunks(jnp.moveaxis(o_inter, 0, 1) + o_intra)
    return o.astype(out_dtype), s_fin


def bidirectional(run, lat_fwd, ctx_fwd, lat_bwd, ctx_bwd, s_zero):
    def flip(t):
        return jnp.flip(t, axis=1)
    o_ctx_f, s_ctx_f = run(*ctx_fwd, s_zero)
    o_lat_f, _ = run(*lat_fwd, s_ctx_f)
    o_ctx_b, s_ctx_b = run(*map(flip, ctx_bwd), s_zero)
    o_lat_b, _ = run(*map(flip, lat_bwd), s_ctx_b)
    return o_lat_f + flip(o_lat_b), o_ctx_f + flip(o_ctx_b)


def pick_direction(args, n_shared, d):
    return args[:n_shared] + tuple(t[:, :, d] for t in args[n_shared:])


def gated_head_norm(o, z, w):
    b_, l_, h_, dv = o.shape
    return (rms_norm(o, w) * jax.nn.silu(z.reshape(b_, l_, h_, dv))).reshape(b_, l_, h_ * dv)


def short_conv_branch(b_gate, c_gate, x_in, z, conv_w, on_grid):
    u = c_gate * x_in
    y = row_conv(u, conv_w) if on_grid else dw_conv_centred(u, conv_w)
    return b_gate * y * jax.nn.silu(z)


def deltanet_prep(qkv, a, b, conv_w, a_log, dt_bias):
    b_, l_ = qkv.shape[:2]
    qkv = jax.nn.silu(dw_conv_centred(qkv, conv_w))
    q, k, v = jnp.split(qkv, [DN_HEADS * DN_DK, 2 * DN_HEADS * DN_DK], axis=-1)
    q = l2_normalize(q.reshape(b_, l_, DN_HEADS, DN_DK)) * (DN_DK ** -0.5)
    k = l2_normalize(k.reshape(b_, l_, DN_HEADS, DN_DK))
    v = v.reshape(b_, l_, DN_HEADS, DN_DV)
    a = a.astype(F32).reshape(b_, l_, N_DIR, DN_HEADS)
    g = -jnp.exp(a_log.astype(F32)) * jax.nn.softplus(a + dt_bias.astype(F32))
    beta = jax.nn.sigmoid(b.astype(F32).reshape(b_, l_, N_DIR, DN_HEADS))
    return (q, k, v, g, beta)


def gla_prep(q, k, v, r, w_r2, b_r):
    b_, l_ = q.shape[:2]
    q = q.reshape(b_, l_, GLA_HEADS, GLA_DK) * (GLA_DK ** -0.5)
    k = k.reshape(b_, l_, GLA_HEADS, GLA_DK)
    v = v.reshape(b_, l_, GLA_HEADS, GLA_DV)
    r = r.reshape(b_, l_, N_DIR, GLA_RANK)
    logit = jnp.einsum("bler,erk->blek", r, w_r2) + b_r
    gk = jax.nn.log_sigmoid(logit.astype(F32)) / GLA_GATE_NORM
    return (q, k, v, gk.reshape(b_, l_, N_DIR, GLA_HEADS, GLA_DK))


def merge_branches(y_a, y_b, y_c, logits, b_merge, w_pa, w_pb, w_pc, w_o):
    g_a, g_b, g_c = jnp.split(jax.nn.sigmoid(logits + b_merge), N_BRANCH, axis=-1)
    m = g_a * (y_a @ w_pa) + g_b * (y_b @ w_pb) + g_c * (y_c @ w_pc)
    return m @ w_o


def hybrid_layer(x, ctx, c, c_ctx, w_mod, b_mod, norm_w, w_in, b_merge, conv_a, conv_dn,
                 dn_a_log, dn_dt_bias, dn_onorm, gla_w_r2, gla_b_r, gla_onorm,
                 w_pa, w_pb, w_pc, w_o, need_ctx):
    b_ = x.shape[0]
    shift, scale, gate = jnp.split(jax.nn.silu(c) @ w_mod + b_mod, 3, axis=-1)
    shift_c, scale_c, gate_c = jnp.split(jax.nn.silu(c_ctx) @ w_mod + b_mod, 3, axis=-1)
    h = rms_norm(x, norm_w) * (1 + scale[:, None]) + shift[:, None]
    hc = rms_norm(ctx, norm_w) * (1 + scale_c) + shift_c
    (a_b, a_c, a_x, a_z, dn_qkv, dn_z, dn_a, dn_b,
     g_q, g_k, g_v, g_z, g_r, mg) = jnp.split(h @ w_in, SPLIT_IDX, axis=-1)
    (ca_b, ca_c, ca_x, ca_z, cdn_qkv, cdn_z, cdn_a, cdn_b,
     cg_q, cg_k, cg_v, cg_z, cg_r, cmg) = jnp.split(hc @ w_in, SPLIT_IDX, axis=-1)

    y_a = short_conv_branch(a_b, a_c, a_x, a_z, conv_a, on_grid=True)

    lat_b = deltanet_prep(dn_qkv, dn_a, dn_b, conv_dn, dn_a_log, dn_dt_bias)
    ctx_b = deltanet_prep(cdn_qkv, cdn_a, cdn_b, conv_dn, dn_a_log, dn_dt_bias)
    s0_b = jnp.zeros((b_, DN_HEADS, DN_DK, DN_DV), F32)
    o_b, o_bc = bidirectional(gated_delta_chunked,
                              pick_direction(lat_b, 3, 0), pick_direction(ctx_b, 3, 0),
                              pick_direction(lat_b, 3, 1), pick_direction(ctx_b, 3, 1), s0_b)
    y_b = gated_head_norm(o_b, dn_z, dn_onorm)

    lat_c = gla_prep(to_col_major(g_q), to_col_major(g_k), to_col_major(g_v), to_col_major(g_r),
                     gla_w_r2, gla_b_r)
    ctx_c = gla_prep(cg_q, cg_k, cg_v, cg_r, gla_w_r2, gla_b_r)
    s0_c = jnp.zeros((b_, GLA_HEADS, GLA_DK, GLA_DV), F32)
    o_c_cm, o_cc = bidirectional(gla_chunked,
                                 pick_direction(lat_c, 3, 0), pick_direction(ctx_c, 3, 0),
                                 pick_direction(lat_c, 3, 1), pick_direction(ctx_c, 3, 1), s0_c)
    y_c = gated_head_norm(to_row_major(o_c_cm), g_z, gla_onorm)

    x = x + gate[:, None] * merge_branches(y_a, y_b, y_c, mg, b_merge, w_pa, w_pb, w_pc, w_o)
    if need_ctx:
        y_ac = short_conv_branch(ca_b, ca_c, ca_x, ca_z, conv_a, on_grid=False)
        y_bc = gated_head_norm(o_bc, cdn_z, dn_onorm)
        y_cc = gated_head_norm(o_cc, cg_z, gla_onorm)
        ctx = ctx + gate_c * merge_branches(y_ac, y_bc, y_cc, cmg, b_merge, w_pa, w_pb, w_pc, w_o)
    return x, ctx


def setup_inputs(seed: int = 0) -> dict:
    key = jax.random.key(seed)
    ks = jax.random.split(key, 24)
    d = D_MODEL
    nrm = jax.random.normal
    x = nrm(ks[0], (BATCH, SEQ, d), F32)
    c = nrm(ks[1], (BATCH, d), F32)
    ctx = nrm(ks[2], (BATCH, CTX_LEN, d), F32)
    c_ctx = nrm(ks[3], (d,), F32)
    w_mod = nrm(ks[4], (DEPTH, d, 3 * d), F32) * (0.5 * d ** -0.5)
    b_mod = nrm(ks[5], (DEPTH, 3 * d), F32) * 0.02
    norm_w = 1.0 + 0.02 * nrm(ks[6], (DEPTH, d), F32)
    w_in = nrm(ks[7], (DEPTH, d, IN_WIDTH), F32) * d ** -0.5
    b_merge = nrm(ks[8], (DEPTH, N_BRANCH * d), F32) * 0.02
    conv_a = nrm(ks[9], (DEPTH, A_CONV, A_WIDTH), F32) * A_CONV ** -0.5
    conv_dn = nrm(ks[10], (DEPTH, DN_CONV, DN_HEADS * (2 * DN_DK + DN_DV)), F32) * DN_CONV ** -0.5
    dn_a_log = jnp.log(jax.random.uniform(ks[11], (DEPTH, N_DIR, DN_HEADS), F32, 1.0, 16.0))
    dt = jnp.exp(jax.random.uniform(ks[12], (DEPTH, N_DIR, DN_HEADS), F32,
                                    math.log(1e-3), math.log(1e-1)))
    dn_dt_bias = dt + jnp.log(-jnp.expm1(-dt))
    dn_onorm = 1.0 + 0.02 * nrm(ks[13], (DEPTH, DN_DV), F32)
    gla_w_r2 = nrm(ks[14], (DEPTH, N_DIR, GLA_RANK, GLA_HEADS * GLA_DK), F32) * GLA_RANK ** -0.5
    gla_b_r = nrm(ks[15], (DEPTH, N_DIR, GLA_HEADS * GLA_DK), F32) * 0.1
    gla_onorm = 1.0 + 0.02 * nrm(ks[16], (DEPTH, GLA_DV), F32)
    w_pa = nrm(ks[17], (DEPTH, A_WIDTH, d), F32) * A_WIDTH ** -0.5
    w_pb = nrm(ks[18], (DEPTH, DN_HEADS * DN_DV, d), F32) * (DN_HEADS * DN_DV) ** -0.5
    w_pc = nrm(ks[19], (DEPTH, GLA_HEADS * GLA_DV, d), F32) * (GLA_HEADS * GLA_DV) ** -0.5
    w_o = nrm(ks[20], (DEPTH, d, d), F32) * d ** -0.5
    final_norm_w = 1.0 + 0.02 * nrm(ks[21], (d,), F32)
    return {"x": x, "c": c, "ctx": ctx, "c_ctx": c_ctx, "w_mod": w_mod, "b_mod": b_mod,
            "norm_w": norm_w, "w_in": w_in, "b_merge": b_merge, "conv_a": conv_a,
            "conv_dn": conv_dn, "dn_a_log": dn_a_log, "dn_dt_bias": dn_dt_bias,
            "dn_onorm": dn_onorm, "gla_w_r2": gla_w_r2, "gla_b_r": gla_b_r,
            "gla_onorm": gla_onorm, "w_pa": w_pa, "w_pb": w_pb, "w_pc": w_pc, "w_o": w_o,
            "final_norm_w": final_norm_w}


def reference(x, c, ctx, c_ctx, w_mod, b_mod, norm_w, w_in, b_merge, conv_a, conv_dn,
              dn_a_log, dn_dt_bias, dn_onorm, gla_w_r2, gla_b_r, gla_onorm,
              w_pa, w_pb, w_pc, w_o, final_norm_w):
    for l in range(DEPTH):
        x, ctx = hybrid_layer(x, ctx, c, c_ctx, w_mod[l], b_mod[l], norm_w[l], w_in[l], b_merge[l],
                              conv_a[l], conv_dn[l], dn_a_log[l], dn_dt_bias[l], dn_onorm[l],
                              gla_w_r2[l], gla_b_r[l], gla_onorm[l], w_pa[l], w_pb[l], w_pc[l],
                              w_o[l], need_ctx=(l < DEPTH - 1))
    return rms_norm(x, final_norm_w)
```

```python
import contextlib
import os
import numpy as np
import concourse.bass as bass
import concourse.mybir as mybir
from concourse.bass_utils import run_bass_kernel_spmd

F32 = mybir.dt.float32
BF16 = mybir.dt.bfloat16
ALU = mybir.AluOpType
AF = mybir.ActivationFunctionType

ENGS = ("pe", "act", "dve", "pool", "sp")
N_DMA_SLOTS = 40

D = 1024
T = 4352
NBLK = 34
NCTX = 256
NLAT = 4096
EPS = 1e-6
NCH = 69


class Buf:
    __slots__ = ("w", "r", "pid")

    def __init__(self):
        self.w = None
        self.r = {}
        self.pid = None


class TL:
    def __init__(self, t):
        self.t = t
        self.b = Buf()
        self.bank = None

    def __getitem__(self, idx):
        return self.t[idx]


def _banks(xs):
    out = []
    for x in xs:
        bk = getattr(x, "bank", None)
        if bk is not None and bk not in out:
            out.append(bk)
    return out


def _bufs(xs):
    out = []
    for x in xs:
        if x is None:
            continue
        out.append(x.b if hasattr(x, "b") else x)
    return out


class Prog:
    def __init__(self, nc):
        self.nc = nc
        self.streams = {e: [] for e in ENGS}
        self.count = {e: 0 for e in ENGS}
        self.known = {e: {} for e in ENGS}
        self.slot_val = [0] * N_DMA_SLOTS
        self.next_slot = 0

    def _wait(self, eng, tok, same_ok=False):
        if tok is None:
            return
        key, val = tok
        if key == eng and not same_ok:
            return
        if self.known[eng].get(key, 0) >= val:
            return
        self.known[eng][key] = val
        self.streams[eng].append(("wait", key, val))

    def _deps(self, eng, reads, writes):
        for b in list(reads) + list(writes):
            if b.pid is not self:
                b.pid = self
                b.w = None
                b.r = {}
        so = (eng != "pe")
        for b in reads:
            self._wait(eng, b.w, same_ok=so)
        for b in writes:
            self._wait(eng, b.w, same_ok=so)
            for k, v in b.r.items():
                self._wait(eng, (k, v), same_ok=so)

    def _mark(self, tok, reads, writes):
        k, v = tok
        for b in reads:
            if b.r.get(k, 0) < v:
                b.r[k] = v
        for b in writes:
            b.w = tok
            b.r = {}

    def op(self, eng, fn, reads=(), writes=()):
        banks = _banks(list(reads) + list(writes))
        reads = _bufs(reads)
        writes = _bufs(writes)
        self._deps(eng, reads, writes)
        for bk in banks:
            if bk.pid is not self:
                bk.pid = self
                bk.w = None
            self._wait(eng, bk.w)
        self.count[eng] += 1
        self.streams[eng].append(("inst", fn))
        self._mark((eng, self.count[eng]), reads, writes)
        for bk in banks:
            bk.w = (eng, self.count[eng])

    def dma(self, q, out, in_, reads=(), writes=(), **kw):
        reads = _bufs(reads)
        writes = _bufs(writes)
        self._deps(q, reads, writes)
        s = self.next_slot
        self.next_slot = (s + 1) % N_DMA_SLOTS
        key = "d%d" % s
        if self.slot_val[s] > 0:
            self._wait(q, (key, self.slot_val[s]))
        self.slot_val[s] += 16
        self.streams[q].append(("dma", out, in_, key, kw))
        self._mark((key, self.slot_val[s]), reads, writes)

    def check(self):
        sem = {}
        pos = {e: 0 for e in ENGS}
        progress = True
        while progress:
            progress = False
            for e in ENGS:
                st = self.streams[e]
                while pos[e] < len(st):
                    it = st[pos[e]]
                    if it[0] == "wait":
                        if sem.get(it[1], 0) < it[2]:
                            break
                    elif it[0] == "inst":
                        sem[e] = sem.get(e, 0) + 1
                    else:
                        sem[it[3]] = sem.get(it[3], 0) + 16
                    pos[e] += 1
                    progress = True
        for e in ENGS:
            if pos[e] < len(self.streams[e]):
                raise RuntimeError("sync deadlock: %s stuck at %d/%d on %r (sem=%r)" % (
                    e, pos[e], len(self.streams[e]), self.streams[e][pos[e]], sem.get(self.streams[e][pos[e]][1])))

    def emit(self):
        nc = self.nc
        for s in range(N_DMA_SLOTS):
            if self.slot_val[s] > 0:
                self._wait("sp", ("d%d" % s, self.slot_val[s]))
        for e in ENGS:
            if e != "sp" and self.count[e] > 0:
                self._wait("sp", (e, self.count[e]))
        self.count["sp"] += 1
        self.streams["sp"].append(("inst", lambda e: e.nop()))
        for e in ENGS:
            if e != "sp":
                self.streams[e].append(("wait", "sp", self.count["sp"]))
        self.check()
        with contextlib.ExitStack() as st:
            sems = {}
            for e in ENGS:
                sems[e] = st.enter_context(nc.semaphore("s_" + e))
            for s in range(N_DMA_SLOTS):
                if self.slot_val[s] > 0:
                    sems["d%d" % s] = st.enter_context(nc.semaphore("s_d%d" % s))
            nc.all_engine_barrier()
            for sm in sems.values():
                nc.gpsimd.sem_clear(sm)
            nc.all_engine_barrier()
            block = st.enter_context(nc.Block())

            def run(engname):
                stream = self.streams[engname]

                def body(eng):
                    for it in stream:
                        if it[0] == "wait":
                            eng.wait_ge(sems[it[1]], it[2])
                        elif it[0] == "inst":
                            it[1](eng).then_inc(sems[engname], 1)
                        else:
                            _, out, in_, key, kw = it
                            eng.dma_start(out=out, in_=in_, **kw).then_inc(sems[key], 16)
                return body

            block.tensor(run("pe"))
            block.scalar(run("act"))
            block.vector(run("dve"))
            block.gpsimd(run("pool"))
            block.sync(run("sp"))


_UID = [0]


class Stage:
    def __init__(self, nc):
        self.nc = nc
        self.st = contextlib.ExitStack()
        self.P = Prog(nc)
        self.n = 0

    def __enter__(self):
        self.st.__enter__()
        return self

    def __exit__(self, *a):
        if a[0] is None:
            self.P.emit()
        return self.st.__exit__(*a)

    def sb(self, shape, dt=F32):
        _UID[0] += 1
        return TL(self.st.enter_context(self.nc.sbuf_tensor("sb_%d" % _UID[0], list(shape), dt)))

    def ps(self, shape=(128, 512), dt=F32):
        _UID[0] += 1
        t = TL(self.st.enter_context(self.nc.psum_tensor("ps_%d" % _UID[0], list(shape), dt)))
        t.bank = Buf()
        return t

    def mm(self, out, lhsT, rhs, start=True, stop=True, r=(), w=()):
        self.P.op("pe", lambda e: e.matmul(out, lhsT=lhsT, rhs=rhs, start=start, stop=stop), r, w)

    def tr(self, out, in_, ident, r=(), w=()):
        self.P.op("pe", lambda e: e.transpose(out, in_, ident), r, w)

    def act(self, out, in_, func, bias=None, scale=None, accum=None, r=(), w=()):
        kw = {}
        if bias is not None:
            kw["bias"] = bias
        if scale is not None:
            kw["scale"] = scale
        if accum is not None:
            kw["accum_out"] = accum
        self.P.op("act", lambda e: e.activation(out=out, in_=in_, func=func, **kw), r, w)

    def ts(self, eng, out, in0, s1, s2, op0, op1=None, r=(), w=()):
        if op1 is None:
            self.P.op(eng, lambda e: e.tensor_scalar(out=out, in0=in0, scalar1=s1, scalar2=0.0, op0=op0, op1=ALU.add), r, w)
        else:
            self.P.op(eng, lambda e: e.tensor_scalar(out=out, in0=in0, scalar1=s1, scalar2=s2, op0=op0, op1=op1), r, w)

    def stt(self, eng, out, in0, scalar, in1, op0, op1, r=(), w=()):
        self.P.op(eng, lambda e: e.scalar_tensor_tensor(out=out, in0=in0, scalar=scalar, in1=in1, op0=op0, op1=op1), r, w)

    def tt(self, eng, out, in0, in1, op, r=(), w=()):
        self.P.op(eng, lambda e: e.tensor_tensor(out=out, in0=in0, in1=in1, op=op), r, w)

    def copy(self, eng, out, in_, r=(), w=()):
        if eng == "act":
            self.P.op("act", lambda e: e.copy(out=out, in_=in_), r, w)
        else:
            self.P.op(eng, lambda e: e.tensor_copy(out=out, in_=in_), r, w)

    def memset(self, eng, ap, val, w=()):
        self.P.op(eng, lambda e: e.memset(ap, val), (), w)

    def dma(self, q, out, in_, r=(), w=(), **kw):
        self.P.dma(q, out, in_, r, w, **kw)


C_ID, C_TRIU, C_TRIL, C_TRIUS, C_TRILS, C_ONES, C_NEGF, C_NEGB, C_SEL = [i * 128 for i in range(9)]
C_MD16, C_MS16, C_MS32, C_MS64 = [11 * 128 + i * 512 for i in range(4)]
NCST = 11 * 128 + 4 * 512


def make_consts():
    c = np.zeros((128, NCST), np.float32)
    t = np.arange(128)[:, None]
    i = np.arange(128)[None, :]
    c[:, C_ID:C_ID + 128] = (t == i)
    c[:, C_TRIU:C_TRIU + 128] = (t <= i)
    c[:, C_TRIL:C_TRIL + 128] = (t >= i)
    c[:, C_TRIUS:C_TRIUS + 128] = (t < i)
    c[:, C_TRILS:C_TRILS + 128] = (t > i)
    c[:, C_ONES:C_ONES + 128] = 1.0
    c[:, C_NEGF:C_NEGF + 128] = np.where(t <= i, 0.0, -1e5)
    c[:, C_NEGB:C_NEGB + 128] = np.where(t >= i, 0.0, -1e5)
    for s in range(3):
        c[s, C_SEL + s * 128:C_SEL + (s + 1) * 128] = 1.0
    md16 = (t // 16 == i // 16)
    for off, b in ((C_MS16, 16), (C_MS32, 32), (C_MS64, 64)):
        m = (t // (2 * b) == i // (2 * b)) & (t // b != i // b)
        for q in range(4):
            c[:, off + q * 128:off + (q + 1) * 128] = m
    for q in range(4):
        c[:, C_MD16 + q * 128:C_MD16 + (q + 1) * 128] = md16
    return c


O_AB, O_AC, O_AX, O_AZ = 0, 512, 1024, 1536
O_Q, O_K, O_V, O_DZ = 2048, 2560, 3072, 3584
O_DA, O_DB = 4096, 4104
O_GQ, O_GK, O_GV, O_GZ, O_GR, O_MG = 4112, 4368, 4624, 5136, 5648, 5680
INW = 8752


def chunk_list():
    ch = []
    for c0 in range(0, 4096, 128):
        ch.append((c0, 128))
    for c0 in range(O_GQ, O_GR, 128):
        ch.append((c0, 128))
    ch.append((O_GR, 32))
    for c0 in range(O_MG, INW, 128):
        ch.append((c0, 128))
    assert len(ch) == NCH
    return ch


CH = chunk_list()
CI_AB, CI_AC, CI_AX, CI_AZ = 0, 4, 8, 12
CI_Q, CI_K, CI_V, CI_DZ = 16, 20, 24, 28
CI_GQ, CI_GK, CI_GV, CI_GZ, CI_GR, CI_MG = 32, 34, 36, 40, 44, 45


class Ctx:
    pass


def tok_src(g, l, s, n):
    if n < 2:
        base = g.ctx_in if l == 0 else g.xrc
        return base[s, n * 128:(n + 1) * 128, :]
    base = g.x_in if l == 0 else g.xrl
    return base[s, (n - 2) * 128:(n - 1) * 128, :]


def stage_mod(g, l):
    nc = g.nc
    with Stage(nc) as S:
        c3T = S.sb([128, 8, 3])
        sc3T = S.sb([128, 8, 3])
        bcol = S.sb([128, 24])
        ncol = S.sb([128, 8])
        brow = S.sb([3, 1024])
        grow = S.sb([3, 1024])
        wm = [S.sb([128, 8, 512]) for _ in range(2)]
        pm = S.ps([128, 512])
        pr = S.ps([128, 512])
        pg = [S.ps([128, 512]) for _ in range(2)]
        for s in range(2):
            S.dma("sp", c3T[:, :, s], g.c_in[s, :].rearrange("(k p) -> p k", p=128), w=[c3T], allow_slow_non_contiguous=True)
        S.dma("sp", c3T[:, :, 2], g.cctx_in[0, :].rearrange("(k p) -> p k", p=128), w=[c3T], allow_slow_non_contiguous=True)
        S.dma("sp", bcol[:, :], g.b_mod[l, :].rearrange("(c p) -> p c", p=128), w=[bcol], allow_slow_non_contiguous=True)
        S.dma("sp", ncol[:, :], g.norm_w[l, :].rearrange("(c p) -> p c", p=128), w=[ncol], allow_slow_non_contiguous=True)
        S.dma("sp", brow[:, :], g.b_mod[l:l + 1, 2048:3072].partition_broadcast(3), w=[brow])
        S.act(sc3T[:, :, :], c3T[:, :, :], AF.Silu, r=[c3T], w=[sc3T])
        import os
        lvl = int(os.environ.get("MODLVL", "99"))
        wv = g.w_mod[l].rearrange("(k p) n -> p k n", p=128)
        for cg in range(6 if lvl >= 2 else 0):
            wt = wm[cg % 2]
            S.dma("sp", wt[:, :, :], wv[:, :, cg * 512:(cg + 1) * 512], w=[wt])
            for j in range(4):
                chn = cg * 4 + j
                for k in range(8):
                    S.mm(pm[:, j * 4:j * 4 + 3], wt[:, k, j * 128:(j + 1) * 128], sc3T[:, k, :], start=(k == 0), stop=(k == 7), r=[wt, sc3T], w=[pm])
                S.ts("dve", g.modcol[:, chn, :], pm[:, j * 4:j * 4 + 3], bcol[:, chn:chn + 1], None, ALU.add, r=[pm, bcol], w=[g.modcol])
            if cg >= 4 and lvl >= 3:
                for k in range(8):
                    S.mm(pr[0:3, :], sc3T[:, k, :], wt[:, k, :], start=(k == 0), stop=(k == 7), r=[wt, sc3T], w=[pr])
                S.tt("dve", grow[:, (cg - 4) * 512:(cg - 3) * 512], pr[0:3, :], brow[:, (cg - 4) * 512:(cg - 3) * 512], ALU.add, r=[pr, brow], w=[grow])
        for s in range(3 if lvl >= 4 else 0):
            for cg in range(2):
                p = pg[cg]
                S.mm(p[:, :], g.cst[0:3, C_SEL + s * 128:C_SEL + (s + 1) * 128], grow[0:3, cg * 512:(cg + 1) * 512], r=[grow, g.cst], w=[p])
                S.copy("act", g.Gb[:, s, cg * 512:(cg + 1) * 512], p[:, :], r=[p], w=[g.Gb])
        for k in range(8 if lvl >= 5 else 0):
            S.ts("dve", g.A1[:, k, :], g.modcol[:, 8 + k, :], 1.0, ncol[:, k:k + 1], ALU.add, ALU.mult, r=[g.modcol, ncol], w=[g.A1])


def stage_proj(g, l, s):
    nc = g.nc
    with Stage(nc) as S:
        hT = S.sb([128, 8, T], BF16)
        xb = [S.sb([128, D]) for _ in range(2)]
        xn = [S.sb([128, D]) for _ in range(2)]
        junk = S.sb([128, D])
        stat = S.sb([128, NBLK, 4])
        wst = [S.sb([128, 8, 512]) for _ in range(2)]
        wb = [S.sb([128, 8, 512], BF16) for _ in range(2)]
        stg = [S.sb([128, 512]) for _ in range(4)]
        wab32 = S.sb([128, 8, 16])
        wab = S.sb([128, 8, 16], BF16)
        gbst = S.sb([128, NBLK, 16])
        ptr = [S.ps([128, 512]) for _ in range(2)]
        pp = [S.ps([128, 512]) for _ in range(4)]
        pab = S.ps([128, 512])
        ident = g.cst[:, C_ID:C_ID + 128]
        for n in range(NBLK):
            sidx = 2 if n < 2 else s
            x_t = xb[n % 2]
            xn_t = xn[n % 2]
            S.dma("sp", x_t[:, :], tok_src(g, l, s, n), w=[x_t])
            S.act(junk[:, :], x_t[:, :], AF.Square, accum=stat[:, n, 0:1], r=[x_t], w=[junk, stat])
            S.ts("dve", stat[:, n, 1:2], stat[:, n, 0:1], 1.0 / D, EPS, ALU.mult, ALU.add, r=[stat], w=[stat])
            S.act(stat[:, n, 2:3], stat[:, n, 1:2], AF.Ln, r=[stat], w=[stat])
            S.act(stat[:, n, 3:4], stat[:, n, 2:3], AF.Exp, scale=-0.5, r=[stat], w=[stat])
            S.ts("dve", xn_t[:, :], x_t[:, :], stat[:, n, 3:4], None, ALU.mult, r=[x_t, stat], w=[xn_t])
            for k in range(8):
                p = ptr[k // 4]
                S.tr(p[:, (k % 4) * 128:(k % 4 + 1) * 128], xn_t[:, k * 128:(k + 1) * 128], ident, r=[xn_t, g.cst], w=[p])
            for k in range(8):
                p = ptr[k // 4]
                src = p[:, (k % 4) * 128:(k % 4 + 1) * 128]
                dst = hT[:, k, n * 128:(n + 1) * 128]
                if k // 4 == 0:
                    S.ts("dve", dst, src, g.A1[:, k, sidx:sidx + 1], g.modcol[:, k, sidx:sidx + 1], ALU.mult, ALU.add, r=[p, g.A1, g.modcol], w=[hT])
                else:
                    S.act(dst, src, AF.Identity, bias=g.modcol[:, k, sidx:sidx + 1], scale=g.A1[:, k, sidx:sidx + 1], r=[p, g.A1, g.modcol], w=[hT])
        wv = g.w_in[l].rearrange("(k p) n -> p k n", p=128)
        S.dma("sp", wab32[:, :, :], wv[:, :, O_DA:O_DA + 16], w=[wab32])
        S.copy("dve", wab[:, :, :], wab32[:, :, :], r=[wab32], w=[wab])
        for n in range(NBLK):
            for k in range(8):
                S.mm(pab[:, (n % 8) * 16:(n % 8 + 1) * 16], hT[:, k, n * 128:(n + 1) * 128], wab[:, k, :], start=(k == 0), stop=(k == 7), r=[hT, wab], w=[pab])
            S.copy("dve", gbst[:, n, :], pab[:, (n % 8) * 16:(n % 8 + 1) * 16], r=[pab], w=[gbst])
        S.dma("act", g.GB[:, :].rearrange("(n p) c -> p n c", p=128), gbst[:, :, :], r=[gbst])
        tiles = [(0, NCTX)] + [(NCTX + i * 512, 512) for i in range(8)]
        ngrp = (NCH + 3) // 4
        ev = 0
        for gi in range(ngrp):
            chs = list(range(gi * 4, min(NCH, gi * 4 + 4)))
            ws = wst[gi % 2]
            wbt = wb[gi % 2]
            for j, c in enumerate(chs):
                c0, ncol_ = CH[c]
                if c == CI_GR:
                    S.memset("pool", ws[:, :, j * 128:(j + 1) * 128], 0.0, w=[ws])
                    S.dma("sp", ws[:, :, j * 128:j * 128 + 16], wv[:, :, c0:c0 + 16], w=[ws])
                    S.dma("sp", ws[:, :, j * 128 + 32:j * 128 + 48], wv[:, :, c0 + 16:c0 + 32], w=[ws])
                else:
                    S.dma("sp", ws[:, :, j * 128:j * 128 + ncol_], wv[:, :, c0:c0 + ncol_], w=[ws])
            nw = len(chs) * 128
            S.copy("pool" if gi % 2 == 0 else "dve", wbt[:, :, 0:nw], ws[:, :, 0:nw], r=[ws], w=[wbt])
            for (t0, n) in tiles:
                for j, c in enumerate(chs):
                    nrow = 48 if c == CI_GR else CH[c][1]
                    p = pp[ev % 4]
                    sg = stg[ev % 4]
                    for k in range(8):
                        S.mm(p[0:nrow, 0:n], wbt[:, k, j * 128:j * 128 + nrow], hT[:, k, t0:t0 + n], start=(k == 0), stop=(k == 7), r=[wbt, hT], w=[p])
                    if ev % 2 == 0:
                        S.copy("act", sg[0:nrow, 0:n], p[0:nrow, 0:n], r=[p], w=[sg])
                    else:
                        S.copy("dve", sg[0:nrow, 0:n], p[0:nrow, 0:n], r=[p], w=[sg])
                    S.dma("act", g.P[c * 128:c * 128 + nrow, t0:t0 + n], sg[0:nrow, 0:n], r=[sg])
                    ev += 1


TILES = [(0, NCTX)] + [(NCTX + i * 512, 512) for i in range(8)]


def rows_view(ap, n, lo, hi):
    if n == NCTX:
        return ap[:, lo:n + hi]
    v = ap.rearrange("p (r c) -> p r c", c=64)
    return v[:, :, lo:64 + hi]


def stage_a(g, l, s):
    nc = g.nc
    with Stage(nc) as S:
        cw = S.sb([128, 4, 3])
        for k in range(3):
            S.dma("sp", cw[:, :, k], g.conv_a[l, k, :].rearrange("(c p) -> p c", p=128), w=[cw], allow_slow_non_contiguous=True)
        NB = 3
        tin = [[S.sb([128, 512]) for _ in range(4)] for _ in range(NB)]
        tu = [S.sb([128, 512]) for _ in range(NB)]
        ty = [S.sb([128, 512]) for _ in range(NB)]
        tsz = [S.sb([128, 512]) for _ in range(NB)]
        tout = [S.sb([128, 512], BF16) for _ in range(NB)]
        it = 0
        for (t0, n) in TILES:
            for j in range(4):
                ab, ac, ax, az = tin[it % NB]
                u, y, sz, yo = tu[it % NB], ty[it % NB], tsz[it % NB], tout[it % NB]
                for tl, ci in ((ab, CI_AB), (ac, CI_AC), (ax, CI_AX), (az, CI_AZ)):
                    S.dma("sp", tl[:, 0:n], g.P[(ci + j) * 128:(ci + j + 1) * 128, t0:t0 + n], w=[tl])
                S.tt("pool", u[:, 0:n], ac[:, 0:n], ax[:, 0:n], ALU.mult, r=[ac, ax], w=[u])
                S.act(y[:, 0:n], u[:, 0:n], AF.Copy, scale=cw[:, j, 1:2], r=[u, cw], w=[y])
                S.stt("dve", rows_view(y[:, 0:n], n, 1, 0), rows_view(u[:, 0:n], n, 0, -1), cw[:, j, 0:1], rows_view(y[:, 0:n], n, 1, 0),
                      ALU.mult, ALU.add, r=[u, y, cw], w=[y])
                S.stt("dve", rows_view(y[:, 0:n], n, 0, -1), rows_view(u[:, 0:n], n, 1, 0), cw[:, j, 2:3], rows_view(y[:, 0:n], n, 0, -1),
                      ALU.mult, ALU.add, r=[u, y, cw], w=[y])
                S.act(sz[:, 0:n], az[:, 0:n], AF.Silu, r=[az], w=[sz])
                S.tt("pool", u[:, 0:n], ab[:, 0:n], y[:, 0:n], ALU.mult, r=[ab, y], w=[u])
                S.tt("dve", yo[:, 0:n], u[:, 0:n], sz[:, 0:n], ALU.mult, r=[u, sz], w=[yo])
                S.dma("act", g.YA[j * 128:(j + 1) * 128, t0:t0 + n], yo[:, 0:n], r=[yo])
                it += 1


def stage_merge(g, l, s, last):
    nc = g.nc
    with Stage(nc) as S:
        wp = [S.sb([128, 4, D], BF16) for _ in range(3)]
        wo = S.sb([128, 8, D], BF16)
        wst = [S.sb([128, 4, D])] * 2
        bmc = S.sb([128, 24])
        S.dma("sp", bmc[:, :], g.b_merge[l, :].rearrange("(c p) -> p c", p=128), w=[bmc], allow_slow_non_contiguous=True)
        srcs = [g.w_pa, g.w_pb, g.w_pc]
        for br in range(3):
            S.dma("sp", wst[br % 2][:, :, :], srcs[br][l].rearrange("(k p) n -> p k n", p=128), w=[wst[br % 2]])
            S.copy("pool", wp[br][:, :, :], wst[br % 2][:, :, :], r=[wst[br % 2]], w=[wp[br]])
        for hf in range(2):
            S.dma("sp", wst[(hf + 1) % 2][:, :, :], g.w_o[l].rearrange("(k p) n -> p k n", p=128)[:, hf * 4:(hf + 1) * 4, :], w=[wst[(hf + 1) % 2]])
            S.copy("pool", wo[:, hf * 4:(hf + 1) * 4, :], wst[(hf + 1) % 2][:, :, :], r=[wst[(hf + 1) % 2]], w=[wo])
        if last:
            fnw = S.sb([128, D])
            S.dma("sp", fnw[:, :], g.fnw[0:1, :].partition_broadcast(128), w=[fnw])
            fst = S.sb([128, 8, 4])
        ysb = [[S.sb([128, 4, 512], BF16) for _ in range(3)]] * 2
        mgt = [S.sb([128, 512]) for _ in range(6)]
        gt = [S.sb([128, 512]) for _ in range(6)]
        tt_ = [S.sb([128, 512]) for _ in range(6)]
        mT = [S.sb([128, 8, 512], BF16) for _ in range(2)]
        xblk = [S.sb([128, D]) for _ in range(2)]
        xo = [S.sb([128, D]) for _ in range(2)]
        tmp = [S.sb([128, 512]) for _ in range(2)]
        junk = S.sb([128, D]) if last else None
        pb = [S.ps([128, 512]) for _ in range(6)]
        po = [S.ps([128, 512]) for _ in range(2)]
        ysrc = [g.YA, g.YB, g.YC]
        e3 = 0
        eo = 0
        for ti, (t0, n) in enumerate(TILES):
            if last and ti == 0:
                continue
            sidx = 2 if ti == 0 else s
            yt = ysb[ti % 2]
            m_t = mT[ti % 2]
            for br in range(3):
                S.dma("sp", yt[br][:, :, 0:n], ysrc[br][:, t0:t0 + n].rearrange("(j p) t -> p j t", p=128), w=[yt[br]])
            for oc in range(8):
                trip = []
                for br in range(3):
                    i6 = e3 % 6
                    e3 += 1
                    ci = CI_MG + br * 8 + oc
                    S.dma("sp", mgt[i6][:, 0:n], g.P[ci * 128:(ci + 1) * 128, t0:t0 + n], w=[mgt[i6]])
                    S.act(gt[i6][:, 0:n], mgt[i6][:, 0:n], AF.Sigmoid, bias=bmc[:, br * 8 + oc:br * 8 + oc + 1], r=[mgt[i6], bmc], w=[gt[i6]])
                    for j in range(4):
                        S.mm(pb[i6][:, 0:n], wp[br][:, j, oc * 128:(oc + 1) * 128], yt[br][:, j, 0:n], start=(j == 0), stop=(j == 3), r=[wp[br], yt[br]], w=[pb[i6]])
                    S.tt("dve", tt_[i6][:, 0:n], pb[i6][:, 0:n], gt[i6][:, 0:n], ALU.mult, r=[pb[i6], gt[i6]], w=[tt_[i6]])
                    trip.append(tt_[i6])
                S.tt("pool", trip[0][:, 0:n], trip[0][:, 0:n], trip[1][:, 0:n], ALU.add, r=[trip[0], trip[1]], w=[trip[0]])
                S.tt("pool", m_t[:, oc, 0:n], trip[0][:, 0:n], trip[2][:, 0:n], ALU.add, r=[trip[0], trip[2]], w=[m_t])
            for bi in range(n // 128):
                nblk = t0 // 128 + bi
                xb_ = xblk[eo % 2]
                xo_ = xo[eo % 2]
                eo += 1
                S.dma("sp", xb_[:, :], tok_src(g, l, s, nblk), w=[xb_])
                for cg in range(2):
                    p = po[cg]
                    for k in range(8):
                        S.mm(p[:, :], m_t[:, k, bi * 128:(bi + 1) * 128], wo[:, k, cg * 512:(cg + 1) * 512], start=(k == 0), stop=(k == 7), r=[m_t, wo], w=[p])
                    S.tt("dve", tmp[cg][:, :], p[:, :], g.Gb[:, sidx, cg * 512:(cg + 1) * 512], ALU.mult, r=[p, g.Gb], w=[tmp[cg]])
                    S.tt("pool", xo_[:, cg * 512:(cg + 1) * 512], tmp[cg][:, :], xb_[:, cg * 512:(cg + 1) * 512], ALU.add, r=[tmp[cg], xb_], w=[xo_])
                if not last:
                    dst = g.xrc[s, nblk * 128:(nblk + 1) * 128, :] if nblk < 2 else g.xrl[s, (nblk - 2) * 128:(nblk - 1) * 128, :]
                    S.dma("act", dst, xo_[:, :], r=[xo_])
                else:
                    q = eo % 8
                    S.act(junk[:, :], xo_[:, :], AF.Square, accum=fst[:, q, 0:1], r=[xo_], w=[junk, fst])
                    S.ts("dve", fst[:, q, 1:2], fst[:, q, 0:1], 1.0 / D, EPS, ALU.mult, ALU.add, r=[fst], w=[fst])
                    S.act(fst[:, q, 2:3], fst[:, q, 1:2], AF.Ln, r=[fst], w=[fst])
                    S.act(fst[:, q, 3:4], fst[:, q, 2:3], AF.Exp, scale=-0.5, r=[fst], w=[fst])
                    S.stt("dve", xb_[:, :], xo_[:, :], fst[:, q, 3:4], fnw[:, :], ALU.mult, ALU.mult, r=[xo_, fst, fnw], w=[xb_])
                    S.dma("act", g.out[s, (nblk - 2) * 128:(nblk - 1) * 128, :], xb_[:, :], r=[xb_])


def stage_bprep(g, l, s):
    nc = g.nc
    with Stage(nc) as S:
        cw = S.sb([128, 12, 3])
        for k in range(3):
            S.dma("sp", cw[:, :, k], g.conv_dn[l, k, :].rearrange("(c p) -> p c", p=128), w=[cw], allow_slow_non_contiguous=True)
        raw = [S.sb([128, 514]) for _ in range(3)]
        acc = [S.sb([128, 512]) for _ in range(3)]
        sil = [S.sb([128, 512]) for _ in range(8)]
        vout = [S.sb([128, 512], BF16) for _ in range(2)]
        sq = [S.sb([128, 512]) for _ in range(2)]
        lnt = [S.sb([128, 512]) for _ in range(2)]
        qo = [S.sb([128, 512], BF16) for _ in range(2)]
        pn = [S.ps([128, 512]) for _ in range(2)]
        onesb = g.cstb[:, C_ONES:C_ONES + 128]
        sqh = [S.sb([128, 512], BF16) for _ in range(2)]
        sql = [S.sb([128, 512], BF16) for _ in range(2)]
        it = 0
        for (t0, n) in TILES[int(os.environ.get('TLO', '0')):int(os.environ.get('THI', '9'))]:
            seq_lo, seq_hi = (0, NCTX) if t0 == 0 else (NCTX, T)
            for j in range(int(os.environ.get("JLO", "0")), 12):
                RS = int(os.environ.get("RING", "3"))
                rw = raw[it % RS]
                ac_ = acc[it % RS]
                it += 1
                zl = 1 if t0 - 1 < seq_lo else 0
                zr = 1 if t0 + n + 1 > seq_hi else 0
                if zl and not os.environ.get("NOMEMSET"):
                    S.memset("pool", rw[:, 0:1], 0.0, w=[rw])
                if zr and not os.environ.get("NOMEMSET"):
                    S.memset("pool", rw[:, n + 1:n + 2], 0.0, w=[rw])
                S.dma("sp", rw[:, zl:n + 2 - zr], g.P[(CI_Q + j) * 128:(CI_Q + j + 1) * 128, t0 - 1 + zl:t0 + n + 1 - zr], w=[rw])
                S.act(ac_[:, 0:n], rw[:, 1:n + 1], AF.Copy, scale=cw[:, j, 1:2], r=[rw, cw], w=[ac_])
                S.stt("dve", ac_[:, 0:n], rw[:, 0:n], cw[:, j, 0:1], ac_[:, 0:n], ALU.mult, ALU.add, r=[rw, ac_, cw], w=[ac_])
                S.stt("dve", ac_[:, 0:n], rw[:, 2:n + 2], cw[:, j, 2:3], ac_[:, 0:n], ALU.mult, ALU.add, r=[rw, ac_, cw], w=[ac_])
                if j < 8:
                    S.act(sil[j][:, 0:n], ac_[:, 0:n], AF.Silu, r=[ac_], w=[sil[j]])
                else:
                    vo = vout[j % (1 if os.environ.get('RING') == '1' else 2)]
                    S.act(vo[:, 0:n], ac_[:, 0:n], AF.Silu, r=[ac_], w=[vo])
                    S.dma("act", g.VT[(j - 8) * 128:(j - 7) * 128, t0:t0 + n], vo[:, 0:n], r=[vo])
            for j in range(0 if os.environ.get("NONORM") else 8):
                sq_ = sq[j % 2]
                ln_ = lnt[j % 2]
                q_ = qo[j % 2]
                p = pn[j % 2]
                NL = int(os.environ.get("NORMLVL", "9"))
                S.tt("pool", sq_[:, 0:n], sil[j][:, 0:n], sil[j][:, 0:n], ALU.mult, r=[sil[j]], w=[sq_])
                sh_, sl_ = sqh[j % 2], sql[j % 2]
                S.copy("pool", sh_[:, 0:n], sq_[:, 0:n], r=[sq_], w=[sh_])
                S.tt("pool", sl_[:, 0:n], sq_[:, 0:n], sh_[:, 0:n], ALU.subtract, r=[sq_, sh_], w=[sl_])
                if NL >= 2:
                    S.mm(p[:, 0:n], onesb, sh_[:, 0:n], start=True, stop=False, r=[g.cstb, sh_], w=[p])
                    S.mm(p[:, 0:n], onesb, sl_[:, 0:n], start=False, stop=True, r=[g.cstb, sl_], w=[p])
                if NL >= 3:
                    S.act(ln_[:, 0:n], p[:, 0:n], AF.Ln, bias=g.epsc[:, 0:1], r=[p, g.epsc], w=[ln_])
                if NL >= 4:
                    S.act(ln_[:, 0:n], ln_[:, 0:n], AF.Exp, scale=-0.5, r=[ln_], w=[ln_])
                if NL >= 5:
                    S.stt("dve", q_[:, 0:n], sil[j][:, 0:n], (128.0 ** -0.5) if j < 4 else 1.0, ln_[:, 0:n], ALU.mult, ALU.mult, r=[sil[j], ln_], w=[q_])
                dst = g.QT if j < 4 else g.KT
                S.dma("act", dst[(j % 4) * 128:(j % 4 + 1) * 128, t0:t0 + n], q_[:, 0:n], r=[q_])


class Sub:
    def __init__(self, tl):
        self.t = tl.t
        self.b = Buf()
        self.bank = tl.bank

    def __getitem__(self, idx):
        return self.t[idx]


ORD_F = list(range(NBLK))
ORD_B = [1, 0] + list(range(NBLK - 1, 1, -1))


def stage_bscan(g, l, s):
    nc = g.nc
    nsteps = int(os.environ.get("BSTEPS", str(NBLK)))
    with Stage(nc) as S:
        cst = g.cst
        CI = lambda o: cst[:, o:o + 128]
        ident, ones = CI(C_ID), CI(C_ONES)
        identb4 = S.sb([128, 4, 128], BF16)
        for q in range(4):
            S.copy("dve", identb4[:, q, :], ident, r=[cst], w=[identb4])
        gbr = S.sb([128, NBLK, 16])
        S.dma("sp", gbr[:, :, :], g.GB[:, :].rearrange("(n p) c -> p n c", p=128), w=[gbr])
        dtb = S.sb([128, 8])
        alog = S.sb([128, 8])
        S.dma("sp", dtb[:, :], g.dn_dt_bias[l:l + 1, :].partition_broadcast(128), w=[dtb])
        S.dma("sp", alog[:, :], g.dn_a_log[l:l + 1, :].partition_broadcast(128), w=[alog])
        nega = S.sb([128, 8])
        S.act(nega[:, :], alog[:, :], AF.Exp, r=[alog], w=[nega])
        S.ts("dve", nega[:, :], nega[:, :], -1.0, None, ALU.mult, r=[nega], w=[nega])
        t1 = S.sb([128, NBLK, 8])
        G = S.sb([128, NBLK, 8])
        NG = S.sb([128, NBLK, 8])
        BT = S.sb([128, NBLK, 8])
        NB = S.sb([128, NBLK, 8])
        for c in range(8):
            S.act(t1[:, :, c], gbr[:, :, c], AF.Exp, bias=dtb[:, c:c + 1], r=[gbr, dtb], w=[t1])
        S.act(t1[:, :, :], t1[:, :, :], AF.Ln, bias=1.0, r=[t1], w=[t1])
        for c in range(8):
            S.ts("dve", G[:, :, c], t1[:, :, c], nega[:, c:c + 1], None, ALU.mult, r=[t1, nega], w=[G])
        S.ts("dve", NG[:, :, :], G[:, :, :], -1.0, None, ALU.mult, r=[G], w=[NG])
        S.act(BT[:, :, :], gbr[:, :, 8:16], AF.Exp, scale=-1.0, r=[gbr], w=[BT])
        S.ts("dve", BT[:, :, :], BT[:, :, :], 1.0, None, ALU.add, r=[BT], w=[BT])
        S.P.op("dve", lambda e: e.reciprocal(out=BT[:, :, :], in_=BT[:, :, :]), [BT], [BT])
        S.ts("dve", NB[:, :, :], BT[:, :, :], -1.0, None, ALU.mult, r=[BT], w=[NB])
        banks = [S.ps([128, 512]) if bi_ not in (3, 4) else S.ps([128, 1024], BF16) for bi_ in range(8)]
        EX = S.sb([128, NBLK, 24])
        cstb = g.cstb
        CB = lambda o: cstb[:, o:o + 128]
        onesb = CB(C_ONES)
        for n in range(NBLK):
            pb = banks[n // 17]
            o = (n % 17) * 24
            S.mm(pb[:, o:o + 4], CI(C_TRIU), G[:, n, 0:4], r=[cst, G], w=[pb])
            S.mm(pb[:, o + 4:o + 8], CI(C_TRIL), G[:, n, 4:8], r=[cst, G], w=[pb])
            S.mm(pb[:, o + 8:o + 12], CI(C_TRILS), G[:, n, 0:4], r=[cst, G], w=[pb])
            S.mm(pb[:, o + 12:o + 16], CI(C_TRIUS), G[:, n, 4:8], r=[cst, G], w=[pb])
            S.mm(pb[:, o + 16:o + 24], ones, G[:, n, 0:8], r=[cst, G], w=[pb])
        GC = S.sb([128, NBLK, 8])
        GCh = S.sb([128, NBLK, 8], BF16)
        GChf = S.sb([128, NBLK, 8])
        GCl = S.sb([128, NBLK, 8])
        if not os.environ.get("NOGC"):
            for hb in range(2):
                for n_ in range(17):
                    S.copy("dve", GC[:, hb * 17 + n_, :], banks[hb][:, n_ * 24:n_ * 24 + 8], r=[banks[hb]], w=[GC])
            GL = int(os.environ.get("GCLVL", "9"))
            if GL >= 1:
                S.copy("dve", GCh[:, :, :], GC[:, :, :], r=[GC], w=[GCh])
            if GL >= 2:
                S.copy("dve", GChf[:, :, :], GCh[:, :, :], r=[GCh], w=[GChf])
            if GL >= 3:
                S.tt("dve", GCl[:, :, :], GC[:, :, :], GChf[:, :, :], ALU.subtract, r=[GC, GChf], w=[GCl])
        for hb in range(2):
            S.act(EX[:, hb * 17:(hb + 1) * 17, :].rearrange("p n c -> p (n c)"), banks[hb][:, 0:17 * 24], AF.Exp, r=[banks[hb], GC, GCl], w=[EX])
        KCO = S.sb([128, NBLK, 8])
        NEGC = S.sb([128, NBLK, 8])
        S.tt("dve", KCO[:, :, :], EX[:, :, 8:16], BT[:, :, :], ALU.mult, r=[EX, BT], w=[KCO])
        S.ts("dve", NEGC[:, :, :], EX[:, :, 0:8], -1.0, None, ALU.mult, r=[EX], w=[NEGC])
        onc = S.sb([128, 1])
        S.dma("sp", onc[:, :], g.dn_onorm[l, :].rearrange("(p o) -> p o", o=1), w=[onc], allow_slow_non_contiguous=True)

        if os.environ.get("BDBG"):
            for i_, tl_ in enumerate((G, BT, GC, GCl, KCO, NEGC)):
                S.dma("sp", g.dbg[0:128, i_ * 272:(i_ + 1) * 272], tl_[:, :, :].rearrange("p a b -> p (a b)"), r=[tl_])
            S.dma("sp", g.dbg[0:128, 6 * 272:6 * 272 + 816], EX[:, :, :].rearrange("p a b -> p (a b)"), r=[EX])
            return
        qT2 = S.sb([128, 2, T], BF16)
        kT2 = S.sb([128, 2, T], BF16)
        vT2 = S.sb([128, 2, T], BF16)
        otok = S.sb([128, NBLK, 2, 128])
        osub = [[Sub(otok) for _ in range(2)] for _ in range(NBLK)]
        Sf = [S.sb([128, 128]) for _ in range(4)]
        Sb = [S.sb([128, 128], BF16) for _ in range(4)]

        def t4(dt=BF16):
            t = S.sb([128, 4, 128], dt)
            return t, [Sub(t) for _ in range(4)]
        G1, G1s = t4()
        nG1, nG1s = t4()
        ETa, ETas = t4(F32)
        ET, ETs_ = t4(F32)
        ETm, ETms = t4(F32)
        NT, NTs = t4()
        AqT, AqTs = t4()
        kgb, kgbs = t4()
        vtk, vtks = t4()
        rr, rrs = t4()
        yb, ybs = t4()
        Pp = [t4(), t4()]
        PQ = [t4(), t4()]
        PTp = [t4(), t4()]
        XTp = [t4(), t4()]
        XXp = [t4(), t4()]
        Dd, Dds = t4()
        DdT, DdTs = t4()
        C16, C16s = t4()
        C16T, C16Ts = t4()
        C32, C32s = t4()
        C32T, C32Ts = t4()
        C64, C64s = t4()
        M1, M1s = t4()
        M1T, M1Ts = t4()
        bsub = [[Sub(b) for _ in range(4)] for b in banks]
        B_KK, B_QK, B_DT, B_PP, B_PT, B_XX = 0, 1, 2, 5, 6, 7
        trb = banks[3]
        trs = [Sub(trb) for _ in range(8)]
        n0b = banks[4]
        n0s = [Sub(n0b) for _ in range(4)]
        sl = lambda q: slice(q * 128, (q + 1) * 128)
        zsrc = g.P

        for hp in range(2):
            for hh in range(2):
                h = 2 * hp + hh
                S.dma("sp", qT2[:, hh, :], g.QT[h * 128:(h + 1) * 128, :], w=[qT2])
                S.dma("sp", kT2[:, hh, :], g.KT[h * 128:(h + 1) * 128, :], w=[kT2])
                S.dma("sp", vT2[:, hh, :], g.VT[h * 128:(h + 1) * 128, :], w=[vT2])
            for q in range(4):
                S.memset("pool", Sf[q][:, :], 0.0, w=[Sf[q]])
                S.memset("pool", Sb[q][:, :], 0.0, w=[Sb[q]])
            visited = set()
            if nsteps < NBLK:
                S.memset("pool", otok[:, :, :, :], 0.0, w=[osub[n_][h_] for n_ in range(NBLK) for h_ in range(2)])
            for i in range(nsteps):
                chains = []
                for d in range(2):
                    n = (ORD_F if d == 0 else ORD_B)[i]
                    for hh in range(2):
                        chains.append((d * 2 + hh, d, hh, n, d * 4 + 2 * hp + hh))
                for (q, d, hh, n, c) in chains:
                    kTb = kT2[:, hh, n * 128:(n + 1) * 128]
                    qTb = qT2[:, hh, n * 128:(n + 1) * 128]
                    vTb = vT2[:, hh, n * 128:(n + 1) * 128]
                    SK = os.environ.get("P1SKIP", "").split(",")
                    if "tr" not in SK:
                        S.tr(trb[:, sl(q)], kTb, identb4[:, 0, :], r=[kT2, identb4], w=[trs[q]])
                        S.tr(trb[:, 512 + q * 128:512 + (q + 1) * 128], vTb, identb4[:, 0, :], r=[vT2, identb4], w=[trs[4 + q]])
                    if "kk" not in SK:
                        S.mm(banks[B_KK][:, sl(q)], kTb, kTb, r=[kT2], w=[bsub[B_KK][q]])
                        S.mm(banks[B_QK][:, sl(q)], kTb, qTb, r=[kT2, qT2], w=[bsub[B_QK][q]])
                    if "g1" not in SK:
                        S.act(G1[:, q, :], identb4[:, 0, :], AF.Copy, scale=GChf[:, n, c:c + 1], r=[identb4, GChf], w=[G1s[q]])
                        S.act(nG1[:, q, :], identb4[:, 0, :], AF.Copy, scale=GCl[:, n, c:c + 1], r=[identb4, GCl], w=[nG1s[q]])
                    if "dt" not in SK:
                        S.mm(banks[B_DT][:, sl(q)], onesb, G1[:, q, :], start=True, stop=False, r=[cstb, G1s[q]], w=[bsub[B_DT][q]])
                        S.mm(banks[B_DT][:, sl(q)], onesb, nG1[:, q, :], start=False, stop=True, r=[cstb, nG1s[q]], w=[bsub[B_DT][q]])
                BPH = int(os.environ.get('BPH', '9'))
                if BPH < 2:
                    continue
                for (q, d, hh, n, c) in chains:
                    neg = CI(C_NEGF) if d == 0 else CI(C_NEGB)
                    S.stt("dve", ETa[:, q, :], banks[B_DT][:, sl(q)], GC[:, n, c:c + 1], neg, ALU.subtract, ALU.add, r=[bsub[B_DT][q], cst, GC], w=[ETas[q]])
                    S.act(ET[:, q, :], ETa[:, q, :], AF.Exp, r=[ETas[q]], w=[ETs_[q]])
                    S.tt("pool", ETm[:, q, :], ET[:, q, :], ident, ALU.subtract, r=[ETs_[q], cst], w=[ETms[q]])
                    S.stt("dve", NT[:, q, :], banks[B_KK][:, sl(q)], NB[:, n, c:c + 1], ETm[:, q, :], ALU.mult, ALU.mult,
                          r=[bsub[B_KK][q], NB, ETms[q]], w=[NTs[q]])
                    S.stt("dve", AqT[:, q, :], banks[B_QK][:, sl(q)], BT[:, n, c:c + 1], ET[:, q, :], ALU.mult, ALU.mult,
                          r=[bsub[B_QK][q], BT, ETs_[q]], w=[AqTs[q]])
                    S.ts("dve", kgb[:, q, :], trb[:, sl(q)], KCO[:, n, c:c + 1], None, ALU.mult, r=[trs[q], KCO], w=[kgbs[q]])
                S.copy("act", vtk[:, :, :].rearrange("p a b -> p (a b)"), trb[:, 512:1024], r=trs[4:8], w=vtks)
                for (q, d, hh, n, c) in chains:
                    S.tr(n0b[:, sl(q)], NT[:, q, :], identb4[:, 0, :], r=[NTs[q], identb4], w=[n0s[q]])
                Nn, Nns = Pp[0]
                S.copy("act", Nn[:, :, :].rearrange("p a b -> p (a b)"), n0b[:, 0:512], r=n0s, w=Nns)
                if BPH < 3:
                    continue
                fl = lambda t_: t_[:, :, :].rearrange("p a b -> p (a b)")
                MK = lambda o: cstb[:, o:o + 512]
                S.tt("pool", fl(Dd), fl(Nn), MK(C_MD16), ALU.mult, r=Nns + [cstb], w=Dds)
                S.tt("pool", fl(DdT), fl(NT), MK(C_MD16), ALU.mult, r=NTs + [cstb], w=DdTs)
                S.tt("pool", fl(C16), fl(Nn), MK(C_MS16), ALU.mult, r=Nns + [cstb], w=C16s)
                S.tt("pool", fl(C16T), fl(NT), MK(C_MS16), ALU.mult, r=NTs + [cstb], w=C16Ts)
                S.tt("pool", fl(C32), fl(Nn), MK(C_MS32), ALU.mult, r=Nns + [cstb], w=C32s)
                S.tt("pool", fl(C32T), fl(NT), MK(C_MS32), ALU.mult, r=NTs + [cstb], w=C32Ts)
                S.tt("pool", fl(C64), fl(Nn), MK(C_MS64), ALU.mult, r=Nns + [cstb], w=C64s)
                Xc, Xcs = XXp[0]
                XT, XTs = XTp[0]
                S.tt("dve", fl(Xc), fl(Dd), fl(identb4), ALU.add, r=Dds + [identb4], w=Xcs)
                S.tt("dve", fl(XT), fl(DdT), fl(identb4), ALU.add, r=DdTs + [identb4], w=XTs)
                Pc, Pcs, PT, PTs = Dd, Dds, DdT, DdTs
                flip = 0
                for r_ in range(1, 4):
                    Pn, Pns = Pp[r_ % 2 + 0] if r_ % 2 == 1 else Pp[0]
                    Pn, Pns = PQ[r_ % 2]
                    PTn, PTns = PTp[r_ % 2]
                    flip ^= 1
                    Xn, Xns = XXp[flip]
                    XTn, XTns = XTp[flip]
                    for (q, d, hh, n, c) in chains:
                        S.mm(banks[B_PP][:, sl(q)], PT[:, q, :], Pc[:, q, :], r=[PTs[q], Pcs[q]], w=[bsub[B_PP][q]])
                        S.mm(banks[B_PT][:, sl(q)], Pc[:, q, :], PT[:, q, :], r=[PTs[q], Pcs[q]], w=[bsub[B_PT][q]])
                    S.copy("act", fl(Pn), banks[B_PP][:, :], r=bsub[B_PP], w=Pns)
                    S.copy("dve", fl(PTn), banks[B_PT][:, :], r=bsub[B_PT], w=PTns)
                    for (q, d, hh, n, c) in chains:
                        S.mm(banks[B_XX][:, sl(q)], PTn[:, q, :], Xc[:, q, :], r=[PTns[q], Xcs[q]], w=[bsub[B_XX][q]])
                        S.mm(banks[B_DT][:, sl(q)], Pn[:, q, :], XT[:, q, :], r=[Pns[q], XTs[q]], w=[bsub[B_DT][q]])
                    S.tt("dve", fl(Xn), banks[B_XX][:, :], fl(Xc), ALU.add, r=bsub[B_XX] + Xcs, w=Xns)
                    S.tt("dve", fl(XTn), banks[B_DT][:, :], fl(XT), ALU.add, r=bsub[B_DT] + XTs, w=XTns)
                    Pc, Pcs, PT, PTs = Pn, Pns, PTn, PTns
                    Xc, Xcs, XT, XTs = Xn, Xns, XTn, XTns
                for (Cb, Cbs, CbT, CbTs, lastlvl) in ((C16, C16s, C16T, C16Ts, False), (C32, C32s, C32T, C32Ts, False), (C64, C64s, None, None, True)):
                    flip ^= 1
                    Xn, Xns = XXp[flip]
                    XTn, XTns = XTp[flip]
                    for (q, d, hh, n, c) in chains:
                        if not lastlvl:
                            S.mm(banks[B_PP][:, sl(q)], CbT[:, q, :], Xc[:, q, :], r=[CbTs[q], Xcs[q]], w=[bsub[B_PP][q]])
                        S.mm(banks[B_PT][:, sl(q)], Cb[:, q, :], XT[:, q, :], r=[Cbs[q], XTs[q]], w=[bsub[B_PT][q]])
                    if not lastlvl:
                        S.copy("act", fl(M1), banks[B_PP][:, :], r=bsub[B_PP], w=M1s)
                    S.copy("dve", fl(M1T), banks[B_PT][:, :], r=bsub[B_PT], w=M1Ts)
                    for (q, d, hh, n, c) in chains:
                        if not lastlvl:
                            S.mm(banks[B_XX][:, sl(q)], XT[:, q, :], M1[:, q, :], r=[XTs[q], M1s[q]], w=[bsub[B_XX][q]])
                        S.mm(banks[B_DT][:, sl(q)], Xc[:, q, :], M1T[:, q, :], r=[Xcs[q], M1Ts[q]], w=[bsub[B_DT][q]])
                    if not lastlvl:
                        S.tt("dve", fl(Xn), banks[B_XX][:, :], fl(Xc), ALU.add, r=bsub[B_XX] + Xcs, w=Xns)
                    S.tt("dve", fl(XTn), banks[B_DT][:, :], fl(XT), ALU.add, r=bsub[B_DT] + XTs, w=XTns)
                    Xc, Xcs, XT, XTs = Xn, Xns, XTn, XTns
                for (q, d, hh, n, c) in chains:
                    kTb = kT2[:, hh, n * 128:(n + 1) * 128]
                    S.mm(banks[B_KK][:, sl(q)], kTb, Sb[q][:, :], r=[kT2, Sb[q]], w=[bsub[B_KK][q]])
                    S.stt("dve", rr[:, q, :], banks[B_KK][:, sl(q)], NEGC[:, n, c:c + 1], vtk[:, q, :], ALU.mult, ALU.add,
                          r=[bsub[B_KK][q], NEGC, vtks[q]], w=[rrs[q]])
                for (q, d, hh, n, c) in chains:
                    S.mm(banks[B_QK][:, sl(q)], XT[:, q, :], rr[:, q, :], r=[XTs[q], rrs[q]], w=[bsub[B_QK][q]])
                S.copy("act", yb[:, :, :].rearrange("p a b -> p (a b)"), banks[B_QK][:, :], r=bsub[B_QK], w=ybs)
                for (q, d, hh, n, c) in chains:
                    qTb = qT2[:, hh, n * 128:(n + 1) * 128]
                    S.mm(banks[B_DT][:, sl(q)], qTb, Sb[q][:, :], r=[qT2, Sb[q]], w=[bsub[B_DT][q]])
                    S.mm(banks[B_PP][:, sl(q)], AqT[:, q, :], yb[:, q, :], r=[AqTs[q], ybs[q]], w=[bsub[B_PP][q]])
                    S.mm(banks[B_PT][:, sl(q)], kgb[:, q, :], yb[:, q, :], r=[kgbs[q], ybs[q]], w=[bsub[B_PT][q]])
                for (q, d, hh, n, c) in chains:
                    od = otok[:, n, hh, :]
                    ob = osub[n][hh]
                    if (n, hh) not in visited:
                        visited.add((n, hh))
                        S.act(od, banks[B_DT][:, sl(q)], AF.Copy, scale=EX[:, n, c:c + 1], r=[bsub[B_DT][q], EX], w=[ob])
                    else:
                        S.stt("dve", od, banks[B_DT][:, sl(q)], EX[:, n, c:c + 1], od, ALU.mult, ALU.add, r=[bsub[B_DT][q], EX, ob], w=[ob])
                    S.tt("dve", od, banks[B_PP][:, sl(q)], od, ALU.add, r=[bsub[B_PP][q], ob], w=[ob])
                    S.stt("dve", Sf[q][:, :], Sf[q][:, :], EX[:, n, 16 + c:17 + c], banks[B_PT][:, sl(q)], ALU.mult, ALU.add,
                          r=[Sf[q], EX, bsub[B_PT][q]], w=[Sf[q]])
                    S.copy("pool", Sb[q][:, :], Sf[q][:, :], r=[Sf[q]], w=[Sb[q]])
            st_ = S.sb([128, NBLK, 2, 4])
            junk = S.sb([128, 128])
            for n in range(NBLK):
                for hh in range(2):
                    S.act(junk[:, :], otok[:, n, hh, :], AF.Square, accum=st_[:, n, hh, 0:1], r=[osub[n][hh]], w=[junk, st_])
            S.ts("dve", st_[:, :, :, 1], st_[:, :, :, 0], 1.0 / 128, EPS, ALU.mult, ALU.add, r=[st_], w=[st_])
            S.act(st_[:, :, :, 2], st_[:, :, :, 1], AF.Ln, r=[st_], w=[st_])
            S.act(st_[:, :, :, 3], st_[:, :, :, 2], AF.Exp, scale=-0.5, r=[st_], w=[st_])
            zt = [S.sb([128, 512]) for _ in range(2)]
            yt = [S.sb([128, 512]) for _ in range(2)]
            yo = [S.sb([128, 512], BF16) for _ in range(2)]
            ev = 0
            for hh in range(2):
                h = 2 * hp + hh
                for (t0, n_) in TILES:
                    nb_ = n_ // 128
                    bk = banks[ev % 2]
                    for bi in range(nb_):
                        n = t0 // 128 + bi
                        S.act(otok[:, n, hh, :], otok[:, n, hh, :], AF.Copy, scale=st_[:, n, hh, 3:4], r=[osub[n][hh], st_], w=[osub[n][hh]])
                        S.tr(bk[:, bi * 128:(bi + 1) * 128], otok[:, n, hh, :], ident, r=[osub[n][hh], cst], w=[bk])
                    z_, y_, o_ = zt[ev % 2], yt[ev % 2], yo[ev % 2]
                    ev += 1
                    S.dma("sp", z_[:, 0:n_], zsrc[(CI_DZ + h) * 128:(CI_DZ + h + 1) * 128, t0:t0 + n_], w=[z_])
                    S.act(y_[:, 0:n_], bk[:, 0:n_], AF.Copy, scale=onc[:, 0:1], r=[bk, onc], w=[y_])
                    S.act(z_[:, 0:n_], z_[:, 0:n_], AF.Silu, r=[z_], w=[z_])
                    S.tt("dve", o_[:, 0:n_], y_[:, 0:n_], z_[:, 0:n_], ALU.mult, r=[y_, z_], w=[o_])
                    S.dma("act", g.YB[h * 128:(h + 1) * 128, t0:t0 + n_], o_[:, 0:n_], r=[o_])


def cm_view(ap):
    return ap.rearrange("p (r c) -> p c r", c=64)


def stage_c(g, l, s):
    nc = g.nc
    nsteps = int(os.environ.get("CSTEPS", str(NBLK)))
    with Stage(nc) as S:
        cst, cstb = g.cst, g.cstb
        CI = lambda o: cst[:, o:o + 128]
        CB = lambda o: cstb[:, o:o + 128]
        ident, identb = CI(C_ID), CB(C_ID)
        stag = S.sb([128, T])
        qcm = S.sb([128, T])
        kcm = S.sb([128, T])
        vcm = S.sb([128, 2, T], BF16)
        rcm = S.sb([64, T], BF16)
        otok = S.sb([128, NBLK, 2, 128])
        osub = [[Sub(otok) for _ in range(2)] for _ in range(NBLK)]
        ycm = S.sb([128, T])
        wr32 = S.sb([64, 256])
        wrb = S.sb([64, 256], BF16)
        onc = S.sb([128, 1])
        S.dma("sp", onc[:, :], g.gla_onorm[l, :].rearrange("(p o) -> p o", o=1), w=[onc], allow_slow_non_contiguous=True)
        S.memset("pool", wr32[:, :], 0.0, w=[wr32])
        for d in range(2):
            S.dma("sp", wr32[32 * d:32 * d + 16, :], g.gla_w_r2[l, d, :, :], w=[wr32])
            S.dma("sp", wr32[32 * d + 16:32 * d + 17, :], g.gla_b_r[l, d:d + 1, :], w=[wr32])
        S.copy("dve", wrb[:, :], wr32[:, :], r=[wr32], w=[wrb])
        S.dma("sp", stag[0:48, :], g.P[CI_GR * 128:CI_GR * 128 + 48, :], w=[stag])
        S.memset("pool", rcm[:, :], 1.0, w=[rcm])
        for d in range(2):
            S.copy("dve", rcm[32 * d:32 * d + 16, 0:NCTX], stag[32 * d:32 * d + 16, 0:NCTX], r=[stag], w=[rcm])
            S.copy("dve", rcm[32 * d:32 * d + 16, NCTX:T].rearrange("p (c r) -> p c r", r=64), cm_view(stag[32 * d:32 * d + 16, NCTX:T]), r=[stag], w=[rcm])
        Sf = [S.sb([128, 128]) for _ in range(2)]
        Sb = [S.sb([128, 128], BF16) for _ in range(2)]
        bL = [S.ps([128, 512]) for _ in range(2)]
        bV = [S.ps([128, 1024], BF16) for _ in range(2)]
        bA = [S.ps([128, 512]) for _ in range(2)]
        bK = S.ps([128, 512])
        bKs = [Sub(bK) for _ in range(2)]

        def pd(shape, dt=F32):
            return [S.sb(shape, dt) for _ in range(2)]
        et, lt = pd([128, 128]), pd([128, 128])
        lh, lhf, ll = pd([128, 128], BF16), pd([128, 128]), pd([128, 128], BF16)
        eq, ek, ekg = pd([128, 128]), pd([128, 128]), pd([128, 128])
        qg, qgpad, kinvpad, kgpad = pd([128, 128], BF16), pd([128, 2, 128], BF16), pd([128, 2, 128], BF16), pd([128, 2, 128], BF16)
        vtok = pd([128, 2, 128], BF16)
        atm = pd([128, 2, 128], BF16)
        for d in range(2):
            S.memset("pool", qgpad[d][:, :, :], 0.0, w=[qgpad[d]])
            S.memset("pool", kinvpad[d][:, :, :], 0.0, w=[kinvpad[d]])
            S.memset("pool", kgpad[d][:, :, :], 0.0, w=[kgpad[d]])
        ISC = 1.0 / 16.0

        for hp in range(2):
            for (dst, ci) in ((qcm, CI_GQ + hp), (kcm, CI_GK + hp)):
                S.dma("sp", stag[:, :], g.P[ci * 128:(ci + 1) * 128, :], w=[stag])
                S.copy("dve", dst[:, 0:NCTX], stag[:, 0:NCTX], r=[stag], w=[dst])
                S.copy("dve", dst[:, NCTX:T].rearrange("p (c r) -> p c r", r=64), cm_view(stag[:, NCTX:T]), r=[stag], w=[dst])
            for hh in range(2):
                ci = CI_GV + 2 * hp + hh
                S.dma("sp", stag[:, :], g.P[ci * 128:(ci + 1) * 128, :], w=[stag])
                S.copy("pool", vcm[:, hh, 0:NCTX], stag[:, 0:NCTX], r=[stag], w=[vcm])
                S.copy("pool", vcm[:, hh, NCTX:T].rearrange("p (c r) -> p c r", r=64), cm_view(stag[:, NCTX:T]), r=[stag], w=[vcm])
            for d in range(2):
                S.memset("pool", Sf[d][:, :], 0.0, w=[Sf[d]])
                S.memset("pool", Sb[d][:, :], 0.0, w=[Sb[d]])
            if nsteps < NBLK:
                S.memset("pool", otok[:, :, :, :], 0.0, w=[osub[n_][h_] for n_ in range(NBLK) for h_ in range(2)])
            visited = set()
            for i in range(nsteps):
                blks = [(0, ORD_F[i]), (1, ORD_B[i])]
                for (d, n) in blks:
                    bs = slice(n * 128, (n + 1) * 128)
                    S.mm(bL[d][:, 0:128], rcm[32 * d:32 * d + 17, bs], wrb[32 * d:32 * d + 17, hp * 128:(hp + 1) * 128], r=[rcm, wrb], w=[bL[d]])
                    S.tr(bL[d][:, 128:256], kcm[:, bs], ident, r=[kcm, cst], w=[bL[d]])
                    for hh in range(2):
                        S.tr(bV[d][:, hh * 128:(hh + 1) * 128], vcm[:, hh, bs], identb, r=[vcm, cstb], w=[bV[d]])
                for (d, n) in blks:
                    S.act(et[d][:, :], bL[d][:, 0:128], AF.Exp, scale=-1.0, r=[bL[d]], w=[et[d]])
                    S.act(lt[d][:, :], et[d][:, :], AF.Ln, bias=1.0, r=[et[d]], w=[lt[d]])
                    S.copy("dve", lh[d][:, :], lt[d][:, :], r=[lt[d]], w=[lh[d]])
                    S.copy("dve", lhf[d][:, :], lh[d][:, :], r=[lh[d]], w=[lhf[d]])
                    S.tt("dve", ll[d][:, :], lt[d][:, :], lhf[d][:, :], ALU.subtract, r=[lt[d], lhf[d]], w=[ll[d]])
                    S.copy("act", vtok[d][:, :, :].rearrange("p a b -> p (a b)"), bV[d][:, 0:256], r=[bV[d]], w=[vtok[d]])
                for (d, n) in blks:
                    tri = CB(C_TRIU) if d == 0 else CB(C_TRIL)
                    tris = CB(C_TRILS) if d == 0 else CB(C_TRIUS)
                    S.mm(bL[d][:, 384:512], lh[d][:, :], tri, start=True, stop=False, r=[lh[d], cstb], w=[bL[d]])
                    S.mm(bL[d][:, 384:512], ll[d][:, :], tri, start=False, stop=True, r=[ll[d], cstb], w=[bL[d]])
                    S.mm(bL[d][:, 256:384], tris, lh[d][:, :], start=True, stop=False, r=[lh[d], cstb], w=[bL[d]])
                    S.mm(bL[d][:, 256:384], tris, ll[d][:, :], start=False, stop=True, r=[ll[d], cstb], w=[bL[d]])
                for (d, n) in blks:
                    bs = slice(n * 128, (n + 1) * 128)
                    S.act(eq[d][:, :], bL[d][:, 384:512], AF.Exp, scale=-ISC, r=[bL[d]], w=[eq[d]])
                    S.act(ek[d][:, :], bL[d][:, 384:512], AF.Exp, scale=ISC, r=[bL[d]], w=[ek[d]])
                    S.act(ekg[d][:, :], bL[d][:, 256:384], AF.Exp, scale=-ISC, r=[bL[d]], w=[ekg[d]])
                    S.stt("dve", qg[d][:, :], qcm[:, bs], 0.125, eq[d][:, :], ALU.mult, ALU.mult, r=[qcm, eq[d]], w=[qg[d]])
                    for hh in range(2):
                        ps_ = slice(64 * hh, 64 * hh + 64)
                        S.copy("pool", qgpad[d][ps_, hh, :], qg[d][ps_, :], r=[qg[d]], w=[qgpad[d]])
                        S.tt("pool", kinvpad[d][ps_, hh, :], kcm[ps_, bs], ek[d][ps_, :], ALU.mult, r=[kcm, ek[d]], w=[kinvpad[d]])
                        S.tt("dve", kgpad[d][:, hh, 64 * hh:64 * hh + 64], bL[d][:, 128 + 64 * hh:128 + 64 * hh + 64], ekg[d][:, 64 * hh:64 * hh + 64], ALU.mult,
                             r=[bL[d], ekg[d]], w=[kgpad[d]])
                for (d, n) in blks:
                    for hh in range(2):
                        S.mm(bA[d][:, hh * 128:(hh + 1) * 128], kinvpad[d][:, hh, :], qg[d][:, :], r=[kinvpad[d], qg[d]], w=[bA[d]])
                    msk = CI(C_TRIU) if d == 0 else CI(C_TRIL)
                    for hh in range(2):
                        S.tt("dve", atm[d][:, hh, :], bA[d][:, hh * 128:(hh + 1) * 128], msk, ALU.mult, r=[bA[d], cst], w=[atm[d]])
                    for hh in range(2):
                        S.mm(bA[d][:, 256 + hh * 128:256 + (hh + 1) * 128], qgpad[d][:, hh, :], Sb[d][:, :], start=True, stop=False, r=[qgpad[d], Sb[d]], w=[bA[d]])
                        S.mm(bA[d][:, 256 + hh * 128:256 + (hh + 1) * 128], atm[d][:, hh, :], vtok[d][:, hh, :], start=False, stop=True, r=[atm[d], vtok[d]], w=[bA[d]])
                    for hh in range(2):
                        S.mm(bK[:, d * 128:(d + 1) * 128], kgpad[d][:, hh, :], vtok[d][:, hh, :], start=(hh == 0), stop=(hh == 1), r=[kgpad[d], vtok[d]], w=[bKs[d]])
                    for hh in range(2):
                        od = otok[:, n, hh, :]
                        ob = osub[n][hh]
                        src = bA[d][:, 256 + hh * 128:256 + (hh + 1) * 128]
                        if (n, hh) not in visited:
                            visited.add((n, hh))
                            S.copy("act", od, src, r=[bA[d]], w=[ob])
                        else:
                            S.tt("dve", od, src, od, ALU.add, r=[bA[d], ob], w=[ob])
                    gl = eq[d][:, 127:128] if d == 0 else eq[d][:, 0:1]
                    S.stt("dve", Sf[d][:, :], Sf[d][:, :], gl, bK[:, d * 128:(d + 1) * 128], ALU.mult, ALU.add, r=[Sf[d], eq[d], bKs[d]], w=[Sf[d]])
                    S.copy("pool", Sb[d][:, :], Sf[d][:, :], r=[Sf[d]], w=[Sb[d]])
            st_ = S.sb([128, NBLK, 2, 4])
            junk = S.sb([128, 128])
            for n in range(NBLK):
                for hh in range(2):
                    S.act(junk[:, :], otok[:, n, hh, :], AF.Square, accum=st_[:, n, hh, 0:1], r=[osub[n][hh]], w=[junk, st_])
            S.ts("dve", st_[:, :, :, 1], st_[:, :, :, 0], 1.0 / 128, EPS, ALU.mult, ALU.add, r=[st_], w=[st_])
            S.act(st_[:, :, :, 2], st_[:, :, :, 1], AF.Ln, r=[st_], w=[st_])
            S.act(st_[:, :, :, 3], st_[:, :, :, 2], AF.Exp, scale=-0.5, r=[st_], w=[st_])
            yo = S.sb([128, T], BF16)
            for hh in range(2):
                h = 2 * hp + hh
                for g4 in range(0, NBLK, 4):
                    nb_ = min(4, NBLK - g4)
                    bk = bL[(g4 // 4) % 2]
                    for bi in range(nb_):
                        n = g4 + bi
                        S.act(otok[:, n, hh, :], otok[:, n, hh, :], AF.Copy, scale=st_[:, n, hh, 3:4], r=[osub[n][hh], st_], w=[osub[n][hh]])
                        S.tr(bk[:, bi * 128:(bi + 1) * 128], otok[:, n, hh, :], ident, r=[osub[n][hh], cst], w=[bk])
                    S.act(ycm[:, g4 * 128:(g4 + nb_) * 128], bk[:, 0:nb_ * 128], AF.Copy, scale=onc[:, 0:1], r=[bk, onc], w=[ycm])
                S.dma("sp", stag[:, :], g.P[(CI_GZ + h) * 128:(CI_GZ + h + 1) * 128, :], w=[stag])
                S.act(stag[:, :], stag[:, :], AF.Silu, r=[stag], w=[stag])
                S.tt("dve", yo[:, 0:NCTX], ycm[:, 0:NCTX], stag[:, 0:NCTX], ALU.mult, r=[ycm, stag], w=[yo])
                S.tt("dve", yo[:, NCTX:T].rearrange("p (r c) -> p r c", c=64), ycm[:, NCTX:T].rearrange("p (c r) -> p r c", r=64),
                     stag[:, NCTX:T].rearrange("p (r c) -> p r c", c=64), ALU.mult, r=[ycm, stag], w=[yo])
                S.dma("act", g.YC[h * 128:(h + 1) * 128, :], yo[:, :], r=[yo])


def build_nc(nlayer=4, nseq=2, debug=None, dbg_what=None):
    nc = bass.Bass("TRN2", target_bir_lowering=False)
    g = Ctx()
    g.nc = nc
    dt = lambda name, shape, kind="ExternalInput", dtype=F32: nc.dram_tensor(name, list(shape), dtype, kind=kind).ap()
    g.x_in = dt("x", [2, NLAT, D])
    g.ctx_in = dt("ctx", [2, NCTX, D])
    g.c_in = dt("c", [2, D])
    g.cctx_in = dt("c_ctx", [1, D])
    g.w_mod = dt("w_mod", [4, D, 3 * D])
    g.b_mod = dt("b_mod", [4, 3 * D])
    g.norm_w = dt("norm_w", [4, D])
    g.w_in = dt("w_in", [4, D, INW])
    g.b_merge = dt("b_merge", [4, 3 * D])
    g.conv_a = dt("conv_a", [4, 3, 512])
    g.conv_dn = dt("conv_dn", [4, 3, 1536])
    g.dn_a_log = dt("dn_a_log", [4, 8])
    g.dn_dt_bias = dt("dn_dt_bias", [4, 8])
    g.dn_onorm = dt("dn_onorm", [4, 128])
    g.gla_w_r2 = dt("gla_w_r2", [4, 2, 16, 256])
    g.gla_b_r = dt("gla_b_r", [4, 2, 256])
    g.gla_onorm = dt("gla_onorm", [4, 128])
    g.w_pa = dt("w_pa", [4, 512, D])
    g.w_pb = dt("w_pb", [4, 512, D])
    g.w_pc = dt("w_pc", [4, 512, D])
    g.w_o = dt("w_o", [4, D, D])
    g.fnw = dt("final_norm_w", [1, D])
    g.cst_in = dt("cst", [128, NCST])
    g.out = dt("out", [2, NLAT, D], kind="ExternalOutput")
    import os
    g.P = nc.dram_tensor("P_scr", [(1 if os.environ.get("SMALLP") else NCH) * 128, T], F32).ap()
    g.GB = nc.dram_tensor("GB_scr", [T, 16], F32).ap()
    g.xrl = nc.dram_tensor("xrl_scr", [2, NLAT, D], F32).ap()
    g.xrc = nc.dram_tensor("xrc_scr", [2, NCTX, D], F32).ap()
    g.YA = nc.dram_tensor("YA_scr", [512, T], BF16).ap()
    g.YB = nc.dram_tensor("YB_scr", [512, T], BF16).ap()
    g.YC = nc.dram_tensor("YC_scr", [512, T], BF16).ap()
    g.QT = nc.dram_tensor("QT_scr", [512, T], BF16).ap()
    g.KT = nc.dram_tensor("KT_scr", [512, T], BF16).ap()
    g.VT = nc.dram_tensor("VT_scr", [512, T], BF16).ap()
    if debug is not None:
        g.dbg = dt("dbg", debug, kind="ExternalOutput", dtype=(BF16 if ((dbg_what in ("a", "bprep", "b", "c") or os.environ.get("DBGBF16")) and not os.environ.get("DBGF32")) else F32))

    with contextlib.ExitStack() as top:
        def psb(name, shape, dtype=F32):
            return TL(top.enter_context(nc.sbuf_tensor(name, list(shape), dtype)))
        g.cst = psb("cst_sb", [128, NCST])
        g.modcol = psb("modcol", [128, 24, 3])
        g.A1 = psb("A1", [128, 8, 3])
        g.Gb = psb("Gb", [128, 3, D])
        g.epsc = psb("epsc", [128, 2])
        g.cstb = psb("cstb_sb", [128, NCST], BF16)
        with Stage(nc) as S:
            S.dma("sp", g.cst[:, :], g.cst_in, w=[g.cst])
            S.memset("pool", g.epsc[:, :], EPS, w=[g.epsc])
            S.copy("dve", g.cstb[:, :], g.cst[:, :], r=[g.cst], w=[g.cstb])
        def dump2d(src, rows, dtype_rows=128):
            with Stage(nc) as S:
                for r0 in range(0, rows, 128):
                    S.dma("sp", g.dbg[r0:r0 + 128, :], src[r0:r0 + 128, :])
        for l in range(nlayer):
            stage_mod(g, l)
            if dbg_what == "mod":
                with Stage(nc) as S:
                    S.dma("sp", g.dbg[:, 0:72], g.modcol[:, :, :].rearrange("p a b -> p (a b)"), r=[g.modcol])
                    S.dma("sp", g.dbg[:, 72:96], g.A1[:, :, :].rearrange("p a b -> p (a b)"), r=[g.A1])
                    S.dma("sp", g.dbg[:, 96:96 + 3072], g.Gb[:, :, :].rearrange("p a b -> p (a b)"), r=[g.Gb])
                return nc
            for s in range(nseq):
                stage_proj(g, l, s)
                if dbg_what == "proj":
                    dump2d(g.P, NCH * 128)
                    return nc
                stage_a(g, l, s)
                if dbg_what == "hook":
                    return nc
                if dbg_what == "proj2":
                    dump2d(g.P, NCH * 128)
                    return nc
                if dbg_what == "a":
                    dump2d(g.YA, 512)
                    return nc
                stage_bprep(g, l, s)
                if dbg_what == "bprep":
                    dump2d(g.QT, 512)
                    with Stage(nc) as S:
                        for r0 in range(0, 512, 128):
                            S.dma("sp", g.dbg[512 + r0:512 + r0 + 128, :], g.KT[r0:r0 + 128, :])
                            S.dma("sp", g.dbg[1024 + r0:1024 + r0 + 128, :], g.VT[r0:r0 + 128, :])
                    return nc
                stage_bscan(g, l, s)
                if os.environ.get("BDBG"):
                    return nc
                if dbg_what == "b":
                    dump2d(g.YB, 512)
                    return nc
                stage_c(g, l, s)
                if dbg_what == "c":
                    dump2d(g.YC, 512)
                    return nc
                stage_merge(g, l, s, last=(l == nlayer - 1) and not os.environ.get("MERGE_NOTLAST"))
                if dbg_what == "merge":
                    with Stage(nc) as S:
                        for r0 in range(0, NLAT, 128):
                            S.dma("sp", g.dbg[r0:r0 + 128, :], g.xrl[s, r0:r0 + 128, :])
                        for r0 in range(0, NCTX, 128):
                            S.dma("sp", g.dbg[NLAT + r0:NLAT + r0 + 128, :], g.xrc[s, r0:r0 + 128, :])
                    return nc
    return nc


_NC_CACHE = {}


def kernel(**inputs):
    n = 8
    if "nc" not in _NC_CACHE:
        _NC_CACHE["nc"] = build_nc()
    nc = _NC_CACHE["nc"]
    cst = make_consts()
    f = lambda k: np.ascontiguousarray(np.asarray(inputs[k], dtype=np.float32))
    shared = {
        "c_ctx": f("c_ctx").reshape(1, D), "w_mod": f("w_mod"), "b_mod": f("b_mod"), "norm_w": f("norm_w"),
        "w_in": f("w_in"), "b_merge": f("b_merge"), "conv_a": f("conv_a"), "conv_dn": f("conv_dn"),
        "dn_a_log": f("dn_a_log").reshape(4, 8), "dn_dt_bias": f("dn_dt_bias").reshape(4, 8),
        "dn_onorm": f("dn_onorm"), "gla_w_r2": f("gla_w_r2"), "gla_b_r": f("gla_b_r"), "gla_onorm": f("gla_onorm"),
        "w_pa": f("w_pa"), "w_pb": f("w_pb"), "w_pc": f("w_pc"), "w_o": f("w_o"),
        "final_norm_w": f("final_norm_w").reshape(1, D), "cst": cst,
    }
    x = f("x")
    c = f("c")
    ctx = f("ctx")
    in_maps = []
    for i in range(n):
        m = dict(shared)
        m["x"] = x[2 * i:2 * i + 2]
        m["c"] = c[2 * i:2 * i + 2]
        m["ctx"] = ctx[2 * i:2 * i + 2]
        in_maps.append(m)
    res = run_bass_kernel_spmd(nc, in_maps, core_ids=list(range(n)))
    return np.concatenate([r["out"] for r in res.results], axis=0)
```

```python
import contextlib
import os
import numpy as np
import concourse.bass as bass
import concourse.mybir as mybir
from concourse.bass_utils import run_bass_kernel_spmd

F32 = mybir.dt.float32
BF16 = mybir.dt.bfloat16
ALU = mybir.AluOpType
AF = mybir.ActivationFunctionType

ENGS = ("pe", "act", "dve", "pool", "sp")
N_DMA_SLOTS = 40

D = 1024
T = 4352
NBLK = 34
NCTX = 256
NLAT = 4096
EPS = 1e-6
NCH = 69


class Buf:
    __slots__ = ("w", "r", "pid")

    def __init__(self):
        self.w = None
        self.r = {}
        self.pid = None


class TL:
    def __init__(self, t):
        self.t = t
        self.b = Buf()
        self.bank = None

    def __getitem__(self, idx):
        return self.t[idx]


def _banks(xs):
    out = []
    for x in xs:
        bk = getattr(x, "bank", None)
        if bk is not None and bk not in out:
            out.append(bk)
    return out


def _bufs(xs):
    out = []
    for x in xs:
        if x is None:
            continue
        out.append(x.b if hasattr(x, "b") else x)
    return out


class Prog:
    def __init__(self, nc):
        self.nc = nc
        self.streams = {e: [] for e in ENGS}
        self.count = {e: 0 for e in ENGS}
        self.known = {e: {} for e in ENGS}
        self.slot_val = [0] * N_DMA_SLOTS
        self.next_slot = 0

    def _wait(self, eng, tok, same_ok=False):
        if tok is None:
            return
        key, val = tok
        if key == eng and not same_ok:
            return
        if self.known[eng].get(key, 0) >= val:
            return
        self.known[eng][key] = val
        self.streams[eng].append(("wait", key, val))

    def _deps(self, eng, reads, writes):
        for b in list(reads) + list(writes):
            if b.pid is not self:
                b.pid = self
                b.w = None
                b.r = {}
        so = (eng != "pe")
        for b in reads:
            self._wait(eng, b.w, same_ok=so)
        for b in writes:
            self._wait(eng, b.w, same_ok=so)
            for k, v in b.r.items():
                self._wait(eng, (k, v), same_ok=so)

    def _mark(self, tok, reads, writes):
        k, v = tok
        for b in reads:
            if b.r.get(k, 0) < v:
                b.r[k] = v
        for b in writes:
            b.w = tok
            b.r = {}

    def op(self, eng, fn, reads=(), writes=()):
        banks = _banks(list(reads) + list(writes))
        reads = _bufs(reads)
        writes = _bufs(writes)
        self._deps(eng, reads, writes)
        for bk in banks:
            if bk.pid is not self:
                bk.pid = self
                bk.w = None
            self._wait(eng, bk.w)
        self.count[eng] += 1
        self.streams[eng].append(("inst", fn))
        self._mark((eng, self.count[eng]), reads, writes)
        for bk in banks:
            bk.w = (eng, self.count[eng])

    def dma(self, q, out, in_, reads=(), writes=(), **kw):
        reads = _bufs(reads)
        writes = _bufs(writes)
        self._deps(q, reads, writes)
        s = self.next_slot
        self.next_slot = (s + 1) % N_DMA_SLOTS
        key = "d%d" % s
        if self.slot_val[s] > 0:
            self._wait(q, (key, self.slot_val[s]))
        self.slot_val[s] += 16
        self.streams[q].append(("dma", out, in_, key, kw))
        self._mark((key, self.slot_val[s]), reads, writes)

    def check(self):
        sem = {}
        pos = {e: 0 for e in ENGS}
        progress = True
        while progress:
            progress = False
            for e in ENGS:
                st = self.streams[e]
                while pos[e] < len(st):
                    it = st[pos[e]]
                    if it[0] == "wait":
                        if sem.get(it[1], 0) < it[2]:
                            break
                    elif it[0] == "inst":
                        sem[e] = sem.get(e, 0) + 1
                    else:
                        sem[it[3]] = sem.get(it[3], 0) + 16
                    pos[e] += 1
                    progress = True
        for e in ENGS:
            if pos[e] < len(self.streams[e]):
                raise RuntimeError("sync deadlock: %s stuck at %d/%d on %r (sem=%r)" % (
                    e, pos[e], len(self.streams[e]), self.streams[e][pos[e]], sem.get(self.streams[e][pos[e]][1])))

    def emit(self):
        nc = self.nc
        for s in range(N_DMA_SLOTS):
            if self.slot_val[s] > 0:
                self._wait("sp", ("d%d" % s, self.slot_val[s]))
        for e in ENGS:
            if e != "sp" and self.count[e] > 0:
                self._wait("sp", (e, self.count[e]))
        self.count["sp"] += 1
        self.streams["sp"].append(("inst", lambda e: e.nop()))
        for e in ENGS:
            if e != "sp":
                self.streams[e].append(("wait", "sp", self.count["sp"]))
        self.check()
        with contextlib.ExitStack() as st:
            sems = {}
            for e in ENGS:
                sems[e] = st.enter_context(nc.semaphore("s_" + e))
            for s in range(N_DMA_SLOTS):
                if self.slot_val[s] > 0:
                    sems["d%d" % s] = st.enter_context(nc.semaphore("s_d%d" % s))
            nc.all_engine_barrier()
            for sm in sems.values():
                nc.gpsimd.sem_clear(sm)
            nc.all_engine_barrier()
            block = st.enter_context(nc.Block())

            def run(engname):
                stream = self.streams[engname]

                def body(eng):
                    for it in stream:
                        if it[0] == "wait":
                            eng.wait_ge(sems[it[1]], it[2])
                        elif it[0] == "inst":
                            it[1](eng).then_inc(sems[engname], 1)
                        else:
                            _, out, in_, key, kw = it
                            eng.dma_start(out=out, in_=in_, **kw).then_inc(sems[key], 16)
                return body

            block.tensor(run("pe"))
            block.scalar(run("act"))
            block.vector(run("dve"))
            block.gpsimd(run("pool"))
            block.sync(run("sp"))


_UID = [0]


class Stage:
    def __init__(self, nc):
        self.nc = nc
        self.st = contextlib.ExitStack()
        self.P = Prog(nc)
        self.n = 0

    def __enter__(self):
        self.st.__enter__()
        return self

    def __exit__(self, *a):
        if a[0] is None:
            self.P.emit()
        return self.st.__exit__(*a)

    def sb(self, shape, dt=F32):
        _UID[0] += 1
        return TL(self.st.enter_context(self.nc.sbuf_tensor("sb_%d" % _UID[0], list(shape), dt)))

    def ps(self, shape=(128, 512), dt=F32):
        _UID[0] += 1
        t = TL(self.st.enter_context(self.nc.psum_tensor("ps_%d" % _UID[0], list(shape), dt)))
        t.bank = Buf()
        return t

    def mm(self, out, lhsT, rhs, start=True, stop=True, r=(), w=()):
        self.P.op("pe", lambda e: e.matmul(out, lhsT=lhsT, rhs=rhs, start=start, stop=stop), r, w)

    def tr(self, out, in_, ident, r=(), w=()):
        self.P.op("pe", lambda e: e.transpose(out, in_, ident), r, w)

    def act(self, out, in_, func, bias=None, scale=None, accum=None, r=(), w=()):
        kw = {}
        if bias is not None:
            kw["bias"] = bias
        if scale is not None:
            kw["scale"] = scale
        if accum is not None:
            kw["accum_out"] = accum
        self.P.op("act", lambda e: e.activation(out=out, in_=in_, func=func, **kw), r, w)

    def ts(self, eng, out, in0, s1, s2, op0, op1=None, r=(), w=()):
        if op1 is None:
            self.P.op(eng, lambda e: e.tensor_scalar(out=out, in0=in0, scalar1=s1, scalar2=0.0, op0=op0, op1=ALU.add), r, w)
        else:
            self.P.op(eng, lambda e: e.tensor_scalar(out=out, in0=in0, scalar1=s1, scalar2=s2, op0=op0, op1=op1), r, w)

    def stt(self, eng, out, in0, scalar, in1, op0, op1, r=(), w=()):
        self.P.op(eng, lambda e: e.scalar_tensor_tensor(out=out, in0=in0, scalar=scalar, in1=in1, op0=op0, op1=op1), r, w)

    def tt(self, eng, out, in0, in1, op, r=(), w=()):
        self.P.op(eng, lambda e: e.tensor_tensor(out=out, in0=in0, in1=in1, op=op), r, w)

    def copy(self, eng, out, in_, r=(), w=()):
        if eng == "act":
            self.P.op("act", lambda e: e.copy(out=out, in_=in_), r, w)
        else:
            self.P.op(eng, lambda e: e.tensor_copy(out=out, in_=in_), r, w)

    def memset(self, eng, ap, val, w=()):
        self.P.op(eng, lambda e: e.memset(ap, val), (), w)

    def dma(self, q, out, in_, r=(), w=(), **kw):
        self.P.dma(q, out, in_, r, w, **kw)


C_ID, C_TRIU, C_TRIL, C_TRIUS, C_TRILS, C_ONES, C_NEGF, C_NEGB, C_SEL = [i * 128 for i in range(9)]
C_MD16, C_MS16, C_MS32, C_MS64 = [11 * 128 + i * 512 for i in range(4)]
NCST = 11 * 128 + 4 * 512


def make_consts():
    c = np.zeros((128, NCST), np.float32)
    t = np.arange(128)[:, None]
    i = np.arange(128)[None, :]
    c[:, C_ID:C_ID + 128] = (t == i)
    c[:, C_TRIU:C_TRIU + 128] = (t <= i)
    c[:, C_TRIL:C_TRIL + 128] = (t >= i)
    c[:, C_TRIUS:C_TRIUS + 128] = (t < i)
    c[:, C_TRILS:C_TRILS + 128] = (t > i)
    c[:, C_ONES:C_ONES + 128] = 1.0
    c[:, C_NEGF:C_NEGF + 128] = np.where(t <= i, 0.0, -1e5)
    c[:, C_NEGB:C_NEGB + 128] = np.where(t >= i, 0.0, -1e5)
    for s in range(3):
        c[s, C_SEL + s * 128:C_SEL + (s + 1) * 128] = 1.0
    md16 = (t // 16 == i // 16)
    for off, b in ((C_MS16, 16), (C_MS32, 32), (C_MS64, 64)):
        m = (t // (2 * b) == i // (2 * b)) & (t // b != i // b)
        for q in range(4):
            c[:, off + q * 128:off + (q + 1) * 128] = m
    for q in range(4):
        c[:, C_MD16 + q * 128:C_MD16 + (q + 1) * 128] = md16
    return c


O_AB, O_AC, O_AX, O_AZ = 0, 512, 1024, 1536
O_Q, O_K, O_V, O_DZ = 2048, 2560, 3072, 3584
O_DA, O_DB = 4096, 4104
O_GQ, O_GK, O_GV, O_GZ, O_GR, O_MG = 4112, 4368, 4624, 5136, 5648, 5680
INW = 8752


def chunk_list():
    ch = []
    for c0 in range(0, 4096, 128):
        ch.append((c0, 128))
    for c0 in range(O_GQ, O_GR, 128):
        ch.append((c0, 128))
    ch.append((O_GR, 32))
    for c0 in range(O_MG, INW, 128):
        ch.append((c0, 128))
    assert len(ch) == NCH
    return ch


CH = chunk_list()
CI_AB, CI_AC, CI_AX, CI_AZ = 0, 4, 8, 12
CI_Q, CI_K, CI_V, CI_DZ = 16, 20, 24, 28
CI_GQ, CI_GK, CI_GV, CI_GZ, CI_GR, CI_MG = 32, 34, 36, 40, 44, 45


class Ctx:
    pass


def tok_src(g, l, s, n):
    if n < 2:
        base = g.ctx_in if l == 0 else g.xrc
        return base[s, n * 128:(n + 1) * 128, :]
    base = g.x_in if l == 0 else g.xrl
    return base[s, (n - 2) * 128:(n - 1) * 128, :]


def stage_mod(g, l):
    nc = g.nc
    with Stage(nc) as S:
        c3T = S.sb([128, 8, 3])
        sc3T = S.sb([128, 8, 3])
        bcol = S.sb([128, 24])
        ncol = S.sb([128, 8])
        brow = S.sb([3, 1024])
        grow = S.sb([3, 1024])
        wm = [S.sb([128, 8, 512]) for _ in range(2)]
        pm = S.ps([128, 512])
        pr = S.ps([128, 512])
        pg = [S.ps([128, 512]) for _ in range(2)]
        for s in range(2):
            S.dma("sp", c3T[:, :, s], g.c_in[s, :].rearrange("(k p) -> p k", p=128), w=[c3T], allow_slow_non_contiguous=True)
        S.dma("sp", c3T[:, :, 2], g.cctx_in[0, :].rearrange("(k p) -> p k", p=128), w=[c3T], allow_slow_non_contiguous=True)
        S.dma("sp", bcol[:, :], g.b_mod[l, :].rearrange("(c p) -> p c", p=128), w=[bcol], allow_slow_non_contiguous=True)
        S.dma("sp", ncol[:, :], g.norm_w[l, :].rearrange("(c p) -> p c", p=128), w=[ncol], allow_slow_non_contiguous=True)
        S.dma("sp", brow[:, :], g.b_mod[l:l + 1, 2048:3072].partition_broadcast(3), w=[brow])
        S.act(sc3T[:, :, :], c3T[:, :, :], AF.Silu, r=[c3T], w=[sc3T])
        import os
        lvl = int(os.environ.get("MODLVL", "99"))
        wv = g.w_mod[l].rearrange("(k p) n -> p k n", p=128)
        for cg in range(6 if lvl >= 2 else 0):
            wt = wm[cg % 2]
            S.dma("sp", wt[:, :, :], wv[:, :, cg * 512:(cg + 1) * 512], w=[wt])
            for j in range(4):
                chn = cg * 4 + j
                for k in range(8):
                    S.mm(pm[:, j * 4:j * 4 + 3], wt[:, k, j * 128:(j + 1) * 128], sc3T[:, k, :], start=(k == 0), stop=(k == 7), r=[wt, sc3T], w=[pm])
                S.ts("dve", g.modcol[:, chn, :], pm[:, j * 4:j * 4 + 3], bcol[:, chn:chn + 1], None, ALU.add, r=[pm, bcol], w=[g.modcol])
            if cg >= 4 and lvl >= 3:
                for k in range(8):
                    S.mm(pr[0:3, :], sc3T[:, k, :], wt[:, k, :], start=(k == 0), stop=(k == 7), r=[wt, sc3T], w=[pr])
                S.tt("dve", grow[:, (cg - 4) * 512:(cg - 3) * 512], pr[0:3, :], brow[:, (cg - 4) * 512:(cg - 3) * 512], ALU.add, r=[pr, brow], w=[grow])
        for s in range(3 if lvl >= 4 else 0):
            for cg in range(2):
                p = pg[cg]
                S.mm(p[:, :], g.cst[0:3, C_SEL + s * 128:C_SEL + (s + 1) * 128], grow[0:3, cg * 512:(cg + 1) * 512], r=[grow, g.cst], w=[p])
                S.copy("act", g.Gb[:, s, cg * 512:(cg + 1) * 512], p[:, :], r=[p], w=[g.Gb])
        for k in range(8 if lvl >= 5 else 0):
            S.ts("dve", g.A1[:, k, :], g.modcol[:, 8 + k, :], 1.0, ncol[:, k:k + 1], ALU.add, ALU.mult, r=[g.modcol, ncol], w=[g.A1])


def stage_proj(g, l, s):
    nc = g.nc
    with Stage(nc) as S:
        hT = S.sb([128, 8, T], BF16)
        xb = [S.sb([128, D]) for _ in range(2)]
        xn = [S.sb([128, D]) for _ in range(2)]
        junk = S.sb([128, D])
        stat = S.sb([128, NBLK, 4])
        wst = [S.sb([128, 8, 512]) for _ in range(2)]
        wb = [S.sb([128, 8, 512], BF16) for _ in range(2)]
        stg = [S.sb([128, 512]) for _ in range(4)]
        wab32 = S.sb([128, 8, 16])
        wab = S.sb([128, 8, 16], BF16)
        gbst = S.sb([128, NBLK, 16])
        ptr = [S.ps([128, 512]) for _ in range(2)]
        pp = [S.ps([128, 512]) for _ in range(4)]
        pab = S.ps([128, 512])
        ident = g.cst[:, C_ID:C_ID + 128]
        for n in range(NBLK):
            sidx = 2 if n < 2 else s
            x_t = xb[n % 2]
            xn_t = xn[n % 2]
            S.dma("sp", x_t[:, :], tok_src(g, l, s, n), w=[x_t])
            S.act(junk[:, :], x_t[:, :], AF.Square, accum=stat[:, n, 0:1], r=[x_t], w=[junk, stat])
            S.ts("dve", stat[:, n, 1:2], stat[:, n, 0:1], 1.0 / D, EPS, ALU.mult, ALU.add, r=[stat], w=[stat])
            S.act(stat[:, n, 2:3], stat[:, n, 1:2], AF.Ln, r=[stat], w=[stat])
            S.act(stat[:, n, 3:4], stat[:, n, 2:3], AF.Exp, scale=-0.5, r=[stat], w=[stat])
            S.ts("dve", xn_t[:, :], x_t[:, :], stat[:, n, 3:4], None, ALU.mult, r=[x_t, stat], w=[xn_t])
            for k in range(8):
                p = ptr[k // 4]
                S.tr(p[:, (k % 4) * 128:(k % 4 + 1) * 128], xn_t[:, k * 128:(k + 1) * 128], ident, r=[xn_t, g.cst], w=[p])
            for k in range(8):
                p = ptr[k // 4]
                src = p[:, (k % 4) * 128:(k % 4 + 1) * 128]
                dst = hT[:, k, n * 128:(n + 1) * 128]
                if k // 4 == 0:
                    S.ts("dve", dst, src, g.A1[:, k, sidx:sidx + 1], g.modcol[:, k, sidx:sidx + 1], ALU.mult, ALU.add, r=[p, g.A1, g.modcol], w=[hT])
                else:
                    S.act(dst, src, AF.Identity, bias=g.modcol[:, k, sidx:sidx + 1], scale=g.A1[:, k, sidx:sidx + 1], r=[p, g.A1, g.modcol], w=[hT])
        wv = g.w_in[l].rearrange("(k p) n -> p k n", p=128)
        S.dma("sp", wab32[:, :, :], wv[:, :, O_DA:O_DA + 16], w=[wab32])
        S.copy("dve", wab[:, :, :], wab32[:, :, :], r=[wab32], w=[wab])
        for n in range(NBLK):
            for k in range(8):
                S.mm(pab[:, (n % 8) * 16:(n % 8 + 1) * 16], hT[:, k, n * 128:(n + 1) * 128], wab[:, k, :], start=(k == 0), stop=(k == 7), r=[hT, wab], w=[pab])
            S.copy("dve", gbst[:, n, :], pab[:, (n % 8) * 16:(n % 8 + 1) * 16], r=[pab], w=[gbst])
        S.dma("act", g.GB[:, :].rearrange("(n p) c -> p n c", p=128), gbst[:, :, :], r=[gbst])
        tiles = [(0, NCTX)] + [(NCTX + i * 512, 512) for i in range(8)]
        ngrp = (NCH + 3) // 4
        ev = 0
        for gi in range(ngrp):
            chs = list(range(gi * 4, min(NCH, gi * 4 + 4)))
            ws = wst[gi % 2]
            wbt = wb[gi % 2]
            for j, c in enumerate(chs):
                c0, ncol_ = CH[c]
                if c == CI_GR:
                    S.memset("pool", ws[:, :, j * 128:(j + 1) * 128], 0.0, w=[ws])
                    S.dma("sp", ws[:, :, j * 128:j * 128 + 16], wv[:, :, c0:c0 + 16], w=[ws])
                    S.dma("sp", ws[:, :, j * 128 + 32:j * 128 + 48], wv[:, :, c0 + 16:c0 + 32], w=[ws])
                else:
                    S.dma("sp", ws[:, :, j * 128:j * 128 + ncol_], wv[:, :, c0:c0 + ncol_], w=[ws])
            nw = len(chs) * 128
            S.copy("pool" if gi % 2 == 0 else "dve", wbt[:, :, 0:nw], ws[:, :, 0:nw], r=[ws], w=[wbt])
            for (t0, n) in tiles:
                for j, c in enumerate(chs):
                    nrow = 48 if c == CI_GR else CH[c][1]
                    p = pp[ev % 4]
                    sg = stg[ev % 4]
                    for k in range(8):
                        S.mm(p[0:nrow, 0:n], wbt[:, k, j * 128:j * 128 + nrow], hT[:, k, t0:t0 + n], start=(k == 0), stop=(k == 7), r=[wbt, hT], w=[p])
                    if ev % 2 == 0:
                        S.copy("act", sg[0:nrow, 0:n], p[0:nrow, 0:n], r=[p], w=[sg])
                    else:
                        S.copy("dve", sg[0:nrow, 0:n], p[0:nrow, 0:n], r=[p], w=[sg])
                    S.dma("act", g.P[c * 128:c * 128 + nrow, t0:t0 + n], sg[0:nrow, 0:n], r=[sg])
                    ev += 1


TILES = [(0, NCTX)] + [(NCTX + i * 512, 512) for i in range(8)]


def rows_view(ap, n, lo, hi):
    if n == NCTX:
        return ap[:, lo:n + hi]
    v = ap.rearrange("p (r c) -> p r c", c=64)
    return v[:, :, lo:64 + hi]


def stage_a(g, l, s):
    nc = g.nc
    with Stage(nc) as S:
        cw = S.sb([128, 4, 3])
        for k in range(3):
            S.dma("sp", cw[:, :, k], g.conv_a[l, k, :].rearrange("(c p) -> p c", p=128), w=[cw], allow_slow_non_contiguous=True)
        NB = 3
        tin = [[S.sb([128, 512]) for _ in range(4)] for _ in range(NB)]
        tu = [S.sb([128, 512]) for _ in range(NB)]
        ty = [S.sb([128, 512]) for _ in range(NB)]
        tsz = [S.sb([128, 512]) for _ in range(NB)]
        tout = [S.sb([128, 512], BF16) for _ in range(NB)]
        it = 0
        for (t0, n) in TILES:
            for j in range(4):
                ab, ac, ax, az = tin[it % NB]
                u, y, sz, yo = tu[it % NB], ty[it % NB], tsz[it % NB], tout[it % NB]
                for tl, ci in ((ab, CI_AB), (ac, CI_AC), (ax, CI_AX), (az, CI_AZ)):
                    S.dma("sp", tl[:, 0:n], g.P[(ci + j) * 128:(ci + j + 1) * 128, t0:t0 + n], w=[tl])
                S.tt("pool", u[:, 0:n], ac[:, 0:n], ax[:, 0:n], ALU.mult, r=[ac, ax], w=[u])
                S.act(y[:, 0:n], u[:, 0:n], AF.Copy, scale=cw[:, j, 1:2], r=[u, cw], w=[y])
                S.stt("dve", rows_view(y[:, 0:n], n, 1, 0), rows_view(u[:, 0:n], n, 0, -1), cw[:, j, 0:1], rows_view(y[:, 0:n], n, 1, 0),
                      ALU.mult, ALU.add, r=[u, y, cw], w=[y])
                S.stt("dve", rows_view(y[:, 0:n], n, 0, -1), rows_view(u[:, 0:n], n, 1, 0), cw[:, j, 2:3], rows_view(y[:, 0:n], n, 0, -1),
                      ALU.mult, ALU.add, r=[u, y, cw], w=[y])
                S.act(sz[:, 0:n], az[:, 0:n], AF.Silu, r=[az], w=[sz])
                S.tt("pool", u[:, 0:n], ab[:, 0:n], y[:, 0:n], ALU.mult, r=[ab, y], w=[u])
                S.tt("dve", yo[:, 0:n], u[:, 0:n], sz[:, 0:n], ALU.mult, r=[u, sz], w=[yo])
                S.dma("act", g.YA[j * 128:(j + 1) * 128, t0:t0 + n], yo[:, 0:n], r=[yo])
                it += 1


def stage_merge(g, l, s, last):
    nc = g.nc
    with Stage(nc) as S:
        wp = [S.sb([128, 4, D], BF16) for _ in range(3)]
        wo = S.sb([128, 8, D], BF16)
        wst = [S.sb([128, 4, D])] * 2
        bmc = S.sb([128, 24])
        S.dma("sp", bmc[:, :], g.b_merge[l, :].rearrange("(c p) -> p c", p=128), w=[bmc], allow_slow_non_contiguous=True)
        srcs = [g.w_pa, g.w_pb, g.w_pc]
        for br in range(3):
            S.dma("sp", wst[br % 2][:, :, :], srcs[br][l].rearrange("(k p) n -> p k n", p=128), w=[wst[br % 2]])
            S.copy("pool", wp[br][:, :, :], wst[br % 2][:, :, :], r=[wst[br % 2]], w=[wp[br]])
        for hf in range(2):
            S.dma("sp", wst[(hf + 1) % 2][:, :, :], g.w_o[l].rearrange("(k p) n -> p k n", p=128)[:, hf * 4:(hf + 1) * 4, :], w=[wst[(hf + 1) % 2]])
            S.copy("pool", wo[:, hf * 4:(hf + 1) * 4, :], wst[(hf + 1) % 2][:, :, :], r=[wst[(hf + 1) % 2]], w=[wo])
        if last:
            fnw = S.sb([128, D])
            S.dma("sp", fnw[:, :], g.fnw[0:1, :].partition_broadcast(128), w=[fnw])
            fst = S.sb([128, 8, 4])
        ysb = [[S.sb([128, 4, 512], BF16) for _ in range(3)]] * 2
        mgt = [S.sb([128, 512]) for _ in range(6)]
        gt = [S.sb([128, 512]) for _ in range(6)]
        tt_ = [S.sb([128, 512]) for _ in range(6)]
        mT = [S.sb([128, 8, 512], BF16) for _ in range(2)]
        xblk = [S.sb([128, D]) for _ in range(2)]
        xo = [S.sb([128, D]) for _ in range(2)]
        tmp = [S.sb([128, 512]) for _ in range(2)]
        junk = S.sb([128, D]) if last else None
        pb = [S.ps([128, 512]) for _ in range(6)]
        po = [S.ps([128, 512]) for _ in range(2)]
        ysrc = [g.YA, g.YB, g.YC]
        e3 = 0
        eo = 0
        for ti, (t0, n) in enumerate(TILES):
            if last and ti == 0:
                continue
            sidx = 2 if ti == 0 else s
            yt = ysb[ti % 2]
            m_t = mT[ti % 2]
            for br in range(3):
                S.dma("sp", yt[br][:, :, 0:n], ysrc[br][:, t0:t0 + n].rearrange("(j p) t -> p j t", p=128), w=[yt[br]])
            for oc in range(8):
                trip = []
                for br in range(3):
                    i6 = e3 % 6
                    e3 += 1
                    ci = CI_MG + br * 8 + oc
                    S.dma("sp", mgt[i6][:, 0:n], g.P[ci * 128:(ci + 1) * 128, t0:t0 + n], w=[mgt[i6]])
                    S.act(gt[i6][:, 0:n], mgt[i6][:, 0:n], AF.Sigmoid, bias=bmc[:, br * 8 + oc:br * 8 + oc + 1], r=[mgt[i6], bmc], w=[gt[i6]])
                    for j in range(4):
                        S.mm(pb[i6][:, 0:n], wp[br][:, j, oc * 128:(oc + 1) * 128], yt[br][:, j, 0:n], start=(j == 0), stop=(j == 3), r=[wp[br], yt[br]], w=[pb[i6]])
                    S.tt("dve", tt_[i6][:, 0:n], pb[i6][:, 0:n], gt[i6][:, 0:n], ALU.mult, r=[pb[i6], gt[i6]], w=[tt_[i6]])
                    trip.append(tt_[i6])
                S.tt("pool", trip[0][:, 0:n], trip[0][:, 0:n], trip[1][:, 0:n], ALU.add, r=[trip[0], trip[1]], w=[trip[0]])
                S.tt("pool", m_t[:, oc, 0:n], trip[0][:, 0:n], trip[2][:, 0:n], ALU.add, r=[trip[0], trip[2]], w=[m_t])
            for bi in range(n // 128):
                nblk = t0 // 128 + bi
                xb_ = xblk[eo % 2]
                xo_ = xo[eo % 2]
                eo += 1
                S.dma("sp", xb_[:, :], tok_src(g, l, s, nblk), w=[xb_])
                for cg in range(2):
                    p = po[cg]
                    for k in range(8):
                        S.mm(p[:, :], m_t[:, k, bi * 128:(bi + 1) * 128], wo[:, k, cg * 512:(cg + 1) * 512], start=(k == 0), stop=(k == 7), r=[m_t, wo], w=[p])
                    S.tt("dve", tmp[cg][:, :], p[:, :], g.Gb[:, sidx, cg * 512:(cg + 1) * 512], ALU.mult, r=[p, g.Gb], w=[tmp[cg]])
                    S.tt("pool", xo_[:, cg * 512:(cg + 1) * 512], tmp[cg][:, :], xb_[:, cg * 512:(cg + 1) * 512], ALU.add, r=[tmp[cg], xb_], w=[xo_])
                if not last:
                    dst = g.xrc[s, nblk * 128:(nblk + 1) * 128, :] if nblk < 2 else g.xrl[s, (nblk - 2) * 128:(nblk - 1) * 128, :]
                    S.dma("act", dst, xo_[:, :], r=[xo_])
                else:
                    q = eo % 8
                    S.act(junk[:, :], xo_[:, :], AF.Square, accum=fst[:, q, 0:1], r=[xo_], w=[junk, fst])
                    S.ts("dve", fst[:, q, 1:2], fst[:, q, 0:1], 1.0 / D, EPS, ALU.mult, ALU.add, r=[fst], w=[fst])
                    S.act(fst[:, q, 2:3], fst[:, q, 1:2], AF.Ln, r=[fst], w=[fst])
                    S.act(fst[:, q, 3:4], fst[:, q, 2:3], AF.Exp, scale=-0.5, r=[fst], w=[fst])
                    S.stt("dve", xb_[:, :], xo_[:, :], fst[:, q, 3:4], fnw[:, :], ALU.mult, ALU.mult, r=[xo_, fst, fnw], w=[xb_])
                    S.dma("act", g.out[s, (nblk - 2) * 128:(nblk - 1) * 128, :], xb_[:, :], r=[xb_])


def stage_bprep(g, l, s):
    nc = g.nc
    with Stage(nc) as S:
        cw = S.sb([128, 12, 3])
        for k in range(3):
            S.dma("sp", cw[:, :, k], g.conv_dn[l, k, :].rearrange("(c p) -> p c", p=128), w=[cw], allow_slow_non_contiguous=True)
        raw = [S.sb([128, 514]) for _ in range(3)]
        acc = [S.sb([128, 512]) for _ in range(3)]
        sil = [S.sb([128, 512]) for _ in range(8)]
        vout = [S.sb([128, 512], BF16) for _ in range(2)]
        sq = [S.sb([128, 512]) for _ in range(2)]
        lnt = [S.sb([128, 512]) for _ in range(2)]
        qo = [S.sb([128, 512], BF16) for _ in range(2)]
        pn = [S.ps([128, 512]) for _ in range(2)]
        onesb = g.cstb[:, C_ONES:C_ONES + 128]
        sqh = [S.sb([128, 512], BF16) for _ in range(2)]
        sql = [S.sb([128, 512], BF16) for _ in range(2)]
        it = 0
        for (t0, n) in TILES[int(os.environ.get('TLO', '0')):int(os.environ.get('THI', '9'))]:
            seq_lo, seq_hi = (0, NCTX) if t0 == 0 else (NCTX, T)
            for j in range(int(os.environ.get("JLO", "0")), 12):
                RS = int(os.environ.get("RING", "3"))
                rw = raw[it % RS]
                ac_ = acc[it % RS]
                it += 1
                zl = 1 if t0 - 1 < seq_lo else 0
                zr = 1 if t0 + n + 1 > seq_hi else 0
                if zl and not os.environ.get("NOMEMSET"):
                    S.memset("pool", rw[:, 0:1], 0.0, w=[rw])
                if zr and not os.environ.get("NOMEMSET"):
                    S.memset("pool", rw[:, n + 1:n + 2], 0.0, w=[rw])
                S.dma("sp", rw[:, zl:n + 2 - zr], g.P[(CI_Q + j) * 128:(CI_Q + j + 1) * 128, t0 - 1 + zl:t0 + n + 1 - zr], w=[rw])
                S.act(ac_[:, 0:n], rw[:, 1:n + 1], AF.Copy, scale=cw[:, j, 1:2], r=[rw, cw], w=[ac_])
                S.stt("dve", ac_[:, 0:n], rw[:, 0:n], cw[:, j, 0:1], ac_[:, 0:n], ALU.mult, ALU.add, r=[rw, ac_, cw], w=[ac_])
                S.stt("dve", ac_[:, 0:n], rw[:, 2:n + 2], cw[:, j, 2:3], ac_[:, 0:n], ALU.mult, ALU.add, r=[rw, ac_, cw], w=[ac_])
                if j < 8:
                    S.act(sil[j][:, 0:n], ac_[:, 0:n], AF.Silu, r=[ac_], w=[sil[j]])
                else:
                    vo = vout[j % (1 if os.environ.get('RING') == '1' else 2)]
                    S.act(vo[:, 0:n], ac_[:, 0:n], AF.Silu, r=[ac_], w=[vo])
                    S.dma("act", g.VT[(j - 8) * 128:(j - 7) * 128, t0:t0 + n], vo[:, 0:n], r=[vo])
            for j in range(0 if os.environ.get("NONORM") else 8):
                sq_ = sq[j % 2]
                ln_ = lnt[j % 2]
                q_ = qo[j % 2]
                p = pn[j % 2]
                NL = int(os.environ.get("NORMLVL", "9"))
                S.tt("pool", sq_[:, 0:n], sil[j][:, 0:n], sil[j][:, 0:n], ALU.mult, r=[sil[j]], w=[sq_])
                sh_, sl_ = sqh[j % 2], sql[j % 2]
                S.copy("pool", sh_[:, 0:n], sq_[:, 0:n], r=[sq_], w=[sh_])
                S.tt("pool", sl_[:, 0:n], sq_[:, 0:n], sh_[:, 0:n], ALU.subtract, r=[sq_, sh_], w=[sl_])
                if NL >= 2:
                    S.mm(p[:, 0:n], onesb, sh_[:, 0:n], start=True, stop=False, r=[g.cstb, sh_], w=[p])
                    S.mm(p[:, 0:n], onesb, sl_[:, 0:n], start=False, stop=True, r=[g.cstb, sl_], w=[p])
                if NL >= 3:
                    S.act(ln_[:, 0:n], p[:, 0:n], AF.Ln, bias=g.epsc[:, 0:1], r=[p, g.epsc], w=[ln_])
                if NL >= 4:
                    S.act(ln_[:, 0:n], ln_[:, 0:n], AF.Exp, scale=-0.5, r=[ln_], w=[ln_])
                if NL >= 5:
                    S.stt("dve", q_[:, 0:n], sil[j][:, 0:n], (128.0 ** -0.5) if j < 4 else 1.0, ln_[:, 0:n], ALU.mult, ALU.mult, r=[sil[j], ln_], w=[q_])
                dst = g.QT if j < 4 else g.KT
                S.dma("act", dst[(j % 4) * 128:(j % 4 + 1) * 128, t0:t0 + n], q_[:, 0:n], r=[q_])


class Sub:
    def __init__(self, tl):
        self.t = tl.t
        self.b = Buf()
        self.bank = tl.bank

    def __getitem__(self, idx):
        return self.t[idx]


ORD_F = list(range(NBLK))
ORD_B = [1, 0] + list(range(NBLK - 1, 1, -1))


def stage_bscan(g, l, s):
    nc = g.nc
    nsteps = int(os.environ.get("BSTEPS", str(NBLK)))
    with Stage(nc) as S:
        cst = g.cst
        CI = lambda o: cst[:, o:o + 128]
        ident, ones = CI(C_ID), CI(C_ONES)
        identb4 = S.sb([128, 4, 128], BF16)
        for q in range(4):
            S.copy("dve", identb4[:, q, :], ident, r=[cst], w=[identb4])
        gbr = S.sb([128, NBLK, 16])
        S.dma("sp", gbr[:, :, :], g.GB[:, :].rearrange("(n p) c -> p n c", p=128), w=[gbr])
        dtb = S.sb([128, 8])
        alog = S.sb([128, 8])
        S.dma("sp", dtb[:, :], g.dn_dt_bias[l:l + 1, :].partition_broadcast(128), w=[dtb])
        S.dma("sp", alog[:, :], g.dn_a_log[l:l + 1, :].partition_broadcast(128), w=[alog])
        nega = S.sb([128, 8])
        S.act(nega[:, :], alog[:, :], AF.Exp, r=[alog], w=[nega])
        S.ts("dve", nega[:, :], nega[:, :], -1.0, None, ALU.mult, r=[nega], w=[nega])
        t1 = S.sb([128, NBLK, 8])
        G = S.sb([128, NBLK, 8])
        NG = S.sb([128, NBLK, 8])
        BT = S.sb([128, NBLK, 8])
        NB = S.sb([128, NBLK, 8])
        for c in range(8):
            S.act(t1[:, :, c], gbr[:, :, c], AF.Exp, bias=dtb[:, c:c + 1], r=[gbr, dtb], w=[t1])
        S.act(t1[:, :, :], t1[:, :, :], AF.Ln, bias=1.0, r=[t1], w=[t1])
        for c in range(8):
            S.ts("dve", G[:, :, c], t1[:, :, c], nega[:, c:c + 1], None, ALU.mult, r=[t1, nega], w=[G])
        S.ts("dve", NG[:, :, :], G[:, :, :], -1.0, None, ALU.mult, r=[G], w=[NG])
        S.act(BT[:, :, :], gbr[:, :, 8:16], AF.Exp, scale=-1.0, r=[gbr], w=[BT])
        S.ts("dve", BT[:, :, :], BT[:, :, :], 1.0, None, ALU.add, r=[BT], w=[BT])
        S.P.op("dve", lambda e: e.reciprocal(out=BT[:, :, :], in_=BT[:, :, :]), [BT], [BT])
        S.ts("dve", NB[:, :, :], BT[:, :, :], -1.0, None, ALU.mult, r=[BT], w=[NB])
        banks = [S.ps([128, 512]) if bi_ not in (3, 4) else S.ps([128, 1024], BF16) for bi_ in range(8)]
        EX = S.sb([128, NBLK, 24])
        cstb = g.cstb
        CB = lambda o: cstb[:, o:o + 128]
        onesb = CB(C_ONES)
        for n in range(NBLK):
            pb = banks[n // 17]
            o = (n % 17) * 24
            S.mm(pb[:, o:o + 4], CI(C_TRIU), G[:, n, 0:4], r=[cst, G], w=[pb])
            S.mm(pb[:, o + 4:o + 8], CI(C_TRIL), G[:, n, 4:8], r=[cst, G], w=[pb])
            S.mm(pb[:, o + 8:o + 12], CI(C_TRILS), G[:, n, 0:4], r=[cst, G], w=[pb])
            S.mm(pb[:, o + 12:o + 16], CI(C_TRIUS), G[:, n, 4:8], r=[cst, G], w=[pb])
            S.mm(pb[:, o + 16:o + 24], ones, G[:, n, 0:8], r=[cst, G], w=[pb])
        GC = S.sb([128, NBLK, 8])
        GCh = S.sb([128, NBLK, 8], BF16)
        GChf = S.sb([128, NBLK, 8])
        GCl = S.sb([128, NBLK, 8])
        if not os.environ.get("NOGC"):
            for hb in range(2):
                for n_ in range(17):
                    S.copy("dve", GC[:, hb * 17 + n_, :], banks[hb][:, n_ * 24:n_ * 24 + 8], r=[banks[hb]], w=[GC])
            GL = int(os.environ.get("GCLVL", "9"))
            if GL >= 1:
                S.copy("dve", GCh[:, :, :], GC[:, :, :], r=[GC], w=[GCh])
            if GL >= 2:
                S.copy("dve", GChf[:, :, :], GCh[:, :, :], r=[GCh], w=[GChf])
            if GL >= 3:
                S.tt("dve", GCl[:, :, :], GC[:, :, :], GChf[:, :, :], ALU.subtract, r=[GC, GChf], w=[GCl])
        for hb in range(2):
            S.act(EX[:, hb * 17:(hb + 1) * 17, :].rearrange("p n c -> p (n c)"), banks[hb][:, 0:17 * 24], AF.Exp, r=[banks[hb], GC, GCl], w=[EX])
        KCO = S.sb([128, NBLK, 8])
        NEGC = S.sb([128, NBLK, 8])
        S.tt("dve", KCO[:, :, :], EX[:, :, 8:16], BT[:, :, :], ALU.mult, r=[EX, BT], w=[KCO])
        S.ts("dve", NEGC[:, :, :], EX[:, :, 0:8], -1.0, None, ALU.mult, r=[EX], w=[NEGC])
        onc = S.sb([128, 1])
        S.dma("sp", onc[:, :], g.dn_onorm[l, :].rearrange("(p o) -> p o", o=1), w=[onc], allow_slow_non_contiguous=True)

        if os.environ.get("BDBG"):
            for i_, tl_ in enumerate((G, BT, GC, GCl, KCO, NEGC)):
                S.dma("sp", g.dbg[0:128, i_ * 272:(i_ + 1) * 272], tl_[:, :, :].rearrange("p a b -> p (a b)"), r=[tl_])
            S.dma("sp", g.dbg[0:128, 6 * 272:6 * 272 + 816], EX[:, :, :].rearrange("p a b -> p (a b)"), r=[EX])
            return
        qT2 = S.sb([128, 2, T], BF16)
        kT2 = S.sb([128, 2, T], BF16)
        vT2 = S.sb([128, 2, T], BF16)
        otok = S.sb([128, NBLK, 2, 128])
        osub = [[Sub(otok) for _ in range(2)] for _ in range(NBLK)]
        Sf = [S.sb([128, 128]) for _ in range(4)]
        Sb = [S.sb([128, 128], BF16) for _ in range(4)]

        def t4(dt=BF16):
            t = S.sb([128, 4, 128], dt)
            return t, [Sub(t) for _ in range(4)]
        G1, G1s = t4()
        nG1, nG1s = t4()
        ETa, ETas = t4(F32)
        ET, ETs_ = t4(F32)
        ETm, ETms = t4(F32)
        NT, NTs = t4()
        AqT, AqTs = t4()
        kgb, kgbs = t4()
        vtk, vtks = t4()
        rr, rrs = t4()
        yb, ybs = t4()
        Pp = [t4(), t4()]
        PQ = [t4(), t4()]
        PTp = [t4(), t4()]
        XTp = [t4(), t4()]
        XXp = [t4(), t4()]
        Dd, Dds = t4()
        DdT, DdTs = t4()
        C16, C16s = t4()
        C16T, C16Ts = t4()
        C32, C32s = t4()
        C32T, C32Ts = t4()
        C64, C64s = t4()
        M1, M1s = t4()
        M1T, M1Ts = t4()
        GB_ = [[banks[0], banks[1], banks[3], banks[5]], [banks[2], banks[6], banks[4], banks[7]]]
        ROLE = {"KK": (0, 0), "QK": (0, 256), "DT": (1, 0), "kt": (2, 0), "vt": (2, 256), "N0": (2, 512),
                "PP": (3, 0), "PT": (3, 256), "XX": (0, 0), "DTp": (0, 256),
                "ps1": (1, 0), "ps2": (1, 256), "ps3": (3, 0), "ps4": (3, 256), "ps5": (0, 0)}
        rsub = {}

        def R(role, d, hh):
            bi_, off = ROLE[role]
            tl_ = GB_[d][bi_]
            key = (d, bi_, off + hh * 128)
            if key not in rsub:
                rsub[key] = Sub(tl_)
            return tl_[:, off + hh * 128:off + (hh + 1) * 128], rsub[key]

        def RG(role, d):
            bi_, off = ROLE[role]
            tl_ = GB_[d][bi_]
            return tl_[:, off:off + 256], [R(role, d, 0)[1], R(role, d, 1)[1]]

        def g2(t_, d):
            return t_[:, 2 * d:2 * d + 2, :].rearrange("p a b -> p (a b)")
        sl = lambda q: slice(q * 128, (q + 1) * 128)
        zsrc = g.P
        MKg = lambda o: cstb[:, o:o + 256]
        identb2 = identb4[:, 0:2, :].rearrange("p a b -> p (a b)")

        for hp in range(2):
            for hh in range(2):
                h = 2 * hp + hh
                S.dma("sp", qT2[:, hh, :], g.QT[h * 128:(h + 1) * 128, :], w=[qT2])
                S.dma("sp", kT2[:, hh, :], g.KT[h * 128:(h + 1) * 128, :], w=[kT2])
                S.dma("sp", vT2[:, hh, :], g.VT[h * 128:(h + 1) * 128, :], w=[vT2])
            for q in range(4):
                S.memset("pool", Sf[q][:, :], 0.0, w=[Sf[q]])
                S.memset("pool", Sb[q][:, :], 0.0, w=[Sb[q]])
            visited = set()
            if nsteps < NBLK:
                S.memset("pool", otok[:, :, :, :], 0.0, w=[osub[n_][h_] for n_ in range(NBLK) for h_ in range(2)])
            for i in range(nsteps):
                grp = []
                for d in range(2):
                    n = (ORD_F if d == 0 else ORD_B)[i]
                    grp.append((d, n, [(d * 2 + hh, hh, d * 4 + 2 * hp + hh) for hh in range(2)]))
                for (d, n, chs) in grp:
                    for (q, hh, c) in chs:
                        kTb = kT2[:, hh, n * 128:(n + 1) * 128]
                        qTb = qT2[:, hh, n * 128:(n + 1) * 128]
                        vTb = vT2[:, hh, n * 128:(n + 1) * 128]
                        a_, s_ = R("kt", d, hh)
                        S.tr(a_, kTb, identb4[:, 0, :], r=[kT2, identb4], w=[s_])
                        a_, s_ = R("vt", d, hh)
                        S.tr(a_, vTb, identb4[:, 0, :], r=[vT2, identb4], w=[s_])
                        a_, s_ = R("KK", d, hh)
                        S.mm(a_, kTb, kTb, r=[kT2], w=[s_])
                        a_, s_ = R("QK", d, hh)
                        S.mm(a_, kTb, qTb, r=[kT2, qT2], w=[s_])
                        S.act(G1[:, q, :], identb4[:, 0, :], AF.Copy, scale=GChf[:, n, c:c + 1], r=[identb4, GChf], w=[G1s[q]])
                        S.act(nG1[:, q, :], identb4[:, 0, :], AF.Copy, scale=GCl[:, n, c:c + 1], r=[identb4, GCl], w=[nG1s[q]])
                        a_, s_ = R("DT", d, hh)
                        S.mm(a_, onesb, G1[:, q, :], start=True, stop=False, r=[cstb, G1s[q]], w=[s_])
                        S.mm(a_, onesb, nG1[:, q, :], start=False, stop=True, r=[cstb, nG1s[q]], w=[s_])
                for (d, n, chs) in grp:
                    neg = CI(C_NEGF) if d == 0 else CI(C_NEGB)
                    for (q, hh, c) in chs:
                        a_, s_ = R("DT", d, hh)
                        S.stt("dve", ETa[:, q, :], a_, GC[:, n, c:c + 1], neg, ALU.subtract, ALU.add, r=[s_, cst, GC], w=[ETas[q]])
                        S.act(ET[:, q, :], ETa[:, q, :], AF.Exp, r=[ETas[q]], w=[ETs_[q]])
                        S.tt("pool", ETm[:, q, :], ET[:, q, :], ident, ALU.subtract, r=[ETs_[q], cst], w=[ETms[q]])
                        a_, s_ = R("KK", d, hh)
                        S.stt("dve", NT[:, q, :], a_, NB[:, n, c:c + 1], ETm[:, q, :], ALU.mult, ALU.mult, r=[s_, NB, ETms[q]], w=[NTs[q]])
                        a_, s_ = R("QK", d, hh)
                        S.stt("dve", AqT[:, q, :], a_, BT[:, n, c:c + 1], ET[:, q, :], ALU.mult, ALU.mult, r=[s_, BT, ETs_[q]], w=[AqTs[q]])
                        a_, s_ = R("kt", d, hh)
                        S.ts("dve", kgb[:, q, :], a_, KCO[:, n, c:c + 1], None, ALU.mult, r=[s_, KCO], w=[kgbs[q]])
                    a_, ss_ = RG("vt", d)
                    S.copy("dve", g2(vtk, d), a_, r=ss_, w=vtks[2 * d:2 * d + 2])
                    for (q, hh, c) in chs:
                        a_, s_ = R("N0", d, hh)
                        S.tr(a_, NT[:, q, :], identb4[:, 0, :], r=[NTs[q], identb4], w=[s_])
                Nn, Nns = Pp[0]
                for (d, n, chs) in grp:
                    a_, ss_ = RG("N0", d)
                    S.copy("act", g2(Nn, d), a_, r=ss_, w=Nns[2 * d:2 * d + 2])
                for (d, n, chs) in grp:
                    qs = slice(2 * d, 2 * d + 2)
                    for (dst, dss, src, sss, mk) in ((Dd, Dds, Nn, Nns, C_MD16), (DdT, DdTs, NT, NTs, C_MD16), (C16, C16s, Nn, Nns, C_MS16),
                                                     (C16T, C16Ts, NT, NTs, C_MS16), (C32, C32s, Nn, Nns, C_MS32), (C32T, C32Ts, NT, NTs, C_MS32),
                                                     (C64, C64s, Nn, Nns, C_MS64)):
                        S.tt("pool", g2(dst, d), g2(src, d), MKg(mk), ALU.mult, r=sss[qs] + [cstb], w=dss[qs])
                    S.tt("dve", g2(XXp[0][0], d), g2(Dd, d), identb2, ALU.add, r=Dds[qs] + [identb4], w=XXp[0][1][qs])
                    S.tt("dve", g2(XTp[0][0], d), g2(DdT, d), identb2, ALU.add, r=DdTs[qs] + [identb4], w=XTp[0][1][qs])
                Xc, Xcs = XXp[0]
                XT, XTs = XTp[0]
                Pc, Pcs, PT, PTs = Dd, Dds, DdT, DdTs
                flip = 0
                for r_ in range(1, 4):
                    Pn, Pns = PQ[r_ % 2]
                    PTn, PTns = PTp[r_ % 2]
                    flip ^= 1
                    Xn, Xns = XXp[flip]
                    XTn, XTns = XTp[flip]
                    for (d, n, chs) in grp:
                        qs = slice(2 * d, 2 * d + 2)
                        for (q, hh, c) in chs:
                            a_, s_ = R("PP", d, hh)
                            S.mm(a_, PT[:, q, :], Pc[:, q, :], r=[PTs[q], Pcs[q]], w=[s_])
                            a_, s_ = R("PT", d, hh)
                            S.mm(a_, Pc[:, q, :], PT[:, q, :], r=[PTs[q], Pcs[q]], w=[s_])
                        a_, ss_ = RG("PP", d)
                        S.copy("act", g2(Pn, d), a_, r=ss_, w=Pns[qs])
                        a_, ss_ = RG("PT", d)
                        S.copy("act", g2(PTn, d), a_, r=ss_, w=PTns[qs])
                    for (d, n, chs) in grp:
                        qs = slice(2 * d, 2 * d + 2)
                        for (q, hh, c) in chs:
                            a_, s_ = R("XX", d, hh)
                            S.mm(a_, PTn[:, q, :], Xc[:, q, :], r=[PTns[q], Xcs[q]], w=[s_])
                            a_, s_ = R("DTp", d, hh)
                            S.mm(a_, Pn[:, q, :], XT[:, q, :], r=[Pns[q], XTs[q]], w=[s_])
                        a_, ss_ = RG("XX", d)
                        S.tt("dve", g2(Xn, d), a_, g2(Xc, d), ALU.add, r=ss_ + Xcs[qs], w=Xns[qs])
                        a_, ss_ = RG("DTp", d)
                        S.tt("dve", g2(XTn, d), a_, g2(XT, d), ALU.add, r=ss_ + XTs[qs], w=XTns[qs])
                    Pc, Pcs, PT, PTs = Pn, Pns, PTn, PTns
                    Xc, Xcs, XT, XTs = Xn, Xns, XTn, XTns
                for (Cb, Cbs, CbT, CbTs, lastlvl) in ((C16, C16s, C16T, C16Ts, False), (C32, C32s, C32T, C32Ts, False), (C64, C64s, None, None, True)):
                    flip ^= 1
                    Xn, Xns = XXp[flip]
                    XTn, XTns = XTp[flip]
                    for (d, n, chs) in grp:
                        qs = slice(2 * d, 2 * d + 2)
                        for (q, hh, c) in chs:
                            if not lastlvl:
                                a_, s_ = R("PP", d, hh)
                                S.mm(a_, CbT[:, q, :], Xc[:, q, :], r=[CbTs[q], Xcs[q]], w=[s_])
                            a_, s_ = R("PT", d, hh)
                            S.mm(a_, Cb[:, q, :], XT[:, q, :], r=[Cbs[q], XTs[q]], w=[s_])
                        if not lastlvl:
                            a_, ss_ = RG("PP", d)
                            S.copy("act", g2(M1, d), a_, r=ss_, w=M1s[qs])
                        a_, ss_ = RG("PT", d)
                        S.copy("act", g2(M1T, d), a_, r=ss_, w=M1Ts[qs])
                    for (d, n, chs) in grp:
                        qs = slice(2 * d, 2 * d + 2)
                        for (q, hh, c) in chs:
                            if not lastlvl:
                                a_, s_ = R("XX", d, hh)
                                S.mm(a_, XT[:, q, :], M1[:, q, :], r=[XTs[q], M1s[q]], w=[s_])
                            a_, s_ = R("DTp", d, hh)
                            S.mm(a_, Xc[:, q, :], M1T[:, q, :], r=[Xcs[q], M1Ts[q]], w=[s_])
                        if not lastlvl:
                            a_, ss_ = RG("XX", d)
                            S.tt("dve", g2(Xn, d), a_, g2(Xc, d), ALU.add, r=ss_ + Xcs[qs], w=Xns[qs])
                        a_, ss_ = RG("DTp", d)
                        S.tt("dve", g2(XTn, d), a_, g2(XT, d), ALU.add, r=ss_ + XTs[qs], w=XTns[qs])
                    Xc, Xcs, XT, XTs = Xn, Xns, XTn, XTns
                for (d, n, chs) in grp:
                    for (q, hh, c) in chs:
                        kTb = kT2[:, hh, n * 128:(n + 1) * 128]
                        a_, s_ = R("ps1", d, hh)
                        S.mm(a_, kTb, Sb[q][:, :], r=[kT2, Sb[q]], w=[s_])
                        S.stt("dve", rr[:, q, :], a_, NEGC[:, n, c:c + 1], vtk[:, q, :], ALU.mult, ALU.add, r=[s_, NEGC, vtks[q]], w=[rrs[q]])
                for (d, n, chs) in grp:
                    qs = slice(2 * d, 2 * d + 2)
                    for (q, hh, c) in chs:
                        a_, s_ = R("ps2", d, hh)
                        S.mm(a_, XT[:, q, :], rr[:, q, :], r=[XTs[q], rrs[q]], w=[s_])
                    a_, ss_ = RG("ps2", d)
                    S.copy("act", g2(yb, d), a_, r=ss_, w=ybs[qs])
                for (d, n, chs) in grp:
                    for (q, hh, c) in chs:
                        qTb = qT2[:, hh, n * 128:(n + 1) * 128]
                        a3, s3 = R("ps3", d, hh)
                        S.mm(a3, qTb, Sb[q][:, :], r=[qT2, Sb[q]], w=[s3])
                        a4, s4 = R("ps4", d, hh)
                        S.mm(a4, AqT[:, q, :], yb[:, q, :], r=[AqTs[q], ybs[q]], w=[s4])
                        a5, s5 = R("ps5", d, hh)
                        S.mm(a5, kgb[:, q, :], yb[:, q, :], r=[kgbs[q], ybs[q]], w=[s5])
                for (d, n, chs) in grp:
                    for (q, hh, c) in chs:
                        od = otok[:, n, hh, :]
                        ob = osub[n][hh]
                        a3, s3 = R("ps3", d, hh)
                        a4, s4 = R("ps4", d, hh)
                        a5, s5 = R("ps5", d, hh)
                        if (n, hh) not in visited:
                            visited.add((n, hh))
                            S.ts("dve", od, a3, EX[:, n, c:c + 1], None, ALU.mult, r=[s3, EX], w=[ob])
                        else:
                            S.stt("dve", od, a3, EX[:, n, c:c + 1], od, ALU.mult, ALU.add, r=[s3, EX, ob], w=[ob])
                        S.tt("dve", od, a4, od, ALU.add, r=[s4, ob], w=[ob])
                        S.stt("dve", Sf[q][:, :], Sf[q][:, :], EX[:, n, 16 + c:17 + c], a5, ALU.mult, ALU.add, r=[Sf[q], EX, s5], w=[Sf[q]])
                        S.copy("pool", Sb[q][:, :], Sf[q][:, :], r=[Sf[q]], w=[Sb[q]])
            st_ = S.sb([128, NBLK, 2, 4])
            junk = S.sb([128, 128])
            for n in range(NBLK):
                for hh in range(2):
                    S.act(junk[:, :], otok[:, n, hh, :], AF.Square, accum=st_[:, n, hh, 0:1], r=[osub[n][hh]], w=[junk, st_])
            S.ts("dve", st_[:, :, :, 1], st_[:, :, :, 0], 1.0 / 128, EPS, ALU.mult, ALU.add, r=[st_], w=[st_])
            S.act(st_[:, :, :, 2], st_[:, :, :, 1], AF.Ln, r=[st_], w=[st_])
            S.act(st_[:, :, :, 3], st_[:, :, :, 2], AF.Exp, scale=-0.5, r=[st_], w=[st_])
            zt = [S.sb([128, 512]) for _ in range(2)]
            yt = [S.sb([128, 512]) for _ in range(2)]
            yo = [S.sb([128, 512], BF16) for _ in range(2)]
            ev = 0
            for hh in range(2):
                h = 2 * hp + hh
                for (t0, n_) in TILES:
                    nb_ = n_ // 128
                    bk = banks[(ev % 2) * 2]
                    for bi in range(nb_):
                        n = t0 // 128 + bi
                        S.act(otok[:, n, hh, :], otok[:, n, hh, :], AF.Copy, scale=st_[:, n, hh, 3:4], r=[osub[n][hh], st_], w=[osub[n][hh]])
                        S.tr(bk[:, bi * 128:(bi + 1) * 128], otok[:, n, hh, :], ident, r=[osub[n][hh], cst], w=[bk])
                    z_, y_, o_ = zt[ev % 2], yt[ev % 2], yo[ev % 2]
                    ev += 1
                    S.dma("sp", z_[:, 0:n_], zsrc[(CI_DZ + h) * 128:(CI_DZ + h + 1) * 128, t0:t0 + n_], w=[z_])
                    S.act(y_[:, 0:n_], bk[:, 0:n_], AF.Copy, scale=onc[:, 0:1], r=[bk, onc], w=[y_])
                    S.act(z_[:, 0:n_], z_[:, 0:n_], AF.Silu, r=[z_], w=[z_])
                    S.tt("dve", o_[:, 0:n_], y_[:, 0:n_], z_[:, 0:n_], ALU.mult, r=[y_, z_], w=[o_])
                    S.dma("act", g.YB[h * 128:(h + 1) * 128, t0:t0 + n_], o_[:, 0:n_], r=[o_])


def cm_view(ap):
    return ap.rearrange("p (r c) -> p c r", c=64)


def stage_c(g, l, s):
    nc = g.nc
    nsteps = int(os.environ.get("CSTEPS", str(NBLK)))
    with Stage(nc) as S:
        cst, cstb = g.cst, g.cstb
        CI = lambda o: cst[:, o:o + 128]
        CB = lambda o: cstb[:, o:o + 128]
        ident, identb = CI(C_ID), CB(C_ID)
        stag = S.sb([128, T])
        qcm = S.sb([128, T])
        kcm = S.sb([128, T])
        vcm = S.sb([128, 2, T], BF16)
        rcm = S.sb([64, T], BF16)
        otok = S.sb([128, NBLK, 2, 128])
        osub = [[Sub(otok) for _ in range(2)] for _ in range(NBLK)]
        ycm = S.sb([128, T])
        wr32 = S.sb([64, 256])
        wrb = S.sb([64, 256], BF16)
        onc = S.sb([128, 1])
        S.dma("sp", onc[:, :], g.gla_onorm[l, :].rearrange("(p o) -> p o", o=1), w=[onc], allow_slow_non_contiguous=True)
        S.memset("pool", wr32[:, :], 0.0, w=[wr32])
        for d in range(2):
            S.dma("sp", wr32[32 * d:32 * d + 16, :], g.gla_w_r2[l, d, :, :], w=[wr32])
            S.dma("sp", wr32[32 * d + 16:32 * d + 17, :], g.gla_b_r[l, d:d + 1, :], w=[wr32])
        S.copy("dve", wrb[:, :], wr32[:, :], r=[wr32], w=[wrb])
        S.dma("sp", stag[0:48, :], g.P[CI_GR * 128:CI_GR * 128 + 48, :], w=[stag])
        S.memset("pool", rcm[:, :], 1.0, w=[rcm])
        for d in range(2):
            S.copy("dve", rcm[32 * d:32 * d + 16, 0:NCTX], stag[32 * d:32 * d + 16, 0:NCTX], r=[stag], w=[rcm])
            S.copy("dve", rcm[32 * d:32 * d + 16, NCTX:T].rearrange("p (c r) -> p c r", r=64), cm_view(stag[32 * d:32 * d + 16, NCTX:T]), r=[stag], w=[rcm])
        Sf = [S.sb([128, 128]) for _ in range(2)]
        Sb = [S.sb([128, 128], BF16) for _ in range(2)]
        bL = [S.ps([128, 512]) for _ in range(2)]
        bV = [S.ps([128, 1024], BF16) for _ in range(2)]
        bA = [S.ps([128, 512]) for _ in range(2)]
        bK = S.ps([128, 512])
        bKs = [Sub(bK) for _ in range(2)]

        def pd(shape, dt=F32):
            return [S.sb(shape, dt) for _ in range(2)]
        et, lt = pd([128, 128]), pd([128, 128])
        lh, lhf, ll = pd([128, 128], BF16), pd([128, 128]), pd([128, 128], BF16)
        eq, ek, ekg = pd([128, 128]), pd([128, 128]), pd([128, 128])
        qg, qgpad, kinvpad, kgpad = pd([128, 128], BF16), pd([128, 2, 128], BF16), pd([128, 2, 128], BF16), pd([128, 2, 128], BF16)
        vtok = pd([128, 2, 128], BF16)
        atm = pd([128, 2, 128], BF16)
        for d in range(2):
            S.memset("pool", qgpad[d][:, :, :], 0.0, w=[qgpad[d]])
            S.memset("pool", kinvpad[d][:, :, :], 0.0, w=[kinvpad[d]])
            S.memset("pool", kgpad[d][:, :, :], 0.0, w=[kgpad[d]])
        ISC = 1.0 / 16.0

        for hp in range(2):
            for (dst, ci) in ((qcm, CI_GQ + hp), (kcm, CI_GK + hp)):
                S.dma("sp", stag[:, :], g.P[ci * 128:(ci + 1) * 128, :], w=[stag])
                S.copy("dve", dst[:, 0:NCTX], stag[:, 0:NCTX], r=[stag], w=[dst])
                S.copy("dve", dst[:, NCTX:T].rearrange("p (c r) -> p c r", r=64), cm_view(stag[:, NCTX:T]), r=[stag], w=[dst])
            for hh in range(2):
                ci = CI_GV + 2 * hp + hh
                S.dma("sp", stag[:, :], g.P[ci * 128:(ci + 1) * 128, :], w=[stag])
                S.copy("pool", vcm[:, hh, 0:NCTX], stag[:, 0:NCTX], r=[stag], w=[vcm])
                S.copy("pool", vcm[:, hh, NCTX:T].rearrange("p (c r) -> p c r", r=64), cm_view(stag[:, NCTX:T]), r=[stag], w=[vcm])
            for d in range(2):
                S.memset("pool", Sf[d][:, :], 0.0, w=[Sf[d]])
                S.memset("pool", Sb[d][:, :], 0.0, w=[Sb[d]])
            if nsteps < NBLK:
                S.memset("pool", otok[:, :, :, :], 0.0, w=[osub[n_][h_] for n_ in range(NBLK) for h_ in range(2)])
            visited = set()
            for i in range(nsteps):
                blks = [(0, ORD_F[i]), (1, ORD_B[i])]
                for (d, n) in blks:
                    bs = slice(n * 128, (n + 1) * 128)
                    S.mm(bL[d][:, 0:128], rcm[32 * d:32 * d + 17, bs], wrb[32 * d:32 * d + 17, hp * 128:(hp + 1) * 128], r=[rcm, wrb], w=[bL[d]])
                    S.tr(bL[d][:, 128:256], kcm[:, bs], ident, r=[kcm, cst], w=[bL[d]])
                    for hh in range(2):
                        S.tr(bV[d][:, hh * 128:(hh + 1) * 128], vcm[:, hh, bs], identb, r=[vcm, cstb], w=[bV[d]])
                for (d, n) in blks:
                    S.act(et[d][:, :], bL[d][:, 0:128], AF.Exp, scale=-1.0, r=[bL[d]], w=[et[d]])
                    S.act(lt[d][:, :], et[d][:, :], AF.Ln, bias=1.0, r=[et[d]], w=[lt[d]])
                    S.copy("dve", lh[d][:, :], lt[d][:, :], r=[lt[d]], w=[lh[d]])
                    S.copy("dve", lhf[d][:, :], lh[d][:, :], r=[lh[d]], w=[lhf[d]])
                    S.tt("dve", ll[d][:, :], lt[d][:, :], lhf[d][:, :], ALU.subtract, r=[lt[d], lhf[d]], w=[ll[d]])
                    S.copy("act", vtok[d][:, :, :].rearrange("p a b -> p (a b)"), bV[d][:, 0:256], r=[bV[d]], w=[vtok[d]])
                for (d, n) in blks:
                    tri = CB(C_TRIU) if d == 0 else CB(C_TRIL)
                    tris = CB(C_TRILS) if d == 0 else CB(C_TRIUS)
                    S.mm(bL[d][:, 384:512], lh[d][:, :], tri, start=True, stop=False, r=[lh[d], cstb], w=[bL[d]])
                    S.mm(bL[d][:, 384:512], ll[d][:, :], tri, start=False, stop=True, r=[ll[d], cstb], w=[bL[d]])
                    S.mm(bL[d][:, 256:384], tris, lh[d][:, :], start=True, stop=False, r=[lh[d], cstb], w=[bL[d]])
                    S.mm(bL[d][:, 256:384], tris, ll[d][:, :], start=False, stop=True, r=[ll[d], cstb], w=[bL[d]])
                for (d, n) in blks:
                    bs = slice(n * 128, (n + 1) * 128)
                    S.act(eq[d][:, :], bL[d][:, 384:512], AF.Exp, scale=-ISC, r=[bL[d]], w=[eq[d]])
                    S.act(ek[d][:, :], bL[d][:, 384:512], AF.Exp, scale=ISC, r=[bL[d]], w=[ek[d]])
                    S.act(ekg[d][:, :], bL[d][:, 256:384], AF.Exp, scale=-ISC, r=[bL[d]], w=[ekg[d]])
                    S.stt("dve", qg[d][:, :], qcm[:, bs], 0.125, eq[d][:, :], ALU.mult, ALU.mult, r=[qcm, eq[d]], w=[qg[d]])
                    for hh in range(2):
                        ps_ = slice(64 * hh, 64 * hh + 64)
                        S.copy("pool", qgpad[d][ps_, hh, :], qg[d][ps_, :], r=[qg[d]], w=[qgpad[d]])
                        S.tt("pool", kinvpad[d][ps_, hh, :], kcm[ps_, bs], ek[d][ps_, :], ALU.mult, r=[kcm, ek[d]], w=[kinvpad[d]])
                        S.tt("dve", kgpad[d][:, hh, 64 * hh:64 * hh + 64], bL[d][:, 128 + 64 * hh:128 + 64 * hh + 64], ekg[d][:, 64 * hh:64 * hh + 64], ALU.mult,
                             r=[bL[d], ekg[d]], w=[kgpad[d]])
                for (d, n) in blks:
                    for hh in range(2):
                        S.mm(bA[d][:, hh * 128:(hh + 1) * 128], kinvpad[d][:, hh, :], qg[d][:, :], r=[kinvpad[d], qg[d]], w=[bA[d]])
                    msk = CI(C_TRIU) if d == 0 else CI(C_TRIL)
                    for hh in range(2):
                        S.tt("dve", atm[d][:, hh, :], bA[d][:, hh * 128:(hh + 1) * 128], msk, ALU.mult, r=[bA[d], cst], w=[atm[d]])
                    for hh in range(2):
                        S.mm(bA[d][:, 256 + hh * 128:256 + (hh + 1) * 128], qgpad[d][:, hh, :], Sb[d][:, :], start=True, stop=False, r=[qgpad[d], Sb[d]], w=[bA[d]])
                        S.mm(bA[d][:, 256 + hh * 128:256 + (hh + 1) * 128], atm[d][:, hh, :], vtok[d][:, hh, :], start=False, stop=True, r=[atm[d], vtok[d]], w=[bA[d]])
                    for hh in range(2):
                        S.mm(bK[:, d * 128:(d + 1) * 128], kgpad[d][:, hh, :], vtok[d][:, hh, :], start=(hh == 0), stop=(hh == 1), r=[kgpad[d], vtok[d]], w=[bKs[d]])
                    for hh in range(2):
                        od = otok[:, n, hh, :]
                        ob = osub[n][hh]
                        src = bA[d][:, 256 + hh * 128:256 + (hh + 1) * 128]
                        if (n, hh) not in visited:
                            visited.add((n, hh))
                            S.copy("act", od, src, r=[bA[d]], w=[ob])
                        else:
                            S.tt("dve", od, src, od, ALU.add, r=[bA[d], ob], w=[ob])
                    gl = eq[d][:, 127:128] if d == 0 else eq[d][:, 0:1]
                    S.stt("dve", Sf[d][:, :], Sf[d][:, :], gl, bK[:, d * 128:(d + 1) * 128], ALU.mult, ALU.add, r=[Sf[d], eq[d], bKs[d]], w=[Sf[d]])
                    S.copy("pool", Sb[d][:, :], Sf[d][:, :], r=[Sf[d]], w=[Sb[d]])
            st_ = S.sb([128, NBLK, 2, 4])
            junk = S.sb([128, 128])
            for n in range(NBLK):
                for hh in range(2):
                    S.act(junk[:, :], otok[:, n, hh, :], AF.Square, accum=st_[:, n, hh, 0:1], r=[osub[n][hh]], w=[junk, st_])
            S.ts("dve", st_[:, :, :, 1], st_[:, :, :, 0], 1.0 / 128, EPS, ALU.mult, ALU.add, r=[st_], w=[st_])
            S.act(st_[:, :, :, 2], st_[:, :, :, 1], AF.Ln, r=[st_], w=[st_])
            S.act(st_[:, :, :, 3], st_[:, :, :, 2], AF.Exp, scale=-0.5, r=[st_], w=[st_])
            yo = S.sb([128, T], BF16)
            for hh in range(2):
                h = 2 * hp + hh
                for g4 in range(0, NBLK, 4):
                    nb_ = min(4, NBLK - g4)
                    bk = bL[(g4 // 4) % 2]
                    for bi in range(nb_):
                        n = g4 + bi
                        S.act(otok[:, n, hh, :], otok[:, n, hh, :], AF.Copy, scale=st_[:, n, hh, 3:4], r=[osub[n][hh], st_], w=[osub[n][hh]])
                        S.tr(bk[:, bi * 128:(bi + 1) * 128], otok[:, n, hh, :], ident, r=[osub[n][hh], cst], w=[bk])
                    S.act(ycm[:, g4 * 128:(g4 + nb_) * 128], bk[:, 0:nb_ * 128], AF.Copy, scale=onc[:, 0:1], r=[bk, onc], w=[ycm])
                S.dma("sp", stag[:, :], g.P[(CI_GZ + h) * 128:(CI_GZ + h + 1) * 128, :], w=[stag])
                S.act(stag[:, :], stag[:, :], AF.Silu, r=[stag], w=[stag])
                S.tt("dve", yo[:, 0:NCTX], ycm[:, 0:NCTX], stag[:, 0:NCTX], ALU.mult, r=[ycm, stag], w=[yo])
                S.tt("dve", yo[:, NCTX:T].rearrange("p (r c) -> p r c", c=64), ycm[:, NCTX:T].rearrange("p (c r) -> p r c", r=64),
                     stag[:, NCTX:T].rearrange("p (r c) -> p r c", c=64), ALU.mult, r=[ycm, stag], w=[yo])
                S.dma("act", g.YC[h * 128:(h + 1) * 128, :], yo[:, :], r=[yo])


def build_nc(nlayer=4, nseq=2, debug=None, dbg_what=None):
    nc = bass.Bass("TRN2", target_bir_lowering=False)
    g = Ctx()
    g.nc = nc
    dt = lambda name, shape, kind="ExternalInput", dtype=F32: nc.dram_tensor(name, list(shape), dtype, kind=kind).ap()
    g.x_in = dt("x", [2, NLAT, D])
    g.ctx_in = dt("ctx", [2, NCTX, D])
    g.c_in = dt("c", [2, D])
    g.cctx_in = dt("c_ctx", [1, D])
    g.w_mod = dt("w_mod", [4, D, 3 * D])
    g.b_mod = dt("b_mod", [4, 3 * D])
    g.norm_w = dt("norm_w", [4, D])
    g.w_in = dt("w_in", [4, D, INW])
    g.b_merge = dt("b_merge", [4, 3 * D])
    g.conv_a = dt("conv_a", [4, 3, 512])
    g.conv_dn = dt("conv_dn", [4, 3, 1536])
    g.dn_a_log = dt("dn_a_log", [4, 8])
    g.dn_dt_bias = dt("dn_dt_bias", [4, 8])
    g.dn_onorm = dt("dn_onorm", [4, 128])
    g.gla_w_r2 = dt("gla_w_r2", [4, 2, 16, 256])
    g.gla_b_r = dt("gla_b_r", [4, 2, 256])
    g.gla_onorm = dt("gla_onorm", [4, 128])
    g.w_pa = dt("w_pa", [4, 512, D])
    g.w_pb = dt("w_pb", [4, 512, D])
    g.w_pc = dt("w_pc", [4, 512, D])
    g.w_o = dt("w_o", [4, D, D])
    g.fnw = dt("final_norm_w", [1, D])
    g.cst_in = dt("cst", [128, NCST])
    g.out = dt("out", [2, NLAT, D], kind="ExternalOutput")
    import os
    g.P = nc.dram_tensor("P_scr", [(1 if os.environ.get("SMALLP") else NCH) * 128, T], F32).ap()
    g.GB = nc.dram_tensor("GB_scr", [T, 16], F32).ap()
    g.xrl = nc.dram_tensor("xrl_scr", [2, NLAT, D], F32).ap()
    g.xrc = nc.dram_tensor("xrc_scr", [2, NCTX, D], F32).ap()
    g.YA = nc.dram_tensor("YA_scr", [512, T], BF16).ap()
    g.YB = nc.dram_tensor("YB_scr", [512, T], BF16).ap()
    g.YC = nc.dram_tensor("YC_scr", [512, T], BF16).ap()
    g.QT = nc.dram_tensor("QT_scr", [512, T], BF16).ap()
    g.KT = nc.dram_tensor("KT_scr", [512, T], BF16).ap()
    g.VT = nc.dram_tensor("VT_scr", [512, T], BF16).ap()
    if debug is not None:
        g.dbg = dt("dbg", debug, kind="ExternalOutput", dtype=(BF16 if ((dbg_what in ("a", "bprep", "b", "c") or os.environ.get("DBGBF16")) and not os.environ.get("DBGF32")) else F32))

    with contextlib.ExitStack() as top:
        def psb(name, shape, dtype=F32):
            return TL(top.enter_context(nc.sbuf_tensor(name, list(shape), dtype)))
        g.cst = psb("cst_sb", [128, NCST])
        g.modcol = psb("modcol", [128, 24, 3])
        g.A1 = psb("A1", [128, 8, 3])
        g.Gb = psb("Gb", [128, 3, D])
        g.epsc = psb("epsc", [128, 2])
        g.cstb = psb("cstb_sb", [128, NCST], BF16)
        with Stage(nc) as S:
            S.dma("sp", g.cst[:, :], g.cst_in, w=[g.cst])
            S.memset("pool", g.epsc[:, :], EPS, w=[g.epsc])
            S.copy("dve", g.cstb[:, :], g.cst[:, :], r=[g.cst], w=[g.cstb])
        def dump2d(src, rows, dtype_rows=128):
            with Stage(nc) as S:
                for r0 in range(0, rows, 128):
                    S.dma("sp", g.dbg[r0:r0 + 128, :], src[r0:r0 + 128, :])
        for l in range(nlayer):
            stage_mod(g, l)
            if dbg_what == "mod":
                with Stage(nc) as S:
                    S.dma("sp", g.dbg[:, 0:72], g.modcol[:, :, :].rearrange("p a b -> p (a b)"), r=[g.modcol])
                    S.dma("sp", g.dbg[:, 72:96], g.A1[:, :, :].rearrange("p a b -> p (a b)"), r=[g.A1])
                    S.dma("sp", g.dbg[:, 96:96 + 3072], g.Gb[:, :, :].rearrange("p a b -> p (a b)"), r=[g.Gb])
                return nc
            for s in range(nseq):
                stage_proj(g, l, s)
                if dbg_what == "proj":
                    dump2d(g.P, NCH * 128)
                    return nc
                stage_a(g, l, s)
                if dbg_what == "hook":
                    return nc
                if dbg_what == "proj2":
                    dump2d(g.P, NCH * 128)
                    return nc
                if dbg_what == "a":
                    dump2d(g.YA, 512)
                    return nc
                stage_bprep(g, l, s)
                if dbg_what == "bprep":
                    dump2d(g.QT, 512)
                    with Stage(nc) as S:
                        for r0 in range(0, 512, 128):
                            S.dma("sp", g.dbg[512 + r0:512 + r0 + 128, :], g.KT[r0:r0 + 128, :])
                            S.dma("sp", g.dbg[1024 + r0:1024 + r0 + 128, :], g.VT[r0:r0 + 128, :])
                    return nc
                stage_bscan(g, l, s)
                if os.environ.get("BDBG"):
                    return nc
                if dbg_what == "b":
                    dump2d(g.YB, 512)
                    return nc
                stage_c(g, l, s)
                if dbg_what == "c":
                    dump2d(g.YC, 512)
                    return nc
                stage_merge(g, l, s, last=(l == nlayer - 1) and not os.environ.get("MERGE_NOTLAST"))
                if dbg_what == "merge":
                    with Stage(nc) as S:
                        for r0 in range(0, NLAT, 128):
                            S.dma("sp", g.dbg[r0:r0 + 128, :], g.xrl[s, r0:r0 + 128, :])
                        for r0 in range(0, NCTX, 128):
                            S.dma("sp", g.dbg[NLAT + r0:NLAT + r0 + 128, :], g.xrc[s, r0:r0 + 128, :])
                    return nc
    return nc


_NC_CACHE = {}


def kernel(**inputs):
    n = 8
    if "nc" not in _NC_CACHE:
        _NC_CACHE["nc"] = build_nc()
    nc = _NC_CACHE["nc"]
    cst = make_consts()
    f = lambda k: np.ascontiguousarray(np.asarray(inputs[k], dtype=np.float32))
    shared = {
        "c_ctx": f("c_ctx").reshape(1, D), "w_mod": f("w_mod"), "b_mod": f("b_mod"), "norm_w": f("norm_w"),
        "w_in": f("w_in"), "b_merge": f("b_merge"), "conv_a": f("conv_a"), "conv_dn": f("conv_dn"),
        "dn_a_log": f("dn_a_log").reshape(4, 8), "dn_dt_bias": f("dn_dt_bias").reshape(4, 8),
        "dn_onorm": f("dn_onorm"), "gla_w_r2": f("gla_w_r2"), "gla_b_r": f("gla_b_r"), "gla_onorm": f("gla_onorm"),
        "w_pa": f("w_pa"), "w_pb": f("w_pb"), "w_pc": f("w_pc"), "w_o": f("w_o"),
        "final_norm_w": f("final_norm_w").reshape(1, D), "cst": cst,
    }
    x = f("x")
    c = f("c")
    ctx = f("ctx")
    in_maps = []
    for i in range(n):
        m = dict(shared)
        m["x"] = x[2 * i:2 * i + 2]
        m["c"] = c[2 * i:2 * i + 2]
        m["ctx"] = ctx[2 * i:2 * i + 2]
        in_maps.append(m)
    res = run_bass_kernel_spmd(nc, in_maps, core_ids=list(range(n)))
    return np.concatenate([r["out"] for r in res.results], axis=0)
```

```python
import contextlib
import os
import numpy as np
import concourse.bass as bass
import concourse.mybir as mybir
from concourse.bass_utils import run_bass_kernel_spmd

F32 = mybir.dt.float32
BF16 = mybir.dt.bfloat16
ALU = mybir.AluOpType
AF = mybir.ActivationFunctionType

ENGS = ("pe", "act", "dve", "pool", "sp")
N_DMA_SLOTS = 40

D = 1024
T = 4352
NBLK = 34
NCTX = 256
NLAT = 4096
EPS = 1e-6
NCH = 69


class Buf:
    __slots__ = ("w", "r", "pid")

    def __init__(self):
        self.w = None
        self.r = {}
        self.pid = None


class TL:
    def __init__(self, t):
        self.t = t
        self.b = Buf()
        self.bank = None

    def __getitem__(self, idx):
        return self.t[idx]


def _banks(xs):
    out = []
    for x in xs:
        bk = getattr(x, "bank", None)
        if bk is not None and bk not in out:
            out.append(bk)
    return out


def _bufs(xs):
    out = []
    for x in xs:
        if x is None:
            continue
        out.append(x.b if hasattr(x, "b") else x)
    return out


class Prog:
    def __init__(self, nc):
        self.nc = nc
        self.streams = {e: [] for e in ENGS}
        self.count = {e: 0 for e in ENGS}
        self.known = {e: {} for e in ENGS}
        self.slot_val = [0] * N_DMA_SLOTS
        self.next_slot = 0

    def _wait(self, eng, tok, same_ok=False):
        if tok is None:
            return
        key, val = tok
        if key == eng and not same_ok:
            return
        if self.known[eng].get(key, 0) >= val:
            return
        self.known[eng][key] = val
        self.streams[eng].append(("wait", key, val))

    def _deps(self, eng, reads, writes):
        for b in list(reads) + list(writes):
            if b.pid is not self:
                b.pid = self
                b.w = None
                b.r = {}
        so = (eng != "pe")
        for b in reads:
            self._wait(eng, b.w, same_ok=so)
        for b in writes:
            self._wait(eng, b.w, same_ok=so)
            for k, v in b.r.items():
                self._wait(eng, (k, v), same_ok=so)

    def _mark(self, tok, reads, writes):
        k, v = tok
        for b in reads:
            if b.r.get(k, 0) < v:
                b.r[k] = v
        for b in writes:
            b.w = tok
            b.r = {}

    def op(self, eng, fn, reads=(), writes=()):
        banks = _banks(list(reads) + list(writes))
        reads = _bufs(reads)
        writes = _bufs(writes)
        self._deps(eng, reads, writes)
        for bk in banks:
            if bk.pid is not self:
                bk.pid = self
                bk.w = None
            self._wait(eng, bk.w)
        self.count[eng] += 1
        self.streams[eng].append(("inst", fn))
        self._mark((eng, self.count[eng]), reads, writes)
        for bk in banks:
            bk.w = (eng, self.count[eng])

    def dma(self, q, out, in_, reads=(), writes=(), **kw):
        reads = _bufs(reads)
        writes = _bufs(writes)
        self._deps(q, reads, writes)
        s = self.next_slot
        self.next_slot = (s + 1) % N_DMA_SLOTS
        key = "d%d" % s
        if self.slot_val[s] > 0:
            self._wait(q, (key, self.slot_val[s]))
        self.slot_val[s] += 16
        self.streams[q].append(("dma", out, in_, key, kw))
        self._mark((key, self.slot_val[s]), reads, writes)

    def check(self):
        sem = {}
        pos = {e: 0 for e in ENGS}
        progress = True
        while progress:
            progress = False
            for e in ENGS:
                st = self.streams[e]
                while pos[e] < len(st):
                    it = st[pos[e]]
                    if it[0] == "wait":
                        if sem.get(it[1], 0) < it[2]:
                            break
                    elif it[0] == "inst":
                        sem[e] = sem.get(e, 0) + 1
                    else:
                        sem[it[3]] = sem.get(it[3], 0) + 16
                    pos[e] += 1
                    progress = True
        for e in ENGS:
            if pos[e] < len(self.streams[e]):
                raise RuntimeError("sync deadlock: %s stuck at %d/%d on %r (sem=%r)" % (
                    e, pos[e], len(self.streams[e]), self.streams[e][pos[e]], sem.get(self.streams[e][pos[e]][1])))

    def emit(self):
        nc = self.nc
        for s in range(N_DMA_SLOTS):
            if self.slot_val[s] > 0:
                self._wait("sp", ("d%d" % s, self.slot_val[s]))
        for e in ENGS:
            if e != "sp" and self.count[e] > 0:
                self._wait("sp", (e, self.count[e]))
        self.count["sp"] += 1
        self.streams["sp"].append(("inst", lambda e: e.nop()))
        for e in ENGS:
            if e != "sp":
                self.streams[e].append(("wait", "sp", self.count["sp"]))
        self.check()
        with contextlib.ExitStack() as st:
            sems = {}
            for e in ENGS:
                sems[e] = st.enter_context(nc.semaphore("s_" + e))
            for s in range(N_DMA_SLOTS):
                if self.slot_val[s] > 0:
                    sems["d%d" % s] = st.enter_context(nc.semaphore("s_d%d" % s))
            nc.all_engine_barrier()
            for sm in sems.values():
                nc.gpsimd.sem_clear(sm)
            nc.all_engine_barrier()
            block = st.enter_context(nc.Block())

            def run(engname):
                stream = self.streams[engname]

                def body(eng):
                    for it in stream:
                        if it[0] == "wait":
                            eng.wait_ge(sems[it[1]], it[2])
                        elif it[0] == "inst":
                            it[1](eng).then_inc(sems[engname], 1)
                        else:
                            _, out, in_, key, kw = it
                            eng.dma_start(out=out, in_=in_, **kw).then_inc(sems[key], 16)
                return body

            block.tensor(run("pe"))
            block.scalar(run("act"))
            block.vector(run("dve"))
            block.gpsimd(run("pool"))
            block.sync(run("sp"))


_UID = [0]


class Stage:
    def __init__(self, nc):
        self.nc = nc
        self.st = contextlib.ExitStack()
        self.P = Prog(nc)
        self.n = 0

    def __enter__(self):
        self.st.__enter__()
        return self

    def __exit__(self, *a):
        if a[0] is None:
            self.P.emit()
        return self.st.__exit__(*a)

    def sb(self, shape, dt=F32):
        _UID[0] += 1
        return TL(self.st.enter_context(self.nc.sbuf_tensor("sb_%d" % _UID[0], list(shape), dt)))

    def ps(self, shape=(128, 512), dt=F32):
        _UID[0] += 1
        t = TL(self.st.enter_context(self.nc.psum_tensor("ps_%d" % _UID[0], list(shape), dt)))
        t.bank = Buf()
        return t

    def mm(self, out, lhsT, rhs, start=True, stop=True, r=(), w=()):
        self.P.op("pe", lambda e: e.matmul(out, lhsT=lhsT, rhs=rhs, start=start, stop=stop), r, w)

    def tr(self, out, in_, ident, r=(), w=()):
        self.P.op("pe", lambda e: e.transpose(out, in_, ident), r, w)

    def act(self, out, in_, func, bias=None, scale=None, accum=None, r=(), w=()):
        kw = {}
        if bias is not None:
            kw["bias"] = bias
        if scale is not None:
            kw["scale"] = scale
        if accum is not None:
            kw["accum_out"] = accum
        self.P.op("act", lambda e: e.activation(out=out, in_=in_, func=func, **kw), r, w)

    def ts(self, eng, out, in0, s1, s2, op0, op1=None, r=(), w=()):
        if op1 is None:
            self.P.op(eng, lambda e: e.tensor_scalar(out=out, in0=in0, scalar1=s1, scalar2=0.0, op0=op0, op1=ALU.add), r, w)
        else:
            self.P.op(eng, lambda e: e.tensor_scalar(out=out, in0=in0, scalar1=s1, scalar2=s2, op0=op0, op1=op1), r, w)

    def stt(self, eng, out, in0, scalar, in1, op0, op1, r=(), w=()):
        self.P.op(eng, lambda e: e.scalar_tensor_tensor(out=out, in0=in0, scalar=scalar, in1=in1, op0=op0, op1=op1), r, w)

    def tt(self, eng, out, in0, in1, op, r=(), w=()):
        self.P.op(eng, lambda e: e.tensor_tensor(out=out, in0=in0, in1=in1, op=op), r, w)

    def copy(self, eng, out, in_, r=(), w=()):
        if eng == "act":
            self.P.op("act", lambda e: e.copy(out=out, in_=in_), r, w)
        else:
            self.P.op(eng, lambda e: e.tensor_copy(out=out, in_=in_), r, w)

    def memset(self, eng, ap, val, w=()):
        self.P.op(eng, lambda e: e.memset(ap, val), (), w)

    def dma(self, q, out, in_, r=(), w=(), **kw):
        self.P.dma(q, out, in_, r, w, **kw)


C_ID, C_TRIU, C_TRIL, C_TRIUS, C_TRILS, C_ONES, C_NEGF, C_NEGB, C_SEL = [i * 128 for i in range(9)]
C_MD16, C_MS16, C_MS32, C_MS64 = [11 * 128 + i * 512 for i in range(4)]
NCST = 11 * 128 + 4 * 512


def make_consts():
    c = np.zeros((128, NCST), np.float32)
    t = np.arange(128)[:, None]
    i = np.arange(128)[None, :]
    c[:, C_ID:C_ID + 128] = (t == i)
    c[:, C_TRIU:C_TRIU + 128] = (t <= i)
    c[:, C_TRIL:C_TRIL + 128] = (t >= i)
    c[:, C_TRIUS:C_TRIUS + 128] = (t < i)
    c[:, C_TRILS:C_TRILS + 128] = (t > i)
    c[:, C_ONES:C_ONES + 128] = 1.0
    c[:, C_NEGF:C_NEGF + 128] = np.where(t <= i, 0.0, -1e5)
    c[:, C_NEGB:C_NEGB + 128] = np.where(t >= i, 0.0, -1e5)
    for s in range(3):
        c[s, C_SEL + s * 128:C_SEL + (s + 1) * 128] = 1.0
    md16 = (t // 16 == i // 16)
    for off, b in ((C_MS16, 16), (C_MS32, 32), (C_MS64, 64)):
        m = (t // (2 * b) == i // (2 * b)) & (t // b != i // b)
        for q in range(4):
            c[:, off + q * 128:off + (q + 1) * 128] = m
    for q in range(4):
        c[:, C_MD16 + q * 128:C_MD16 + (q + 1) * 128] = md16
    return c


O_AB, O_AC, O_AX, O_AZ = 0, 512, 1024, 1536
O_Q, O_K, O_V, O_DZ = 2048, 2560, 3072, 3584
O_DA, O_DB = 4096, 4104
O_GQ, O_GK, O_GV, O_GZ, O_GR, O_MG = 4112, 4368, 4624, 5136, 5648, 5680
INW = 8752


def chunk_list():
    ch = []
    for c0 in range(0, 4096, 128):
        ch.append((c0, 128))
    for c0 in range(O_GQ, O_GR, 128):
        ch.append((c0, 128))
    ch.append((O_GR, 32))
    for c0 in range(O_MG, INW, 128):
        ch.append((c0, 128))
    assert len(ch) == NCH
    return ch


CH = chunk_list()
CI_AB, CI_AC, CI_AX, CI_AZ = 0, 4, 8, 12
CI_Q, CI_K, CI_V, CI_DZ = 16, 20, 24, 28
CI_GQ, CI_GK, CI_GV, CI_GZ, CI_GR, CI_MG = 32, 34, 36, 40, 44, 45


class Ctx:
    pass


def tok_src(g, l, s, n):
    if n < 2:
        base = g.ctx_in if l == 0 else g.xrc
        return base[s, n * 128:(n + 1) * 128, :]
    base = g.x_in if l == 0 else g.xrl
    return base[s, (n - 2) * 128:(n - 1) * 128, :]


def stage_mod(g, l):
    nc = g.nc
    with Stage(nc) as S:
        c3T = S.sb([128, 8, 3])
        sc3T = S.sb([128, 8, 3])
        bcol = S.sb([128, 24])
        ncol = S.sb([128, 8])
        brow = S.sb([3, 1024])
        grow = S.sb([3, 1024])
        wm = [S.sb([128, 8, 512]) for _ in range(2)]
        pm = S.ps([128, 512])
        pr = S.ps([128, 512])
        pg = [S.ps([128, 512]) for _ in range(2)]
        for s in range(2):
            S.dma("sp", c3T[:, :, s], g.c_in[s, :].rearrange("(k p) -> p k", p=128), w=[c3T], allow_slow_non_contiguous=True)
        S.dma("sp", c3T[:, :, 2], g.cctx_in[0, :].rearrange("(k p) -> p k", p=128), w=[c3T], allow_slow_non_contiguous=True)
        S.dma("sp", bcol[:, :], g.b_mod[l, :].rearrange("(c p) -> p c", p=128), w=[bcol], allow_slow_non_contiguous=True)
        S.dma("sp", ncol[:, :], g.norm_w[l, :].rearrange("(c p) -> p c", p=128), w=[ncol], allow_slow_non_contiguous=True)
        S.dma("sp", brow[:, :], g.b_mod[l:l + 1, 2048:3072].partition_broadcast(3), w=[brow])
        S.act(sc3T[:, :, :], c3T[:, :, :], AF.Silu, r=[c3T], w=[sc3T])
        import os
        lvl = int(os.environ.get("MODLVL", "99"))
        wv = g.w_mod[l].rearrange("(k p) n -> p k n", p=128)
        for cg in range(6 if lvl >= 2 else 0):
            wt = wm[cg % 2]
            S.dma("sp", wt[:, :, :], wv[:, :, cg * 512:(cg + 1) * 512], w=[wt])
            for j in range(4):
                chn = cg * 4 + j
                for k in range(8):
                    S.mm(pm[:, j * 4:j * 4 + 3], wt[:, k, j * 128:(j + 1) * 128], sc3T[:, k, :], start=(k == 0), stop=(k == 7), r=[wt, sc3T], w=[pm])
                S.ts("dve", g.modcol[:, chn, :], pm[:, j * 4:j * 4 + 3], bcol[:, chn:chn + 1], None, ALU.add, r=[pm, bcol], w=[g.modcol])
            if cg >= 4 and lvl >= 3:
                for k in range(8):
                    S.mm(pr[0:3, :], sc3T[:, k, :], wt[:, k, :], start=(k == 0), stop=(k == 7), r=[wt, sc3T], w=[pr])
                S.tt("dve", grow[:, (cg - 4) * 512:(cg - 3) * 512], pr[0:3, :], brow[:, (cg - 4) * 512:(cg - 3) * 512], ALU.add, r=[pr, brow], w=[grow])
        for s in range(3 if lvl >= 4 else 0):
            for cg in range(2):
                p = pg[cg]
                S.mm(p[:, :], g.cst[0:3, C_SEL + s * 128:C_SEL + (s + 1) * 128], grow[0:3, cg * 512:(cg + 1) * 512], r=[grow, g.cst], w=[p])
                S.copy("act", g.Gb[:, s, cg * 512:(cg + 1) * 512], p[:, :], r=[p], w=[g.Gb])
        for k in range(8 if lvl >= 5 else 0):
            S.ts("dve", g.A1[:, k, :], g.modcol[:, 8 + k, :], 1.0, ncol[:, k:k + 1], ALU.add, ALU.mult, r=[g.modcol, ncol], w=[g.A1])


def stage_proj(g, l, s):
    nc = g.nc
    with Stage(nc) as S:
        hT = S.sb([128, 8, T], BF16)
        xb = [S.sb([128, D]) for _ in range(2)]
        xn = [S.sb([128, D]) for _ in range(2)]
        junk = S.sb([128, D])
        stat = S.sb([128, NBLK, 4])
        wst = [S.sb([128, 8, 512]) for _ in range(2)]
        wb = [S.sb([128, 8, 512], BF16) for _ in range(2)]
        stg = [S.sb([128, 512]) for _ in range(4)]
        wab32 = S.sb([128, 8, 16])
        wab = S.sb([128, 8, 16], BF16)
        gbst = S.sb([128, NBLK, 16])
        ptr = [S.ps([128, 512]) for _ in range(2)]
        pp = [S.ps([128, 512]) for _ in range(4)]
        pab = S.ps([128, 512])
        ident = g.cst[:, C_ID:C_ID + 128]
        for n in range(NBLK):
            sidx = 2 if n < 2 else s
            x_t = xb[n % 2]
            xn_t = xn[n % 2]
            S.dma("sp", x_t[:, :], tok_src(g, l, s, n), w=[x_t])
            S.act(junk[:, :], x_t[:, :], AF.Square, accum=stat[:, n, 0:1], r=[x_t], w=[junk, stat])
            S.ts("dve", stat[:, n, 1:2], stat[:, n, 0:1], 1.0 / D, EPS, ALU.mult, ALU.add, r=[stat], w=[stat])
            S.act(stat[:, n, 2:3], stat[:, n, 1:2], AF.Ln, r=[stat], w=[stat])
            S.act(stat[:, n, 3:4], stat[:, n, 2:3], AF.Exp, scale=-0.5, r=[stat], w=[stat])
            S.ts("dve", xn_t[:, :], x_t[:, :], stat[:, n, 3:4], None, ALU.mult, r=[x_t, stat], w=[xn_t])
            for k in range(8):
                p = ptr[k // 4]
                S.tr(p[:, (k % 4) * 128:(k % 4 + 1) * 128], xn_t[:, k * 128:(k + 1) * 128], ident, r=[xn_t, g.cst], w=[p])
            for k in range(8):
                p = ptr[k // 4]
                src = p[:, (k % 4) * 128:(k % 4 + 1) * 128]
                dst = hT[:, k, n * 128:(n + 1) * 128]
                if k // 4 == 0:
                    S.ts("dve", dst, src, g.A1[:, k, sidx:sidx + 1], g.modcol[:, k, sidx:sidx + 1], ALU.mult, ALU.add, r=[p, g.A1, g.modcol], w=[hT])
                else:
                    S.act(dst, src, AF.Identity, bias=g.modcol[:, k, sidx:sidx + 1], scale=g.A1[:, k, sidx:sidx + 1], r=[p, g.A1, g.modcol], w=[hT])
        wv = g.w_in[l].rearrange("(k p) n -> p k n", p=128)
        S.dma("sp", wab32[:, :, :], wv[:, :, O_DA:O_DA + 16], w=[wab32])
        S.copy("dve", wab[:, :, :], wab32[:, :, :], r=[wab32], w=[wab])
        for n in range(NBLK):
            for k in range(8):
                S.mm(pab[:, (n % 8) * 16:(n % 8 + 1) * 16], hT[:, k, n * 128:(n + 1) * 128], wab[:, k, :], start=(k == 0), stop=(k == 7), r=[hT, wab], w=[pab])
            S.copy("dve", gbst[:, n, :], pab[:, (n % 8) * 16:(n % 8 + 1) * 16], r=[pab], w=[gbst])
        S.dma("act", g.GB[:, :].rearrange("(n p) c -> p n c", p=128), gbst[:, :, :], r=[gbst])
        tiles = [(0, NCTX)] + [(NCTX + i * 512, 512) for i in range(8)]
        ngrp = (NCH + 3) // 4
        ev = 0

        def load_w(gi):
            chs = list(range(gi * 4, min(NCH, gi * 4 + 4)))
            ws = wst[gi % 2]
            wbt = wb[gi % 2]
            for j, c in enumerate(chs):
                c0, ncol_ = CH[c]
                if c == CI_GR:
                    S.memset("pool", ws[:, :, j * 128:(j + 1) * 128], 0.0, w=[ws])
                    S.dma("sp", ws[:, :, j * 128:j * 128 + 16], wv[:, :, c0:c0 + 16], w=[ws])
                    S.dma("sp", ws[:, :, j * 128 + 32:j * 128 + 48], wv[:, :, c0 + 16:c0 + 32], w=[ws])
                else:
                    S.dma("sp", ws[:, :, j * 128:j * 128 + ncol_], wv[:, :, c0:c0 + ncol_], w=[ws])
            nw = len(chs) * 128
            S.copy("pool", wbt[:, :, 0:nw], ws[:, :, 0:nw], r=[ws], w=[wbt])

        load_w(0)
        for gi in range(ngrp):
            chs = list(range(gi * 4, min(NCH, gi * 4 + 4)))
            wbt = wb[gi % 2]
            if gi + 1 < ngrp:
                load_w(gi + 1)
            for (t0, n) in tiles:
                for j, c in enumerate(chs):
                    nrow = 48 if c == CI_GR else CH[c][1]
                    p = pp[ev % 4]
                    sg = stg[ev % 4]
                    for k in range(8):
                        S.mm(p[0:nrow, 0:n], wbt[:, k, j * 128:j * 128 + nrow], hT[:, k, t0:t0 + n], start=(k == 0), stop=(k == 7), r=[wbt, hT], w=[p])
                    if ev % 2 == 0:
                        S.copy("act", sg[0:nrow, 0:n], p[0:nrow, 0:n], r=[p], w=[sg])
                    else:
                        S.copy("dve", sg[0:nrow, 0:n], p[0:nrow, 0:n], r=[p], w=[sg])
                    S.dma("sp", g.P[c * 128:c * 128 + nrow, t0:t0 + n], sg[0:nrow, 0:n], r=[sg])
                    ev += 1


TILES = [(0, NCTX)] + [(NCTX + i * 512, 512) for i in range(8)]


def rows_view(ap, n, lo, hi):
    if n == NCTX:
        return ap[:, lo:n + hi]
    v = ap.rearrange("p (r c) -> p r c", c=64)
    return v[:, :, lo:64 + hi]


def stage_a(g, l, s):
    nc = g.nc
    with Stage(nc) as S:
        cw = S.sb([128, 4, 3])
        for k in range(3):
            S.dma("sp", cw[:, :, k], g.conv_a[l, k, :].rearrange("(c p) -> p c", p=128), w=[cw], allow_slow_non_contiguous=True)
        NB = 3
        tin = [[S.sb([128, 512]) for _ in range(4)] for _ in range(NB)]
        tu = [S.sb([128, 512]) for _ in range(NB)]
        ty = [S.sb([128, 512]) for _ in range(NB)]
        tsz = [S.sb([128, 512]) for _ in range(NB)]
        tout = [S.sb([128, 512], BF16) for _ in range(NB)]
        it = 0
        for (t0, n) in TILES:
            for j in range(4):
                ab, ac, ax, az = tin[it % NB]
                u, y, sz, yo = tu[it % NB], ty[it % NB], tsz[it % NB], tout[it % NB]
                for tl, ci in ((ab, CI_AB), (ac, CI_AC), (ax, CI_AX), (az, CI_AZ)):
                    S.dma("sp", tl[:, 0:n], g.P[(ci + j) * 128:(ci + j + 1) * 128, t0:t0 + n], w=[tl])
                S.tt("pool", u[:, 0:n], ac[:, 0:n], ax[:, 0:n], ALU.mult, r=[ac, ax], w=[u])
                S.act(y[:, 0:n], u[:, 0:n], AF.Copy, scale=cw[:, j, 1:2], r=[u, cw], w=[y])
                S.stt("dve", rows_view(y[:, 0:n], n, 1, 0), rows_view(u[:, 0:n], n, 0, -1), cw[:, j, 0:1], rows_view(y[:, 0:n], n, 1, 0),
                      ALU.mult, ALU.add, r=[u, y, cw], w=[y])
                S.stt("dve", rows_view(y[:, 0:n], n, 0, -1), rows_view(u[:, 0:n], n, 1, 0), cw[:, j, 2:3], rows_view(y[:, 0:n], n, 0, -1),
                      ALU.mult, ALU.add, r=[u, y, cw], w=[y])
                S.act(sz[:, 0:n], az[:, 0:n], AF.Silu, r=[az], w=[sz])
                S.tt("pool", u[:, 0:n], ab[:, 0:n], y[:, 0:n], ALU.mult, r=[ab, y], w=[u])
                S.tt("dve", yo[:, 0:n], u[:, 0:n], sz[:, 0:n], ALU.mult, r=[u, sz], w=[yo])
                S.dma("act", g.YA[j * 128:(j + 1) * 128, t0:t0 + n], yo[:, 0:n], r=[yo])
                it += 1


def stage_merge(g, l, s, last):
    nc = g.nc
    with Stage(nc) as S:
        wp = [S.sb([128, 4, D], BF16) for _ in range(3)]
        wo = S.sb([128, 8, D], BF16)
        wst = [S.sb([128, 4, D])] * 2
        bmc = S.sb([128, 24])
        S.dma("sp", bmc[:, :], g.b_merge[l, :].rearrange("(c p) -> p c", p=128), w=[bmc], allow_slow_non_contiguous=True)
        srcs = [g.w_pa, g.w_pb, g.w_pc]
        for br in range(3):
            S.dma("sp", wst[br % 2][:, :, :], srcs[br][l].rearrange("(k p) n -> p k n", p=128), w=[wst[br % 2]])
            S.copy("pool", wp[br][:, :, :], wst[br % 2][:, :, :], r=[wst[br % 2]], w=[wp[br]])
        for hf in range(2):
            S.dma("sp", wst[(hf + 1) % 2][:, :, :], g.w_o[l].rearrange("(k p) n -> p k n", p=128)[:, hf * 4:(hf + 1) * 4, :], w=[wst[(hf + 1) % 2]])
            S.copy("pool", wo[:, hf * 4:(hf + 1) * 4, :], wst[(hf + 1) % 2][:, :, :], r=[wst[(hf + 1) % 2]], w=[wo])
        if last:
            fnw = S.sb([128, D])
            S.dma("sp", fnw[:, :], g.fnw[0:1, :].partition_broadcast(128), w=[fnw])
            fst = S.sb([128, 8, 4])
        ysb = [[S.sb([128, 4, 512], BF16) for _ in range(3)]] * 2
        mgt = [S.sb([128, 512]) for _ in range(6)]
        gt = [S.sb([128, 512]) for _ in range(6)]
        tt_ = [S.sb([128, 512]) for _ in range(6)]
        mT = [S.sb([128, 8, 512], BF16) for _ in range(2)]
        xblk = [S.sb([128, D]) for _ in range(2)]
        xo = [S.sb([128, D]) for _ in range(2)]
        tmp = [S.sb([128, 512]) for _ in range(2)]
        junk = S.sb([128, D]) if last else None
        pb = [S.ps([128, 512]) for _ in range(6)]
        po = [S.ps([128, 512]) for _ in range(2)]
        ysrc = [g.YA, g.YB, g.YC]
        e3 = 0
        eo = 0
        for ti, (t0, n) in enumerate(TILES):
            if last and ti == 0:
                continue
            sidx = 2 if ti == 0 else s
            yt = ysb[ti % 2]
            m_t = mT[ti % 2]
            for br in range(3):
                S.dma("sp", yt[br][:, :, 0:n], ysrc[br][:, t0:t0 + n].rearrange("(j p) t -> p j t", p=128), w=[yt[br]])
            for oc in range(8):
                trip = []
                for br in range(3):
                    i6 = e3 % 6
                    e3 += 1
                    ci = CI_MG + br * 8 + oc
                    S.dma("sp", mgt[i6][:, 0:n], g.P[ci * 128:(ci + 1) * 128, t0:t0 + n], w=[mgt[i6]])
                    S.act(gt[i6][:, 0:n], mgt[i6][:, 0:n], AF.Sigmoid, bias=bmc[:, br * 8 + oc:br * 8 + oc + 1], r=[mgt[i6], bmc], w=[gt[i6]])
                    for j in range(4):
                        S.mm(pb[i6][:, 0:n], wp[br][:, j, oc * 128:(oc + 1) * 128], yt[br][:, j, 0:n], start=(j == 0), stop=(j == 3), r=[wp[br], yt[br]], w=[pb[i6]])
                    S.tt("dve", tt_[i6][:, 0:n], pb[i6][:, 0:n], gt[i6][:, 0:n], ALU.mult, r=[pb[i6], gt[i6]], w=[tt_[i6]])
                    trip.append(tt_[i6])
                S.tt("pool", trip[0][:, 0:n], trip[0][:, 0:n], trip[1][:, 0:n], ALU.add, r=[trip[0], trip[1]], w=[trip[0]])
                S.tt("pool", m_t[:, oc, 0:n], trip[0][:, 0:n], trip[2][:, 0:n], ALU.add, r=[trip[0], trip[2]], w=[m_t])
            for bi in range(n // 128):
                nblk = t0 // 128 + bi
                xb_ = xblk[eo % 2]
                xo_ = xo[eo % 2]
                eo += 1
                S.dma("sp", xb_[:, :], tok_src(g, l, s, nblk), w=[xb_])
                for cg in range(2):
                    p = po[cg]
                    for k in range(8):
                        S.mm(p[:, :], m_t[:, k, bi * 128:(bi + 1) * 128], wo[:, k, cg * 512:(cg + 1) * 512], start=(k == 0), stop=(k == 7), r=[m_t, wo], w=[p])
                    S.tt("dve", tmp[cg][:, :], p[:, :], g.Gb[:, sidx, cg * 512:(cg + 1) * 512], ALU.mult, r=[p, g.Gb], w=[tmp[cg]])
                    S.tt("pool", xo_[:, cg * 512:(cg + 1) * 512], tmp[cg][:, :], xb_[:, cg * 512:(cg + 1) * 512], ALU.add, r=[tmp[cg], xb_], w=[xo_])
                if not last:
                    dst = g.xrc[s, nblk * 128:(nblk + 1) * 128, :] if nblk < 2 else g.xrl[s, (nblk - 2) * 128:(nblk - 1) * 128, :]
                    S.dma("act", dst, xo_[:, :], r=[xo_])
                else:
                    q = eo % 8
                    S.act(junk[:, :], xo_[:, :], AF.Square, accum=fst[:, q, 0:1], r=[xo_], w=[junk, fst])
                    S.ts("dve", fst[:, q, 1:2], fst[:, q, 0:1], 1.0 / D, EPS, ALU.mult, ALU.add, r=[fst], w=[fst])
                    S.act(fst[:, q, 2:3], fst[:, q, 1:2], AF.Ln, r=[fst], w=[fst])
                    S.act(fst[:, q, 3:4], fst[:, q, 2:3], AF.Exp, scale=-0.5, r=[fst], w=[fst])
                    S.stt("dve", xb_[:, :], xo_[:, :], fst[:, q, 3:4], fnw[:, :], ALU.mult, ALU.mult, r=[xo_, fst, fnw], w=[xb_])
                    S.dma("act", g.out[s, (nblk - 2) * 128:(nblk - 1) * 128, :], xb_[:, :], r=[xb_])


def stage_bprep(g, l, s):
    nc = g.nc
    with Stage(nc) as S:
        cw = S.sb([128, 12, 3])
        for k in range(3):
            S.dma("sp", cw[:, :, k], g.conv_dn[l, k, :].rearrange("(c p) -> p c", p=128), w=[cw], allow_slow_non_contiguous=True)
        raw = [S.sb([128, 514]) for _ in range(3)]
        acc = [S.sb([128, 512]) for _ in range(3)]
        sil = [S.sb([128, 512]) for _ in range(8)]
        vout = [S.sb([128, 512], BF16) for _ in range(2)]
        sq = [S.sb([128, 512]) for _ in range(2)]
        lnt = [S.sb([128, 512]) for _ in range(2)]
        qo = [S.sb([128, 512], BF16) for _ in range(2)]
        pn = [S.ps([128, 512]) for _ in range(2)]
        onesb = g.cstb[:, C_ONES:C_ONES + 128]
        sqh = [S.sb([128, 512], BF16) for _ in range(2)]
        sql = [S.sb([128, 512], BF16) for _ in range(2)]
        it = 0
        for (t0, n) in TILES[int(os.environ.get('TLO', '0')):int(os.environ.get('THI', '9'))]:
            seq_lo, seq_hi = (0, NCTX) if t0 == 0 else (NCTX, T)
            for j in range(int(os.environ.get("JLO", "0")), 12):
                RS = int(os.environ.get("RING", "3"))
                rw = raw[it % RS]
                ac_ = acc[it % RS]
                it += 1
                zl = 1 if t0 - 1 < seq_lo else 0
                zr = 1 if t0 + n + 1 > seq_hi else 0
                if zl and not os.environ.get("NOMEMSET"):
                    S.memset("pool", rw[:, 0:1], 0.0, w=[rw])
                if zr and not os.environ.get("NOMEMSET"):
                    S.memset("pool", rw[:, n + 1:n + 2], 0.0, w=[rw])
                S.dma("sp", rw[:, zl:n + 2 - zr], g.P[(CI_Q + j) * 128:(CI_Q + j + 1) * 128, t0 - 1 + zl:t0 + n + 1 - zr], w=[rw])
                S.act(ac_[:, 0:n], rw[:, 1:n + 1], AF.Copy, scale=cw[:, j, 1:2], r=[rw, cw], w=[ac_])
                S.stt("dve", ac_[:, 0:n], rw[:, 0:n], cw[:, j, 0:1], ac_[:, 0:n], ALU.mult, ALU.add, r=[rw, ac_, cw], w=[ac_])
                S.stt("dve", ac_[:, 0:n], rw[:, 2:n + 2], cw[:, j, 2:3], ac_[:, 0:n], ALU.mult, ALU.add, r=[rw, ac_, cw], w=[ac_])
                if j < 8:
                    S.act(sil[j][:, 0:n], ac_[:, 0:n], AF.Silu, r=[ac_], w=[sil[j]])
                else:
                    vo = vout[j % (1 if os.environ.get('RING') == '1' else 2)]
                    S.act(vo[:, 0:n], ac_[:, 0:n], AF.Silu, r=[ac_], w=[vo])
                    S.dma("act", g.VT[(j - 8) * 128:(j - 7) * 128, t0:t0 + n], vo[:, 0:n], r=[vo])
            for j in range(0 if os.environ.get("NONORM") else 8):
                sq_ = sq[j % 2]
                ln_ = lnt[j % 2]
                q_ = qo[j % 2]
                p = pn[j % 2]
                NL = int(os.environ.get("NORMLVL", "9"))
                S.tt("pool", sq_[:, 0:n], sil[j][:, 0:n], sil[j][:, 0:n], ALU.mult, r=[sil[j]], w=[sq_])
                sh_, sl_ = sqh[j % 2], sql[j % 2]
                S.copy("pool", sh_[:, 0:n], sq_[:, 0:n], r=[sq_], w=[sh_])
                S.tt("pool", sl_[:, 0:n], sq_[:, 0:n], sh_[:, 0:n], ALU.subtract, r=[sq_, sh_], w=[sl_])
                if NL >= 2:
                    S.mm(p[:, 0:n], onesb, sh_[:, 0:n], start=True, stop=False, r=[g.cstb, sh_], w=[p])
                    S.mm(p[:, 0:n], onesb, sl_[:, 0:n], start=False, stop=True, r=[g.cstb, sl_], w=[p])
                if NL >= 3:
                    S.act(ln_[:, 0:n], p[:, 0:n], AF.Ln, bias=g.epsc[:, 0:1], r=[p, g.epsc], w=[ln_])
                if NL >= 4:
                    S.act(ln_[:, 0:n], ln_[:, 0:n], AF.Exp, scale=-0.5, r=[ln_], w=[ln_])
                if NL >= 5:
                    S.stt("dve", q_[:, 0:n], sil[j][:, 0:n], (128.0 ** -0.5) if j < 4 else 1.0, ln_[:, 0:n], ALU.mult, ALU.mult, r=[sil[j], ln_], w=[q_])
                dst = g.QT if j < 4 else g.KT
                S.dma("act", dst[(j % 4) * 128:(j % 4 + 1) * 128, t0:t0 + n], q_[:, 0:n], r=[q_])


class Sub:
    def __init__(self, tl):
        self.t = tl.t
        self.b = Buf()
        self.bank = tl.bank

    def __getitem__(self, idx):
        return self.t[idx]


ORD_F = list(range(NBLK))
ORD_B = [1, 0] + list(range(NBLK - 1, 1, -1))


def stage_bscan(g, l, s):
    nc = g.nc
    nsteps = int(os.environ.get("BSTEPS", str(NBLK)))
    with Stage(nc) as S:
        cst = g.cst
        CI = lambda o: cst[:, o:o + 128]
        ident, ones = CI(C_ID), CI(C_ONES)
        identb4 = S.sb([128, 4, 128], BF16)
        for q in range(4):
            S.copy("dve", identb4[:, q, :], ident, r=[cst], w=[identb4])
        gbr = S.sb([128, NBLK, 16])
        S.dma("sp", gbr[:, :, :], g.GB[:, :].rearrange("(n p) c -> p n c", p=128), w=[gbr])
        dtb = S.sb([128, 8])
        alog = S.sb([128, 8])
        S.dma("sp", dtb[:, :], g.dn_dt_bias[l:l + 1, :].partition_broadcast(128), w=[dtb])
        S.dma("sp", alog[:, :], g.dn_a_log[l:l + 1, :].partition_broadcast(128), w=[alog])
        nega = S.sb([128, 8])
        S.act(nega[:, :], alog[:, :], AF.Exp, r=[alog], w=[nega])
        S.ts("dve", nega[:, :], nega[:, :], -1.0, None, ALU.mult, r=[nega], w=[nega])
        t1 = S.sb([128, NBLK, 8])
        G = S.sb([128, NBLK, 8])
        NG = S.sb([128, NBLK, 8])
        BT = S.sb([128, NBLK, 8])
        NB = S.sb([128, NBLK, 8])
        for c in range(8):
            S.act(t1[:, :, c], gbr[:, :, c], AF.Exp, bias=dtb[:, c:c + 1], r=[gbr, dtb], w=[t1])
        S.act(t1[:, :, :], t1[:, :, :], AF.Ln, bias=1.0, r=[t1], w=[t1])
        for c in range(8):
            S.ts("dve", G[:, :, c], t1[:, :, c], nega[:, c:c + 1], None, ALU.mult, r=[t1, nega], w=[G])
        S.ts("dve", NG[:, :, :], G[:, :, :], -1.0, None, ALU.mult, r=[G], w=[NG])
        S.act(BT[:, :, :], gbr[:, :, 8:16], AF.Exp, scale=-1.0, r=[gbr], w=[BT])
        S.ts("dve", BT[:, :, :], BT[:, :, :], 1.0, None, ALU.add, r=[BT], w=[BT])
        S.P.op("dve", lambda e: e.reciprocal(out=BT[:, :, :], in_=BT[:, :, :]), [BT], [BT])
        S.ts("dve", NB[:, :, :], BT[:, :, :], -1.0, None, ALU.mult, r=[BT], w=[NB])
        banks = [S.ps([128, 512]) if bi_ not in (3, 4) else S.ps([128, 1024], BF16) for bi_ in range(8)]
        EX = S.sb([128, NBLK, 24])
        cstb = g.cstb
        CB = lambda o: cstb[:, o:o + 128]
        onesb = CB(C_ONES)
        for n in range(NBLK):
            pb = banks[n // 17]
            o = (n % 17) * 24
            S.mm(pb[:, o:o + 4], CI(C_TRIU), G[:, n, 0:4], r=[cst, G], w=[pb])
            S.mm(pb[:, o + 4:o + 8], CI(C_TRIL), G[:, n, 4:8], r=[cst, G], w=[pb])
            S.mm(pb[:, o + 8:o + 12], CI(C_TRILS), G[:, n, 0:4], r=[cst, G], w=[pb])
            S.mm(pb[:, o + 12:o + 16], CI(C_TRIUS), G[:, n, 4:8], r=[cst, G], w=[pb])
            S.mm(pb[:, o + 16:o + 24], ones, G[:, n, 0:8], r=[cst, G], w=[pb])
        GC = S.sb([128, NBLK, 8])
        GCh = S.sb([128, NBLK, 8], BF16)
        GChf = S.sb([128, NBLK, 8])
        GCl = S.sb([128, NBLK, 8])
        if not os.environ.get("NOGC"):
            for hb in range(2):
                for n_ in range(17):
                    S.copy("dve", GC[:, hb * 17 + n_, :], banks[hb][:, n_ * 24:n_ * 24 + 8], r=[banks[hb]], w=[GC])
            GL = int(os.environ.get("GCLVL", "9"))
            if GL >= 1:
                S.copy("dve", GCh[:, :, :], GC[:, :, :], r=[GC], w=[GCh])
            if GL >= 2:
                S.copy("dve", GChf[:, :, :], GCh[:, :, :], r=[GCh], w=[GChf])
            if GL >= 3:
                S.tt("dve", GCl[:, :, :], GC[:, :, :], GChf[:, :, :], ALU.subtract, r=[GC, GChf], w=[GCl])
        for hb in range(2):
            S.act(EX[:, hb * 17:(hb + 1) * 17, :].rearrange("p n c -> p (n c)"), banks[hb][:, 0:17 * 24], AF.Exp, r=[banks[hb], GC, GCl], w=[EX])
        KCO = S.sb([128, NBLK, 8])
        NEGC = S.sb([128, NBLK, 8])
        S.tt("dve", KCO[:, :, :], EX[:, :, 8:16], BT[:, :, :], ALU.mult, r=[EX, BT], w=[KCO])
        S.ts("dve", NEGC[:, :, :], EX[:, :, 0:8], -1.0, None, ALU.mult, r=[EX], w=[NEGC])
        onc = S.sb([128, 1])
        S.dma("sp", onc[:, :], g.dn_onorm[l, :].rearrange("(p o) -> p o", o=1), w=[onc], allow_slow_non_contiguous=True)

        if os.environ.get("BDBG"):
            for i_, tl_ in enumerate((G, BT, GC, GCl, KCO, NEGC)):
                S.dma("sp", g.dbg[0:128, i_ * 272:(i_ + 1) * 272], tl_[:, :, :].rearrange("p a b -> p (a b)"), r=[tl_])
            S.dma("sp", g.dbg[0:128, 6 * 272:6 * 272 + 816], EX[:, :, :].rearrange("p a b -> p (a b)"), r=[EX])
            return
        qT2 = S.sb([128, 2, T], BF16)
        kT2 = S.sb([128, 2, T], BF16)
        vT2 = S.sb([128, 2, T], BF16)
        otok = S.sb([128, NBLK, 2, 128])
        osub = [[Sub(otok) for _ in range(2)] for _ in range(NBLK)]
        Sf = [S.sb([128, 128]) for _ in range(4)]
        Sb = [S.sb([128, 128], BF16) for _ in range(4)]

        def t4(dt=BF16):
            t = S.sb([128, 4, 128], dt)
            return t, [Sub(t) for _ in range(4)]
        G1, G1s = t4()
        nG1, nG1s = t4()
        ETa, ETas = t4(F32)
        ET, ETs_ = t4(F32)
        ETm, ETms = t4(F32)
        NT, NTs = t4()
        AqT2 = [t4(), t4()]
        kgb2 = [t4(), t4()]
        vtk2 = [t4(), t4()]
        XTf2 = [t4(), t4()]
        rr, rrs = t4()
        yb, ybs = t4()
        Pp = [t4(), t4()]
        PQ = [t4(), t4()]
        PTp = [t4(), t4()]
        XTp = [t4(), t4()]
        XXp = [t4(), t4()]
        Dd, Dds = t4()
        DdT, DdTs = t4()
        C16, C16s = t4()
        C16T, C16Ts = t4()
        C32, C32s = t4()
        C32T, C32Ts = t4()
        C64, C64s = t4()
        M1, M1s = t4()
        M1T, M1Ts = t4()
        GB_ = [[banks[0], banks[1], banks[3], banks[5]], [banks[2], banks[6], banks[4], banks[7]]]
        ROLE = {"KK": (0, 0), "QK": (0, 256), "DT": (1, 0), "kt": (2, 0), "vt": (2, 256), "N0": (2, 512),
                "PP": (3, 0), "PT": (3, 256), "XX": (0, 0), "DTp": (0, 256),
                "ps1": (1, 0), "ps2": (1, 256), "ps3": (1, 0), "ps4": (1, 256), "ps5": (1, 0)}
        rsub = {}

        def R(role, d, hh):
            bi_, off = ROLE[role]
            tl_ = GB_[d][bi_]
            key = (d, bi_, off + hh * 128)
            if key not in rsub:
                rsub[key] = Sub(tl_)
            return tl_[:, off + hh * 128:off + (hh + 1) * 128], rsub[key]

        def RG(role, d):
            bi_, off = ROLE[role]
            tl_ = GB_[d][bi_]
            return tl_[:, off:off + 256], [R(role, d, 0)[1], R(role, d, 1)[1]]

        def g2(t_, d):
            return t_[:, 2 * d:2 * d + 2, :].rearrange("p a b -> p (a b)")
        sl = lambda q: slice(q * 128, (q + 1) * 128)
        zsrc = g.P
        MKg = lambda o: cstb[:, o:o + 256]
        identb2 = identb4[:, 0:2, :].rearrange("p a b -> p (a b)")


        def make_p4(grp, AqT, AqTs, kgb, kgbs, vtk, vtks, XT, XTs):
            def p4a():
                for (d, n, chs) in grp:
                    for (q, hh, c) in chs:
                        kTb = kT2[:, hh, n * 128:(n + 1) * 128]
                        a_, s_ = R("ps1", d, hh)
                        S.mm(a_, kTb, Sb[q][:, :], r=[kT2, Sb[q]], w=[s_])
                        S.stt("dve", rr[:, q, :], a_, NEGC[:, n, c:c + 1], vtk[:, q, :], ALU.mult, ALU.add, r=[s_, NEGC, vtks[q]], w=[rrs[q]])

            def p4b():
                for (d, n, chs) in grp:
                    qs = slice(2 * d, 2 * d + 2)
                    for (q, hh, c) in chs:
                        a_, s_ = R("ps2", d, hh)
                        S.mm(a_, XT[:, q, :], rr[:, q, :], r=[XTs[q], rrs[q]], w=[s_])
                    a_, ss_ = RG("ps2", d)
                    S.copy("act", g2(yb, d), a_, r=ss_, w=ybs[qs])

            def p4c():
                for (d, n, chs) in grp:
                    for (q, hh, c) in chs:
                        qTb = qT2[:, hh, n * 128:(n + 1) * 128]
                        a3, s3 = R("ps3", d, hh)
                        S.mm(a3, qTb, Sb[q][:, :], r=[qT2, Sb[q]], w=[s3])
                        a4, s4 = R("ps4", d, hh)
                        S.mm(a4, AqT[:, q, :], yb[:, q, :], r=[AqTs[q], ybs[q]], w=[s4])
                for (d, n, chs) in grp:
                    for (q, hh, c) in chs:
                        od = otok[:, n, hh, :]
                        ob = osub[n][hh]
                        a3, s3 = R("ps3", d, hh)
                        a4, s4 = R("ps4", d, hh)
                        if (n, hh) not in visited:
                            visited.add((n, hh))
                            S.ts("dve", od, a3, EX[:, n, c:c + 1], None, ALU.mult, r=[s3, EX], w=[ob])
                        else:
                            S.stt("dve", od, a3, EX[:, n, c:c + 1], od, ALU.mult, ALU.add, r=[s3, EX, ob], w=[ob])
                        S.tt("dve", od, a4, od, ALU.add, r=[s4, ob], w=[ob])

            def p4d():
                for (d, n, chs) in grp:
                    for (q, hh, c) in chs:
                        a5, s5 = R("ps5", d, hh)
                        S.mm(a5, kgb[:, q, :], yb[:, q, :], r=[kgbs[q], ybs[q]], w=[s5])
                for (d, n, chs) in grp:
                    for (q, hh, c) in chs:
                        a5, s5 = R("ps5", d, hh)
                        S.stt("dve", Sf[q][:, :], Sf[q][:, :], EX[:, n, 16 + c:17 + c], a5, ALU.mult, ALU.add, r=[Sf[q], EX, s5], w=[Sf[q]])
                        S.copy("pool", Sb[q][:, :], Sf[q][:, :], r=[Sf[q]], w=[Sb[q]])
            return [p4a, p4b, p4c, p4d]

        visited = set()
        for hp in range(2):
            for hh in range(2):
                h = 2 * hp + hh
                S.dma("sp", qT2[:, hh, :], g.QT[h * 128:(h + 1) * 128, :], w=[qT2])
                S.dma("sp", kT2[:, hh, :], g.KT[h * 128:(h + 1) * 128, :], w=[kT2])
                S.dma("sp", vT2[:, hh, :], g.VT[h * 128:(h + 1) * 128, :], w=[vT2])
            for q in range(4):
                S.memset("pool", Sf[q][:, :], 0.0, w=[Sf[q]])
                S.memset("pool", Sb[q][:, :], 0.0, w=[Sb[q]])
            visited.clear()
            if nsteps < NBLK:
                S.memset("pool", otok[:, :, :, :], 0.0, w=[osub[n_][h_] for n_ in range(NBLK) for h_ in range(2)])
            pend = []
            for i in range(nsteps):
                AqT, AqTs = AqT2[i % 2]
                kgb, kgbs = kgb2[i % 2]
                vtk, vtks = vtk2[i % 2]
                XTf, XTfs = XTf2[i % 2]
                grp = []
                for d in range(2):
                    n = (ORD_F if d == 0 else ORD_B)[i]
                    grp.append((d, n, [(d * 2 + hh, hh, d * 4 + 2 * hp + hh) for hh in range(2)]))
                for (d, n, chs) in grp:
                    for (q, hh, c) in chs:
                        kTb = kT2[:, hh, n * 128:(n + 1) * 128]
                        qTb = qT2[:, hh, n * 128:(n + 1) * 128]
                        vTb = vT2[:, hh, n * 128:(n + 1) * 128]
                        a_, s_ = R("kt", d, hh)
                        S.tr(a_, kTb, identb4[:, 0, :], r=[kT2, identb4], w=[s_])
                        a_, s_ = R("vt", d, hh)
                        S.tr(a_, vTb, identb4[:, 0, :], r=[vT2, identb4], w=[s_])
                        a_, s_ = R("KK", d, hh)
                        S.mm(a_, kTb, kTb, r=[kT2], w=[s_])
                        a_, s_ = R("QK", d, hh)
                        S.mm(a_, kTb, qTb, r=[kT2, qT2], w=[s_])
                        S.act(G1[:, q, :], identb4[:, 0, :], AF.Copy, scale=GChf[:, n, c:c + 1], r=[identb4, GChf], w=[G1s[q]])
                        S.act(nG1[:, q, :], identb4[:, 0, :], AF.Copy, scale=GCl[:, n, c:c + 1], r=[identb4, GCl], w=[nG1s[q]])
                        a_, s_ = R("DT", d, hh)
                        S.mm(a_, onesb, G1[:, q, :], start=True, stop=False, r=[cstb, G1s[q]], w=[s_])
                        S.mm(a_, onesb, nG1[:, q, :], start=False, stop=True, r=[cstb, nG1s[q]], w=[s_])
                for (d, n, chs) in grp:
                    neg = CI(C_NEGF) if d == 0 else CI(C_NEGB)
                    for (q, hh, c) in chs:
                        a_, s_ = R("DT", d, hh)
                        S.stt("dve", ETa[:, q, :], a_, GC[:, n, c:c + 1], neg, ALU.subtract, ALU.add, r=[s_, cst, GC], w=[ETas[q]])
                        S.act(ET[:, q, :], ETa[:, q, :], AF.Exp, r=[ETas[q]], w=[ETs_[q]])
                        S.tt("pool", ETm[:, q, :], ET[:, q, :], ident, ALU.subtract, r=[ETs_[q], cst], w=[ETms[q]])
                        a_, s_ = R("KK", d, hh)
                        S.stt("dve", NT[:, q, :], a_, NB[:, n, c:c + 1], ETm[:, q, :], ALU.mult, ALU.mult, r=[s_, NB, ETms[q]], w=[NTs[q]])
                        a_, s_ = R("QK", d, hh)
                        S.stt("dve", AqT[:, q, :], a_, BT[:, n, c:c + 1], ET[:, q, :], ALU.mult, ALU.mult, r=[s_, BT, ETs_[q]], w=[AqTs[q]])
                        a_, s_ = R("kt", d, hh)
                        S.ts("dve", kgb[:, q, :], a_, KCO[:, n, c:c + 1], None, ALU.mult, r=[s_, KCO], w=[kgbs[q]])
                    a_, ss_ = RG("vt", d)
                    S.copy("dve", g2(vtk, d), a_, r=ss_, w=vtks[2 * d:2 * d + 2])
                    for (q, hh, c) in chs:
                        a_, s_ = R("N0", d, hh)
                        S.tr(a_, NT[:, q, :], identb4[:, 0, :], r=[NTs[q], identb4], w=[s_])
                Nn, Nns = Pp[0]
                for (d, n, chs) in grp:
                    a_, ss_ = RG("N0", d)
                    S.copy("act", g2(Nn, d), a_, r=ss_, w=Nns[2 * d:2 * d + 2])
                for (d, n, chs) in grp:
                    qs = slice(2 * d, 2 * d + 2)
                    for (dst, dss, src, sss, mk) in ((Dd, Dds, Nn, Nns, C_MD16), (DdT, DdTs, NT, NTs, C_MD16), (C16, C16s, Nn, Nns, C_MS16),
                                                     (C16T, C16Ts, NT, NTs, C_MS16), (C32, C32s, Nn, Nns, C_MS32), (C32T, C32Ts, NT, NTs, C_MS32),
                                                     (C64, C64s, Nn, Nns, C_MS64)):
                        S.tt("pool", g2(dst, d), g2(src, d), MKg(mk), ALU.mult, r=sss[qs] + [cstb], w=dss[qs])
                    S.tt("dve", g2(XXp[0][0], d), g2(Dd, d), identb2, ALU.add, r=Dds[qs] + [identb4], w=XXp[0][1][qs])
                    S.tt("dve", g2(XTp[0][0], d), g2(DdT, d), identb2, ALU.add, r=DdTs[qs] + [identb4], w=XTp[0][1][qs])
                Xc, Xcs = XXp[0]
                XT, XTs = XTp[0]
                Pc, Pcs, PT, PTs = Dd, Dds, DdT, DdTs
                flip = 0
                for r_ in range(1, 4):
                    Pn, Pns = PQ[r_ % 2]
                    PTn, PTns = PTp[r_ % 2]
                    flip ^= 1
                    Xn, Xns = XXp[flip]
                    XTn, XTns = XTp[flip]
                    for (d, n, chs) in grp:
                        qs = slice(2 * d, 2 * d + 2)
                        for (q, hh, c) in chs:
                            a_, s_ = R("PP", d, hh)
                            S.mm(a_, PT[:, q, :], Pc[:, q, :], r=[PTs[q], Pcs[q]], w=[s_])
                            a_, s_ = R("PT", d, hh)
                            S.mm(a_, Pc[:, q, :], PT[:, q, :], r=[PTs[q], Pcs[q]], w=[s_])
                        a_, ss_ = RG("PP", d)
                        S.copy("act", g2(Pn, d), a_, r=ss_, w=Pns[qs])
                        a_, ss_ = RG("PT", d)
                        S.copy("act", g2(PTn, d), a_, r=ss_, w=PTns[qs])
                    for (d, n, chs) in grp:
                        qs = slice(2 * d, 2 * d + 2)
                        for (q, hh, c) in chs:
                            a_, s_ = R("XX", d, hh)
                            S.mm(a_, PTn[:, q, :], Xc[:, q, :], r=[PTns[q], Xcs[q]], w=[s_])
                            a_, s_ = R("DTp", d, hh)
                            S.mm(a_, Pn[:, q, :], XT[:, q, :], r=[Pns[q], XTs[q]], w=[s_])
                        a_, ss_ = RG("XX", d)
                        S.tt("dve", g2(Xn, d), a_, g2(Xc, d), ALU.add, r=ss_ + Xcs[qs], w=Xns[qs])
                        a_, ss_ = RG("DTp", d)
                        S.tt("dve", g2(XTn, d), a_, g2(XT, d), ALU.add, r=ss_ + XTs[qs], w=XTns[qs])
                    Pc, Pcs, PT, PTs = Pn, Pns, PTn, PTns
                    Xc, Xcs, XT, XTs = Xn, Xns, XTn, XTns
                    if pend:
                        pend.pop(0)()
                for (Cb, Cbs, CbT, CbTs, lastlvl) in ((C16, C16s, C16T, C16Ts, False), (C32, C32s, C32T, C32Ts, False), (C64, C64s, None, None, True)):
                    flip ^= 1
                    Xn, Xns = XXp[flip]
                    XTn, XTns = XTp[flip] if not lastlvl else (XTf, XTfs)
                    for (d, n, chs) in grp:
                        qs = slice(2 * d, 2 * d + 2)
                        for (q, hh, c) in chs:
                            if not lastlvl:
                                a_, s_ = R("PP", d, hh)
                                S.mm(a_, CbT[:, q, :], Xc[:, q, :], r=[CbTs[q], Xcs[q]], w=[s_])
                            a_, s_ = R("PT", d, hh)
                            S.mm(a_, Cb[:, q, :], XT[:, q, :], r=[Cbs[q], XTs[q]], w=[s_])
                        if not lastlvl:
                            a_, ss_ = RG("PP", d)
                            S.copy("act", g2(M1, d), a_, r=ss_, w=M1s[qs])
                        a_, ss_ = RG("PT", d)
                        S.copy("act", g2(M1T, d), a_, r=ss_, w=M1Ts[qs])
                    for (d, n, chs) in grp:
                        qs = slice(2 * d, 2 * d + 2)
                        for (q, hh, c) in chs:
                            if not lastlvl:
                                a_, s_ = R("XX", d, hh)
                                S.mm(a_, XT[:, q, :], M1[:, q, :], r=[XTs[q], M1s[q]], w=[s_])
                            a_, s_ = R("DTp", d, hh)
                            S.mm(a_, Xc[:, q, :], M1T[:, q, :], r=[Xcs[q], M1Ts[q]], w=[s_])
                        if not lastlvl:
                            a_, ss_ = RG("XX", d)
                            S.tt("dve", g2(Xn, d), a_, g2(Xc, d), ALU.add, r=ss_ + Xcs[qs], w=Xns[qs])
                        a_, ss_ = RG("DTp", d)
                        S.tt("dve", g2(XTn, d), a_, g2(XT, d), ALU.add, r=ss_ + XTs[qs], w=XTns[qs])
                    Xc, Xcs, XT, XTs = Xn, Xns, XTn, XTns
                    if pend:
                        pend.pop(0)()
                while pend:
                    pend.pop(0)()
                pend = make_p4(grp, AqT, AqTs, kgb, kgbs, vtk, vtks, XTf, XTfs)
            while pend:
                pend.pop(0)()
            st_ = S.sb([128, NBLK, 2, 4])
            junk = S.sb([128, 128])
            for n in range(NBLK):
                for hh in range(2):
                    S.act(junk[:, :], otok[:, n, hh, :], AF.Square, accum=st_[:, n, hh, 0:1], r=[osub[n][hh]], w=[junk, st_])
            S.ts("dve", st_[:, :, :, 1], st_[:, :, :, 0], 1.0 / 128, EPS, ALU.mult, ALU.add, r=[st_], w=[st_])
            S.act(st_[:, :, :, 2], st_[:, :, :, 1], AF.Ln, r=[st_], w=[st_])
            S.act(st_[:, :, :, 3], st_[:, :, :, 2], AF.Exp, scale=-0.5, r=[st_], w=[st_])
            zt = [S.sb([128, 512]) for _ in range(2)]
            yt = [S.sb([128, 512]) for _ in range(2)]
            yo = [S.sb([128, 512], BF16) for _ in range(2)]
            ev = 0
            for hh in range(2):
                h = 2 * hp + hh
                for (t0, n_) in TILES:
                    nb_ = n_ // 128
                    bk = banks[(ev % 2) * 2]
                    for bi in range(nb_):
                        n = t0 // 128 + bi
                        S.act(otok[:, n, hh, :], otok[:, n, hh, :], AF.Copy, scale=st_[:, n, hh, 3:4], r=[osub[n][hh], st_], w=[osub[n][hh]])
                        S.tr(bk[:, bi * 128:(bi + 1) * 128], otok[:, n, hh, :], ident, r=[osub[n][hh], cst], w=[bk])
                    z_, y_, o_ = zt[ev % 2], yt[ev % 2], yo[ev % 2]
                    ev += 1
                    S.dma("sp", z_[:, 0:n_], zsrc[(CI_DZ + h) * 128:(CI_DZ + h + 1) * 128, t0:t0 + n_], w=[z_])
                    S.act(y_[:, 0:n_], bk[:, 0:n_], AF.Copy, scale=onc[:, 0:1], r=[bk, onc], w=[y_])
                    S.act(z_[:, 0:n_], z_[:, 0:n_], AF.Silu, r=[z_], w=[z_])
                    S.tt("dve", o_[:, 0:n_], y_[:, 0:n_], z_[:, 0:n_], ALU.mult, r=[y_, z_], w=[o_])
                    S.dma("act", g.YB[h * 128:(h + 1) * 128, t0:t0 + n_], o_[:, 0:n_], r=[o_])


def cm_view(ap):
    return ap.rearrange("p (r c) -> p c r", c=64)


def stage_c(g, l, s):
    nc = g.nc
    nsteps = int(os.environ.get("CSTEPS", str(NBLK)))
    with Stage(nc) as S:
        cst, cstb = g.cst, g.cstb
        CI = lambda o: cst[:, o:o + 128]
        CB = lambda o: cstb[:, o:o + 128]
        ident, identb = CI(C_ID), CB(C_ID)
        stag = S.sb([128, T])
        qcm = S.sb([128, T])
        kcm = S.sb([128, T])
        vcm = S.sb([128, 2, T], BF16)
        rcm = S.sb([64, T], BF16)
        otok = S.sb([128, NBLK, 2, 128])
        osub = [[Sub(otok) for _ in range(2)] for _ in range(NBLK)]
        ycm = S.sb([128, T])
        wr32 = S.sb([64, 256])
        wrb = S.sb([64, 256], BF16)
        onc = S.sb([128, 1])
        S.dma("sp", onc[:, :], g.gla_onorm[l, :].rearrange("(p o) -> p o", o=1), w=[onc], allow_slow_non_contiguous=True)
        S.memset("pool", wr32[:, :], 0.0, w=[wr32])
        for d in range(2):
            S.dma("sp", wr32[32 * d:32 * d + 16, :], g.gla_w_r2[l, d, :, :], w=[wr32])
            S.dma("sp", wr32[32 * d + 16:32 * d + 17, :], g.gla_b_r[l, d:d + 1, :], w=[wr32])
        S.copy("dve", wrb[:, :], wr32[:, :], r=[wr32], w=[wrb])
        S.dma("sp", stag[0:48, :], g.P[CI_GR * 128:CI_GR * 128 + 48, :], w=[stag])
        S.memset("pool", rcm[:, :], 1.0, w=[rcm])
        for d in range(2):
            S.copy("dve", rcm[32 * d:32 * d + 16, 0:NCTX], stag[32 * d:32 * d + 16, 0:NCTX], r=[stag], w=[rcm])
            S.copy("dve", rcm[32 * d:32 * d + 16, NCTX:T].rearrange("p (c r) -> p c r", r=64), cm_view(stag[32 * d:32 * d + 16, NCTX:T]), r=[stag], w=[rcm])
        Sf = [S.sb([128, 128]) for _ in range(2)]
        Sb = [S.sb([128, 128], BF16) for _ in range(2)]
        bL = [S.ps([128, 512]) for _ in range(2)]
        bV = [S.ps([128, 1024], BF16) for _ in range(2)]
        bA = [S.ps([128, 512]) for _ in range(2)]
        bK = S.ps([128, 512])
        bKs = [Sub(bK) for _ in range(2)]

        def pd(shape, dt=F32):
            return [S.sb(shape, dt) for _ in range(2)]
        et, lt = pd([128, 128]), pd([128, 128])
        lh, lhf, ll = pd([128, 128], BF16), pd([128, 128]), pd([128, 128], BF16)
        eq, ek, ekg = pd([128, 128]), pd([128, 128]), pd([128, 128])
        qg, qgpad, kinvpad, kgpad = pd([128, 128], BF16), pd([128, 2, 128], BF16), pd([128, 2, 128], BF16), pd([128, 2, 128], BF16)
        vtok = pd([128, 2, 128], BF16)
        atm = pd([128, 2, 128], BF16)
        for d in range(2):
            S.memset("pool", qgpad[d][:, :, :], 0.0, w=[qgpad[d]])
            S.memset("pool", kinvpad[d][:, :, :], 0.0, w=[kinvpad[d]])
            S.memset("pool", kgpad[d][:, :, :], 0.0, w=[kgpad[d]])
        ISC = 1.0 / 16.0

        for hp in range(2):
            for (dst, ci) in ((qcm, CI_GQ + hp), (kcm, CI_GK + hp)):
                S.dma("sp", stag[:, :], g.P[ci * 128:(ci + 1) * 128, :], w=[stag])
                S.copy("dve", dst[:, 0:NCTX], stag[:, 0:NCTX], r=[stag], w=[dst])
                S.copy("dve", dst[:, NCTX:T].rearrange("p (c r) -> p c r", r=64), cm_view(stag[:, NCTX:T]), r=[stag], w=[dst])
            for hh in range(2):
                ci = CI_GV + 2 * hp + hh
                S.dma("sp", stag[:, :], g.P[ci * 128:(ci + 1) * 128, :], w=[stag])
                S.copy("pool", vcm[:, hh, 0:NCTX], stag[:, 0:NCTX], r=[stag], w=[vcm])
                S.copy("pool", vcm[:, hh, NCTX:T].rearrange("p (c r) -> p c r", r=64), cm_view(stag[:, NCTX:T]), r=[stag], w=[vcm])
            for d in range(2):
                S.memset("pool", Sf[d][:, :], 0.0, w=[Sf[d]])
                S.memset("pool", Sb[d][:, :], 0.0, w=[Sb[d]])
            if nsteps < NBLK:
                S.memset("pool", otok[:, :, :, :], 0.0, w=[osub[n_][h_] for n_ in range(NBLK) for h_ in range(2)])
            visited = set()
            for i in range(nsteps):
                blks = [(0, ORD_F[i]), (1, ORD_B[i])]
                for (d, n) in blks:
                    bs = slice(n * 128, (n + 1) * 128)
                    S.mm(bL[d][:, 0:128], rcm[32 * d:32 * d + 17, bs], wrb[32 * d:32 * d + 17, hp * 128:(hp + 1) * 128], r=[rcm, wrb], w=[bL[d]])
                    S.tr(bL[d][:, 128:256], kcm[:, bs], ident, r=[kcm, cst], w=[bL[d]])
                    for hh in range(2):
                        S.tr(bV[d][:, hh * 128:(hh + 1) * 128], vcm[:, hh, bs], identb, r=[vcm, cstb], w=[bV[d]])
                for (d, n) in blks:
                    S.act(et[d][:, :], bL[d][:, 0:128], AF.Exp, scale=-1.0, r=[bL[d]], w=[et[d]])
                    S.act(lt[d][:, :], et[d][:, :], AF.Ln, bias=1.0, r=[et[d]], w=[lt[d]])
                    S.copy("dve", lh[d][:, :], lt[d][:, :], r=[lt[d]], w=[lh[d]])
                    S.copy("dve", lhf[d][:, :], lh[d][:, :], r=[lh[d]], w=[lhf[d]])
                    S.tt("dve", ll[d][:, :], lt[d][:, :], lhf[d][:, :], ALU.subtract, r=[lt[d], lhf[d]], w=[ll[d]])
                    S.copy("act", vtok[d][:, :, :].rearrange("p a b -> p (a b)"), bV[d][:, 0:256], r=[bV[d]], w=[vtok[d]])
                for (d, n) in blks:
                    tri = CB(C_TRIU) if d == 0 else CB(C_TRIL)
                    tris = CB(C_TRILS) if d == 0 else CB(C_TRIUS)
                    S.mm(bL[d][:, 384:512], lh[d][:, :], tri, start=True, stop=False, r=[lh[d], cstb], w=[bL[d]])
                    S.mm(bL[d][:, 384:512], ll[d][:, :], tri, start=False, stop=True, r=[ll[d], cstb], w=[bL[d]])
                    S.mm(bL[d][:, 256:384], tris, lh[d][:, :], start=True, stop=False, r=[lh[d], cstb], w=[bL[d]])
                    S.mm(bL[d][:, 256:384], tris, ll[d][:, :], start=False, stop=True, r=[ll[d], cstb], w=[bL[d]])
                for (d, n) in blks:
                    bs = slice(n * 128, (n + 1) * 128)
                    S.act(eq[d][:, :], bL[d][:, 384:512], AF.Exp, scale=-ISC, r=[bL[d]], w=[eq[d]])
                    S.act(ek[d][:, :], bL[d][:, 384:512], AF.Exp, scale=ISC, r=[bL[d]], w=[ek[d]])
                    S.act(ekg[d][:, :], bL[d][:, 256:384], AF.Exp, scale=-ISC, r=[bL[d]], w=[ekg[d]])
                    S.stt("dve", qg[d][:, :], qcm[:, bs], 0.125, eq[d][:, :], ALU.mult, ALU.mult, r=[qcm, eq[d]], w=[qg[d]])
                    for hh in range(2):
                        ps_ = slice(64 * hh, 64 * hh + 64)
                        S.copy("pool", qgpad[d][ps_, hh, :], qg[d][ps_, :], r=[qg[d]], w=[qgpad[d]])
                        S.tt("pool", kinvpad[d][ps_, hh, :], kcm[ps_, bs], ek[d][ps_, :], ALU.mult, r=[kcm, ek[d]], w=[kinvpad[d]])
                        S.tt("dve", kgpad[d][:, hh, 64 * hh:64 * hh + 64], bL[d][:, 128 + 64 * hh:128 + 64 * hh + 64], ekg[d][:, 64 * hh:64 * hh + 64], ALU.mult,
                             r=[bL[d], ekg[d]], w=[kgpad[d]])
                for (d, n) in blks:
                    for hh in range(2):
                        S.mm(bA[d][:, hh * 128:(hh + 1) * 128], kinvpad[d][:, hh, :], qg[d][:, :], r=[kinvpad[d], qg[d]], w=[bA[d]])
                    msk = CI(C_TRIU) if d == 0 else CI(C_TRIL)
                    for hh in range(2):
                        S.tt("dve", atm[d][:, hh, :], bA[d][:, hh * 128:(hh + 1) * 128], msk, ALU.mult, r=[bA[d], cst], w=[atm[d]])
                    for hh in range(2):
                        S.mm(bA[d][:, 256 + hh * 128:256 + (hh + 1) * 128], qgpad[d][:, hh, :], Sb[d][:, :], start=True, stop=False, r=[qgpad[d], Sb[d]], w=[bA[d]])
                        S.mm(bA[d][:, 256 + hh * 128:256 + (hh + 1) * 128], atm[d][:, hh, :], vtok[d][:, hh, :], start=False, stop=True, r=[atm[d], vtok[d]], w=[bA[d]])
                    for hh in range(2):
                        S.mm(bK[:, d * 128:(d + 1) * 128], kgpad[d][:, hh, :], vtok[d][:, hh, :], start=(hh == 0), stop=(hh == 1), r=[kgpad[d], vtok[d]], w=[bKs[d]])
                    for hh in range(2):
                        od = otok[:, n, hh, :]
                        ob = osub[n][hh]
                        src = bA[d][:, 256 + hh * 128:256 + (hh + 1) * 128]
                        if (n, hh) not in visited:
                            visited.add((n, hh))
                            S.copy("act", od, src, r=[bA[d]], w=[ob])
                        else:
                            S.tt("dve", od, src, od, ALU.add, r=[bA[d], ob], w=[ob])
                    gl = eq[d][:, 127:128] if d == 0 else eq[d][:, 0:1]
                    S.stt("dve", Sf[d][:, :], Sf[d][:, :], gl, bK[:, d * 128:(d + 1) * 128], ALU.mult, ALU.add, r=[Sf[d], eq[d], bKs[d]], w=[Sf[d]])
                    S.copy("pool", Sb[d][:, :], Sf[d][:, :], r=[Sf[d]], w=[Sb[d]])
            st_ = S.sb([128, NBLK, 2, 4])
            junk = S.sb([128, 128])
            for n in range(NBLK):
                for hh in range(2):
                    S.act(junk[:, :], otok[:, n, hh, :], AF.Square, accum=st_[:, n, hh, 0:1], r=[osub[n][hh]], w=[junk, st_])
            S.ts("dve", st_[:, :, :, 1], st_[:, :, :, 0], 1.0 / 128, EPS, ALU.mult, ALU.add, r=[st_], w=[st_])
            S.act(st_[:, :, :, 2], st_[:, :, :, 1], AF.Ln, r=[st_], w=[st_])
            S.act(st_[:, :, :, 3], st_[:, :, :, 2], AF.Exp, scale=-0.5, r=[st_], w=[st_])
            yo = S.sb([128, T], BF16)
            for hh in range(2):
                h = 2 * hp + hh
                for g4 in range(0, NBLK, 4):
                    nb_ = min(4, NBLK - g4)
                    bk = bL[(g4 // 4) % 2]
                    for bi in range(nb_):
                        n = g4 + bi
                        S.act(otok[:, n, hh, :], otok[:, n, hh, :], AF.Copy, scale=st_[:, n, hh, 3:4], r=[osub[n][hh], st_], w=[osub[n][hh]])
                        S.tr(bk[:, bi * 128:(bi + 1) * 128], otok[:, n, hh, :], ident, r=[osub[n][hh], cst], w=[bk])
                    S.act(ycm[:, g4 * 128:(g4 + nb_) * 128], bk[:, 0:nb_ * 128], AF.Copy, scale=onc[:, 0:1], r=[bk, onc], w=[ycm])
                S.dma("sp", stag[:, :], g.P[(CI_GZ + h) * 128:(CI_GZ + h + 1) * 128, :], w=[stag])
                S.act(stag[:, :], stag[:, :], AF.Silu, r=[stag], w=[stag])
                S.tt("dve", yo[:, 0:NCTX], ycm[:, 0:NCTX], stag[:, 0:NCTX], ALU.mult, r=[ycm, stag], w=[yo])
                S.tt("dve", yo[:, NCTX:T].rearrange("p (r c) -> p r c", c=64), ycm[:, NCTX:T].rearrange("p (c r) -> p r c", r=64),
                     stag[:, NCTX:T].rearrange("p (r c) -> p r c", c=64), ALU.mult, r=[ycm, stag], w=[yo])
                S.dma("act", g.YC[h * 128:(h + 1) * 128, :], yo[:, :], r=[yo])


def build_nc(nlayer=4, nseq=2, debug=None, dbg_what=None):
    nc = bass.Bass("TRN2", target_bir_lowering=False)
    g = Ctx()
    g.nc = nc
    dt = lambda name, shape, kind="ExternalInput", dtype=F32: nc.dram_tensor(name, list(shape), dtype, kind=kind).ap()
    g.x_in = dt("x", [2, NLAT, D])
    g.ctx_in = dt("ctx", [2, NCTX, D])
    g.c_in = dt("c", [2, D])
    g.cctx_in = dt("c_ctx", [1, D])
    g.w_mod = dt("w_mod", [4, D, 3 * D])
    g.b_mod = dt("b_mod", [4, 3 * D])
    g.norm_w = dt("norm_w", [4, D])
    g.w_in = dt("w_in", [4, D, INW])
    g.b_merge = dt("b_merge", [4, 3 * D])
    g.conv_a = dt("conv_a", [4, 3, 512])
    g.conv_dn = dt("conv_dn", [4, 3, 1536])
    g.dn_a_log = dt("dn_a_log", [4, 8])
    g.dn_dt_bias = dt("dn_dt_bias", [4, 8])
    g.dn_onorm = dt("dn_onorm", [4, 128])
    g.gla_w_r2 = dt("gla_w_r2", [4, 2, 16, 256])
    g.gla_b_r = dt("gla_b_r", [4, 2, 256])
    g.gla_onorm = dt("gla_onorm", [4, 128])
    g.w_pa = dt("w_pa", [4, 512, D])
    g.w_pb = dt("w_pb", [4, 512, D])
    g.w_pc = dt("w_pc", [4, 512, D])
    g.w_o = dt("w_o", [4, D, D])
    g.fnw = dt("final_norm_w", [1, D])
    g.cst_in = dt("cst", [128, NCST])
    g.out = dt("out", [2, NLAT, D], kind="ExternalOutput")
    import os
    g.P = nc.dram_tensor("P_scr", [(1 if os.environ.get("SMALLP") else NCH) * 128, T], F32).ap()
    g.GB = nc.dram_tensor("GB_scr", [T, 16], F32).ap()
    g.xrl = nc.dram_tensor("xrl_scr", [2, NLAT, D], F32).ap()
    g.xrc = nc.dram_tensor("xrc_scr", [2, NCTX, D], F32).ap()
    g.YA = nc.dram_tensor("YA_scr", [512, T], BF16).ap()
    g.YB = nc.dram_tensor("YB_scr", [512, T], BF16).ap()
    g.YC = nc.dram_tensor("YC_scr", [512, T], BF16).ap()
    g.QT = nc.dram_tensor("QT_scr", [512, T], BF16).ap()
    g.KT = nc.dram_tensor("KT_scr", [512, T], BF16).ap()
    g.VT = nc.dram_tensor("VT_scr", [512, T], BF16).ap()
    if debug is not None:
        g.dbg = dt("dbg", debug, kind="ExternalOutput", dtype=(BF16 if ((dbg_what in ("a", "bprep", "b", "c") or os.environ.get("DBGBF16")) and not os.environ.get("DBGF32")) else F32))

    with contextlib.ExitStack() as top:
        def psb(name, shape, dtype=F32):
            return TL(top.enter_context(nc.sbuf_tensor(name, list(shape), dtype)))
        g.cst = psb("cst_sb", [128, NCST])
        g.modcol = psb("modcol", [128, 24, 3])
        g.A1 = psb("A1", [128, 8, 3])
        g.Gb = psb("Gb", [128, 3, D])
        g.epsc = psb("epsc", [128, 2])
        g.cstb = psb("cstb_sb", [128, NCST], BF16)
        with Stage(nc) as S:
            S.dma("sp", g.cst[:, :], g.cst_in, w=[g.cst])
            S.memset("pool", g.epsc[:, :], EPS, w=[g.epsc])
            S.copy("dve", g.cstb[:, :], g.cst[:, :], r=[g.cst], w=[g.cstb])
        def dump2d(src, rows, dtype_rows=128):
            with Stage(nc) as S:
                for r0 in range(0, rows, 128):
                    S.dma("sp", g.dbg[r0:r0 + 128, :], src[r0:r0 + 128, :])
        for l in range(nlayer):
            stage_mod(g, l)
            if dbg_what == "mod":
                with Stage(nc) as S:
                    S.dma("sp", g.dbg[:, 0:72], g.modcol[:, :, :].rearrange("p a b -> p (a b)"), r=[g.modcol])
                    S.dma("sp", g.dbg[:, 72:96], g.A1[:, :, :].rearrange("p a b -> p (a b)"), r=[g.A1])
                    S.dma("sp", g.dbg[:, 96:96 + 3072], g.Gb[:, :, :].rearrange("p a b -> p (a b)"), r=[g.Gb])
                return nc
            for s in range(nseq):
                stage_proj(g, l, s)
                if dbg_what == "proj":
                    dump2d(g.P, NCH * 128)
                    return nc
                stage_a(g, l, s)
                if dbg_what == "hook":
                    return nc
                if dbg_what == "proj2":
                    dump2d(g.P, NCH * 128)
                    return nc
                if dbg_what == "a":
                    dump2d(g.YA, 512)
                    return nc
                stage_bprep(g, l, s)
                if dbg_what == "bprep":
                    dump2d(g.QT, 512)
                    with Stage(nc) as S:
                        for r0 in range(0, 512, 128):
                            S.dma("sp", g.dbg[512 + r0:512 + r0 + 128, :], g.KT[r0:r0 + 128, :])
                            S.dma("sp", g.dbg[1024 + r0:1024 + r0 + 128, :], g.VT[r0:r0 + 128, :])
                    return nc
                stage_bscan(g, l, s)
                if os.environ.get("BDBG"):
                    return nc
                if dbg_what == "b":
                    dump2d(g.YB, 512)
                    return nc
                stage_c(g, l, s)
                if dbg_what == "c":
                    dump2d(g.YC, 512)
                    return nc
                stage_merge(g, l, s, last=(l == nlayer - 1) and not os.environ.get("MERGE_NOTLAST"))
                if dbg_what == "merge":
                    with Stage(nc) as S:
                        for r0 in range(0, NLAT, 128):
                            S.dma("sp", g.dbg[r0:r0 + 128, :], g.xrl[s, r0:r0 + 128, :])
                        for r0 in range(0, NCTX, 128):
                            S.dma("sp", g.dbg[NLAT + r0:NLAT + r0 + 128, :], g.xrc[s, r0:r0 + 128, :])
                    return nc
    return nc


_NC_CACHE = {}


def kernel(**inputs):
    n = 8
    if "nc" not in _NC_CACHE:
        _NC_CACHE["nc"] = build_nc()
    nc = _NC_CACHE["nc"]
    cst = make_consts()
    f = lambda k: np.ascontiguousarray(np.asarray(inputs[k], dtype=np.float32))
    shared = {
        "c_ctx": f("c_ctx").reshape(1, D), "w_mod": f("w_mod"), "b_mod": f("b_mod"), "norm_w": f("norm_w"),
        "w_in": f("w_in"), "b_merge": f("b_merge"), "conv_a": f("conv_a"), "conv_dn": f("conv_dn"),
        "dn_a_log": f("dn_a_log").reshape(4, 8), "dn_dt_bias": f("dn_dt_bias").reshape(4, 8),
        "dn_onorm": f("dn_onorm"), "gla_w_r2": f("gla_w_r2"), "gla_b_r": f("gla_b_r"), "gla_onorm": f("gla_onorm"),
        "w_pa": f("w_pa"), "w_pb": f("w_pb"), "w_pc": f("w_pc"), "w_o": f("w_o"),
        "final_norm_w": f("final_norm_w").reshape(1, D), "cst": cst,
    }
    x = f("x")
    c = f("c")
    ctx = f("ctx")
    in_maps = []
    for i in range(n):
        m = dict(shared)
        m["x"] = x[2 * i:2 * i + 2]
        m["c"] = c[2 * i:2 * i + 2]
        m["ctx"] = ctx[2 * i:2 * i + 2]
        in_maps.append(m)
    res = run_bass_kernel_spmd(nc, in_maps, core_ids=list(range(n)))
    return np.concatenate([r["out"] for r in res.results], axis=0)
```

```python
import contextlib
import os
import numpy as np
import concourse.bass as bass
import concourse.mybir as mybir
from concourse.bass_utils import run_bass_kernel_spmd

F32 = mybir.dt.float32
BF16 = mybir.dt.bfloat16
ALU = mybir.AluOpType
AF = mybir.ActivationFunctionType

ENGS = ("pe", "act", "dve", "pool", "sp")
N_DMA_SLOTS = 40

D = 1024
T = 4352
NBLK = 34
NCTX = 256
NLAT = 4096
EPS = 1e-6
NCH = 69


class Buf:
    __slots__ = ("w", "r", "pid")

    def __init__(self):
        self.w = None
        self.r = {}
        self.pid = None


class TL:
    def __init__(self, t):
        self.t = t
        self.b = Buf()
        self.bank = None

    def __getitem__(self, idx):
        return self.t[idx]


def _banks(xs):
    out = []
    for x in xs:
        bk = getattr(x, "bank", None)
        if bk is not None and bk not in out:
            out.append(bk)
    return out


def _bufs(xs):
    out = []
    for x in xs:
        if x is None:
            continue
        out.append(x.b if hasattr(x, "b") else x)
    return out


class Prog:
    def __init__(self, nc):
        self.nc = nc
        self.streams = {e: [] for e in ENGS}
        self.count = {e: 0 for e in ENGS}
        self.known = {e: {} for e in ENGS}
        self.slot_val = [0] * N_DMA_SLOTS
        self.next_slot = 0

    def _wait(self, eng, tok, same_ok=False):
        if tok is None:
            return
        key, val = tok
        if key == eng and not same_ok:
            return
        if self.known[eng].get(key, 0) >= val:
            return
        self.known[eng][key] = val
        self.streams[eng].append(("wait", key, val))

    def _deps(self, eng, reads, writes):
        for b in list(reads) + list(writes):
            if b.pid is not self:
                b.pid = self
                b.w = None
                b.r = {}
        so = (eng != "pe")
        for b in reads:
            self._wait(eng, b.w, same_ok=so)
        for b in writes:
            self._wait(eng, b.w, same_ok=so)
            for k, v in b.r.items():
                self._wait(eng, (k, v), same_ok=so)

    def _mark(self, tok, reads, writes):
        k, v = tok
        for b in reads:
            if b.r.get(k, 0) < v:
                b.r[k] = v
        for b in writes:
            b.w = tok
            b.r = {}

    def op(self, eng, fn, reads=(), writes=()):
        banks = _banks(list(reads) + list(writes))
        reads = _bufs(reads)
        writes = _bufs(writes)
        self._deps(eng, reads, writes)
        for bk in banks:
            if bk.pid is not self:
                bk.pid = self
                bk.w = None
            self._wait(eng, bk.w)
        self.count[eng] += 1
        self.streams[eng].append(("inst", fn))
        self._mark((eng, self.count[eng]), reads, writes)
        for bk in banks:
            bk.w = (eng, self.count[eng])

    def dma(self, q, out, in_, reads=(), writes=(), **kw):
        reads = _bufs(reads)
        writes = _bufs(writes)
        self._deps(q, reads, writes)
        s = self.next_slot
        self.next_slot = (s + 1) % N_DMA_SLOTS
        key = "d%d" % s
        if self.slot_val[s] > 0:
            self._wait(q, (key, self.slot_val[s]))
        self.slot_val[s] += 16
        self.streams[q].append(("dma", out, in_, key, kw))
        self._mark((key, self.slot_val[s]), reads, writes)

    def check(self):
        sem = {}
        pos = {e: 0 for e in ENGS}
        progress = True
        while progress:
            progress = False
            for e in ENGS:
                st = self.streams[e]
                while pos[e] < len(st):
                    it = st[pos[e]]
                    if it[0] == "wait":
                        if sem.get(it[1], 0) < it[2]:
                            break
                    elif it[0] == "inst":
                        sem[e] = sem.get(e, 0) + 1
                    else:
                        sem[it[3]] = sem.get(it[3], 0) + 16
                    pos[e] += 1
                    progress = True
        for e in ENGS:
            if pos[e] < len(self.streams[e]):
                raise RuntimeError("sync deadlock: %s stuck at %d/%d on %r (sem=%r)" % (
                    e, pos[e], len(self.streams[e]), self.streams[e][pos[e]], sem.get(self.streams[e][pos[e]][1])))

    def emit(self):
        nc = self.nc
        for s in range(N_DMA_SLOTS):
            if self.slot_val[s] > 0:
                self._wait("sp", ("d%d" % s, self.slot_val[s]))
        for e in ENGS:
            if e != "sp" and self.count[e] > 0:
                self._wait("sp", (e, self.count[e]))
        self.count["sp"] += 1
        self.streams["sp"].append(("inst", lambda e: e.nop()))
        for e in ENGS:
            if e != "sp":
                self.streams[e].append(("wait", "sp", self.count["sp"]))
        self.check()
        with contextlib.ExitStack() as st:
            sems = {}
            for e in ENGS:
                sems[e] = st.enter_context(nc.semaphore("s_" + e))
            for s in range(N_DMA_SLOTS):
                if self.slot_val[s] > 0:
                    sems["d%d" % s] = st.enter_context(nc.semaphore("s_d%d" % s))
            nc.all_engine_barrier()
            for sm in sems.values():
                nc.gpsimd.sem_clear(sm)
            nc.all_engine_barrier()
            block = st.enter_context(nc.Block())

            def run(engname):
                stream = self.streams[engname]

                def body(eng):
                    for it in stream:
                        if it[0] == "wait":
                            eng.wait_ge(sems[it[1]], it[2])
                        elif it[0] == "inst":
                            it[1](eng).then_inc(sems[engname], 1)
                        else:
                            _, out, in_, key, kw = it
                            eng.dma_start(out=out, in_=in_, **kw).then_inc(sems[key], 16)
                return body

            block.tensor(run("pe"))
            block.scalar(run("act"))
            block.vector(run("dve"))
            block.gpsimd(run("pool"))
            block.sync(run("sp"))


_UID = [0]


class Stage:
    def __init__(self, nc):
        self.nc = nc
        self.st = contextlib.ExitStack()
        self.P = Prog(nc)
        self.n = 0

    def __enter__(self):
        self.st.__enter__()
        return self

    def __exit__(self, *a):
        if a[0] is None:
            self.P.emit()
        return self.st.__exit__(*a)

    def sb(self, shape, dt=F32):
        _UID[0] += 1
        return TL(self.st.enter_context(self.nc.sbuf_tensor("sb_%d" % _UID[0], list(shape), dt)))

    def ps(self, shape=(128, 512), dt=F32):
        _UID[0] += 1
        t = TL(self.st.enter_context(self.nc.psum_tensor("ps_%d" % _UID[0], list(shape), dt)))
        t.bank = Buf()
        return t

    def mm(self, out, lhsT, rhs, start=True, stop=True, r=(), w=()):
        self.P.op("pe", lambda e: e.matmul(out, lhsT=lhsT, rhs=rhs, start=start, stop=stop), r, w)

    def tr(self, out, in_, ident, r=(), w=()):
        self.P.op("pe", lambda e: e.transpose(out, in_, ident), r, w)

    def act(self, out, in_, func, bias=None, scale=None, accum=None, r=(), w=()):
        kw = {}
        if bias is not None:
            kw["bias"] = bias
        if scale is not None:
            kw["scale"] = scale
        if accum is not None:
            kw["accum_out"] = accum
        self.P.op("act", lambda e: e.activation(out=out, in_=in_, func=func, **kw), r, w)

    def ts(self, eng, out, in0, s1, s2, op0, op1=None, r=(), w=()):
        if op1 is None:
            self.P.op(eng, lambda e: e.tensor_scalar(out=out, in0=in0, scalar1=s1, scalar2=0.0, op0=op0, op1=ALU.add), r, w)
        else:
            self.P.op(eng, lambda e: e.tensor_scalar(out=out, in0=in0, scalar1=s1, scalar2=s2, op0=op0, op1=op1), r, w)

    def stt(self, eng, out, in0, scalar, in1, op0, op1, r=(), w=()):
        self.P.op(eng, lambda e: e.scalar_tensor_tensor(out=out, in0=in0, scalar=scalar, in1=in1, op0=op0, op1=op1), r, w)

    def tt(self, eng, out, in0, in1, op, r=(), w=()):
        self.P.op(eng, lambda e: e.tensor_tensor(out=out, in0=in0, in1=in1, op=op), r, w)

    def copy(self, eng, out, in_, r=(), w=()):
        if eng == "act":
            self.P.op("act", lambda e: e.copy(out=out, in_=in_), r, w)
        else:
            self.P.op(eng, lambda e: e.tensor_copy(out=out, in_=in_), r, w)

    def memset(self, eng, ap, val, w=()):
        self.P.op(eng, lambda e: e.memset(ap, val), (), w)

    def dma(self, q, out, in_, r=(), w=(), **kw):
        self.P.dma(q, out, in_, r, w, **kw)


C_ID, C_TRIU, C_TRIL, C_TRIUS, C_TRILS, C_ONES, C_NEGF, C_NEGB, C_SEL = [i * 128 for i in range(9)]
C_MD16, C_MS16, C_MS32, C_MS64 = [11 * 128 + i * 512 for i in range(4)]
NCST = 11 * 128 + 4 * 512


def make_consts():
    c = np.zeros((128, NCST), np.float32)
    t = np.arange(128)[:, None]
    i = np.arange(128)[None, :]
    c[:, C_ID:C_ID + 128] = (t == i)
    c[:, C_TRIU:C_TRIU + 128] = (t <= i)
    c[:, C_TRIL:C_TRIL + 128] = (t >= i)
    c[:, C_TRIUS:C_TRIUS + 128] = (t < i)
    c[:, C_TRILS:C_TRILS + 128] = (t > i)
    c[:, C_ONES:C_ONES + 128] = 1.0
    c[:, C_NEGF:C_NEGF + 128] = np.where(t <= i, 0.0, -1e5)
    c[:, C_NEGB:C_NEGB + 128] = np.where(t >= i, 0.0, -1e5)
    for s in range(3):
        c[s, C_SEL + s * 128:C_SEL + (s + 1) * 128] = 1.0
    md16 = (t // 16 == i // 16)
    for off, b in ((C_MS16, 16), (C_MS32, 32), (C_MS64, 64)):
        m = (t // (2 * b) == i // (2 * b)) & (t // b != i // b)
        for q in range(4):
            c[:, off + q * 128:off + (q + 1) * 128] = m
    for q in range(4):
        c[:, C_MD16 + q * 128:C_MD16 + (q + 1) * 128] = md16
    return c


O_AB, O_AC, O_AX, O_AZ = 0, 512, 1024, 1536
O_Q, O_K, O_V, O_DZ = 2048, 2560, 3072, 3584
O_DA, O_DB = 4096, 4104
O_GQ, O_GK, O_GV, O_GZ, O_GR, O_MG = 4112, 4368, 4624, 5136, 5648, 5680
INW = 8752


def chunk_list():
    ch = []
    for c0 in range(0, 4096, 128):
        ch.append((c0, 128))
    for c0 in range(O_GQ, O_GR, 128):
        ch.append((c0, 128))
    ch.append((O_GR, 32))
    for c0 in range(O_MG, INW, 128):
        ch.append((c0, 128))
    assert len(ch) == NCH
    return ch


CH = chunk_list()
CI_AB, CI_AC, CI_AX, CI_AZ = 0, 4, 8, 12
CI_Q, CI_K, CI_V, CI_DZ = 16, 20, 24, 28
CI_GQ, CI_GK, CI_GV, CI_GZ, CI_GR, CI_MG = 32, 34, 36, 40, 44, 45


class Ctx:
    pass


def tok_src(g, l, s, n):
    if n < 2:
        base = g.ctx_in if l == 0 else g.xrc
        return base[s, n * 128:(n + 1) * 128, :]
    base = g.x_in if l == 0 else g.xrl
    return base[s, (n - 2) * 128:(n - 1) * 128, :]


def stage_mod(g, l):
    nc = g.nc
    with Stage(nc) as S:
        c3T = S.sb([128, 8, 3])
        sc3T = S.sb([128, 8, 3])
        bcol = S.sb([128, 24])
        ncol = S.sb([128, 8])
        brow = S.sb([3, 1024])
        grow = S.sb([3, 1024])
        wm = [S.sb([128, 8, 512]) for _ in range(2)]
        pm = S.ps([128, 512])
        pr = S.ps([128, 512])
        pg = [S.ps([128, 512]) for _ in range(2)]
        for s in range(2):
            S.dma("sp", c3T[:, :, s], g.c_in[s, :].rearrange("(k p) -> p k", p=128), w=[c3T], allow_slow_non_contiguous=True)
        S.dma("sp", c3T[:, :, 2], g.cctx_in[0, :].rearrange("(k p) -> p k", p=128), w=[c3T], allow_slow_non_contiguous=True)
        S.dma("sp", bcol[:, :], g.b_mod[l, :].rearrange("(c p) -> p c", p=128), w=[bcol], allow_slow_non_contiguous=True)
        S.dma("sp", ncol[:, :], g.norm_w[l, :].rearrange("(c p) -> p c", p=128), w=[ncol], allow_slow_non_contiguous=True)
        S.dma("sp", brow[:, :], g.b_mod[l:l + 1, 2048:3072].partition_broadcast(3), w=[brow])
        S.act(sc3T[:, :, :], c3T[:, :, :], AF.Silu, r=[c3T], w=[sc3T])
        import os
        lvl = int(os.environ.get("MODLVL", "99"))
        wv = g.w_mod[l].rearrange("(k p) n -> p k n", p=128)
        for cg in range(6 if lvl >= 2 else 0):
            wt = wm[cg % 2]
            S.dma("sp", wt[:, :, :], wv[:, :, cg * 512:(cg + 1) * 512], w=[wt])
            for j in range(4):
                chn = cg * 4 + j
                for k in range(8):
                    S.mm(pm[:, j * 4:j * 4 + 3], wt[:, k, j * 128:(j + 1) * 128], sc3T[:, k, :], start=(k == 0), stop=(k == 7), r=[wt, sc3T], w=[pm])
                S.ts("dve", g.modcol[:, chn, :], pm[:, j * 4:j * 4 + 3], bcol[:, chn:chn + 1], None, ALU.add, r=[pm, bcol], w=[g.modcol])
            if cg >= 4 and lvl >= 3:
                for k in range(8):
                    S.mm(pr[0:3, :], sc3T[:, k, :], wt[:, k, :], start=(k == 0), stop=(k == 7), r=[wt, sc3T], w=[pr])
                S.tt("dve", grow[:, (cg - 4) * 512:(cg - 3) * 512], pr[0:3, :], brow[:, (cg - 4) * 512:(cg - 3) * 512], ALU.add, r=[pr, brow], w=[grow])
        for s in range(3 if lvl >= 4 else 0):
            for cg in range(2):
                p = pg[cg]
                S.mm(p[:, :], g.cst[0:3, C_SEL + s * 128:C_SEL + (s + 1) * 128], grow[0:3, cg * 512:(cg + 1) * 512], r=[grow, g.cst], w=[p])
                S.copy("act", g.Gb[:, s, cg * 512:(cg + 1) * 512], p[:, :], r=[p], w=[g.Gb])
        for k in range(8 if lvl >= 5 else 0):
            S.ts("dve", g.A1[:, k, :], g.modcol[:, 8 + k, :], 1.0, ncol[:, k:k + 1], ALU.add, ALU.mult, r=[g.modcol, ncol], w=[g.A1])


def stage_proj(g, l, s):
    nc = g.nc
    with Stage(nc) as S:
        hT = S.sb([128, 8, T], BF16)
        xb = [S.sb([128, D]) for _ in range(2)]
        xn = [S.sb([128, D]) for _ in range(2)]
        junk = S.sb([128, D])
        stat = S.sb([128, NBLK, 4])
        wst = [S.sb([128, 8, 512]) for _ in range(2)]
        wb = [S.sb([128, 8, 512], BF16) for _ in range(2)]
        stg = [S.sb([128, 512]) for _ in range(4)]
        wab32 = S.sb([128, 8, 16])
        wab = S.sb([128, 8, 16], BF16)
        gbst = S.sb([128, NBLK, 16])
        ptr = [S.ps([128, 512]) for _ in range(2)]
        pp = [S.ps([128, 512]) for _ in range(4)]
        pab = S.ps([128, 512])
        ident = g.cst[:, C_ID:C_ID + 128]
        for n in range(NBLK):
            sidx = 2 if n < 2 else s
            x_t = xb[n % 2]
            xn_t = xn[n % 2]
            S.dma("sp", x_t[:, :], tok_src(g, l, s, n), w=[x_t])
            S.act(junk[:, :], x_t[:, :], AF.Square, accum=stat[:, n, 0:1], r=[x_t], w=[junk, stat])
            S.ts("dve", stat[:, n, 1:2], stat[:, n, 0:1], 1.0 / D, EPS, ALU.mult, ALU.add, r=[stat], w=[stat])
            S.act(stat[:, n, 2:3], stat[:, n, 1:2], AF.Ln, r=[stat], w=[stat])
            S.act(stat[:, n, 3:4], stat[:, n, 2:3], AF.Exp, scale=-0.5, r=[stat], w=[stat])
            S.ts("dve", xn_t[:, :], x_t[:, :], stat[:, n, 3:4], None, ALU.mult, r=[x_t, stat], w=[xn_t])
            for k in range(8):
                p = ptr[k // 4]
                S.tr(p[:, (k % 4) * 128:(k % 4 + 1) * 128], xn_t[:, k * 128:(k + 1) * 128], ident, r=[xn_t, g.cst], w=[p])
            for k in range(8):
                p = ptr[k // 4]
                src = p[:, (k % 4) * 128:(k % 4 + 1) * 128]
                dst = hT[:, k, n * 128:(n + 1) * 128]
                if k // 4 == 0:
                    S.ts("dve", dst, src, g.A1[:, k, sidx:sidx + 1], g.modcol[:, k, sidx:sidx + 1], ALU.mult, ALU.add, r=[p, g.A1, g.modcol], w=[hT])
                else:
                    S.act(dst, src, AF.Identity, bias=g.modcol[:, k, sidx:sidx + 1], scale=g.A1[:, k, sidx:sidx + 1], r=[p, g.A1, g.modcol], w=[hT])
        wv = g.w_in[l].rearrange("(k p) n -> p k n", p=128)
        S.dma("sp", wab32[:, :, :], wv[:, :, O_DA:O_DA + 16], w=[wab32])
        S.copy("dve", wab[:, :, :], wab32[:, :, :], r=[wab32], w=[wab])
        for n in range(NBLK):
            for k in range(8):
                S.mm(pab[:, (n % 8) * 16:(n % 8 + 1) * 16], hT[:, k, n * 128:(n + 1) * 128], wab[:, k, :], start=(k == 0), stop=(k == 7), r=[hT, wab], w=[pab])
            S.copy("dve", gbst[:, n, :], pab[:, (n % 8) * 16:(n % 8 + 1) * 16], r=[pab], w=[gbst])
        S.dma("act", g.GB[:, :].rearrange("(n p) c -> p n c", p=128), gbst[:, :, :], r=[gbst])
        tiles = [(0, NCTX)] + [(NCTX + i * 512, 512) for i in range(8)]
        ngrp = (NCH + 3) // 4
        ev = 0
        gst = [S.sb([128, 512], BF16) for _ in range(4)]
        bmc = S.sb([128, 24])
        S.dma("sp", bmc[:, :], g.b_merge[l, :].rearrange("(c p) -> p c", p=128), w=[bmc], allow_slow_non_contiguous=True)

        def load_w(gi):
            chs = list(range(gi * 4, min(NCH, gi * 4 + 4)))
            ws = wst[gi % 2]
            wbt = wb[gi % 2]
            for j, c in enumerate(chs):
                c0, ncol_ = CH[c]
                if c == CI_GR:
                    S.memset("pool", ws[:, :, j * 128:(j + 1) * 128], 0.0, w=[ws])
                    S.dma("sp", ws[:, :, j * 128:j * 128 + 16], wv[:, :, c0:c0 + 16], w=[ws])
                    S.dma("sp", ws[:, :, j * 128 + 32:j * 128 + 48], wv[:, :, c0 + 16:c0 + 32], w=[ws])
                else:
                    S.dma("sp", ws[:, :, j * 128:j * 128 + ncol_], wv[:, :, c0:c0 + ncol_], w=[ws])
            nw = len(chs) * 128
            S.copy("pool", wbt[:, :, 0:nw], ws[:, :, 0:nw], r=[ws], w=[wbt])

        load_w(0)
        for gi in range(ngrp):
            chs = list(range(gi * 4, min(NCH, gi * 4 + 4)))
            wbt = wb[gi % 2]
            if gi + 1 < ngrp:
                load_w(gi + 1)
            for (t0, n) in tiles:
                for j, c in enumerate(chs):
                    nrow = 48 if c == CI_GR else CH[c][1]
                    p = pp[ev % 4]
                    sg = stg[ev % 4]
                    for k in range(8):
                        S.mm(p[0:nrow, 0:n], wbt[:, k, j * 128:j * 128 + nrow], hT[:, k, t0:t0 + n], start=(k == 0), stop=(k == 7), r=[wbt, hT], w=[p])
                    if c >= CI_MG:
                        gt_ = gst[ev % 4]
                        S.act(gt_[:, 0:n], p[:, 0:n], AF.Sigmoid, bias=bmc[:, c - CI_MG:c - CI_MG + 1], r=[p, bmc], w=[gt_])
                        S.dma("sp", g.GATE[(c - CI_MG) * 128:(c - CI_MG + 1) * 128, t0:t0 + n], gt_[:, 0:n], r=[gt_])
                        ev += 1
                        continue
                    if ev % 2 == 0:
                        S.copy("act", sg[0:nrow, 0:n], p[0:nrow, 0:n], r=[p], w=[sg])
                    else:
                        S.copy("dve", sg[0:nrow, 0:n], p[0:nrow, 0:n], r=[p], w=[sg])
                    S.dma("sp", g.P[c * 128:c * 128 + nrow, t0:t0 + n], sg[0:nrow, 0:n], r=[sg])
                    ev += 1


TILES = [(0, NCTX)] + [(NCTX + i * 512, 512) for i in range(8)]


def rows_view(ap, n, lo, hi):
    if n == NCTX:
        return ap[:, lo:n + hi]
    v = ap.rearrange("p (r c) -> p r c", c=64)
    return v[:, :, lo:64 + hi]


def stage_a(g, l, s):
    nc = g.nc
    with Stage(nc) as S:
        cw = S.sb([128, 4, 3])
        for k in range(3):
            S.dma("sp", cw[:, :, k], g.conv_a[l, k, :].rearrange("(c p) -> p c", p=128), w=[cw], allow_slow_non_contiguous=True)
        NB = 3
        tin = [[S.sb([128, 512]) for _ in range(4)] for _ in range(NB)]
        tu = [S.sb([128, 512]) for _ in range(NB)]
        ty = [S.sb([128, 512]) for _ in range(NB)]
        tsz = [S.sb([128, 512]) for _ in range(NB)]
        tout = [S.sb([128, 512], BF16) for _ in range(NB)]
        it = 0
        for (t0, n) in TILES:
            for j in range(4):
                ab, ac, ax, az = tin[it % NB]
                u, y, sz, yo = tu[it % NB], ty[it % NB], tsz[it % NB], tout[it % NB]
                for tl, ci in ((ab, CI_AB), (ac, CI_AC), (ax, CI_AX), (az, CI_AZ)):
                    S.dma("sp", tl[:, 0:n], g.P[(ci + j) * 128:(ci + j + 1) * 128, t0:t0 + n], w=[tl])
                S.tt("pool", u[:, 0:n], ac[:, 0:n], ax[:, 0:n], ALU.mult, r=[ac, ax], w=[u])
                S.act(y[:, 0:n], u[:, 0:n], AF.Copy, scale=cw[:, j, 1:2], r=[u, cw], w=[y])
                S.stt("dve", rows_view(y[:, 0:n], n, 1, 0), rows_view(u[:, 0:n], n, 0, -1), cw[:, j, 0:1], rows_view(y[:, 0:n], n, 1, 0),
                      ALU.mult, ALU.add, r=[u, y, cw], w=[y])
                S.stt("dve", rows_view(y[:, 0:n], n, 0, -1), rows_view(u[:, 0:n], n, 1, 0), cw[:, j, 2:3], rows_view(y[:, 0:n], n, 0, -1),
                      ALU.mult, ALU.add, r=[u, y, cw], w=[y])
                S.act(sz[:, 0:n], az[:, 0:n], AF.Silu, r=[az], w=[sz])
                S.tt("pool", u[:, 0:n], ab[:, 0:n], y[:, 0:n], ALU.mult, r=[ab, y], w=[u])
                S.tt("dve", yo[:, 0:n], u[:, 0:n], sz[:, 0:n], ALU.mult, r=[u, sz], w=[yo])
                S.dma("act", g.YA[j * 128:(j + 1) * 128, t0:t0 + n], yo[:, 0:n], r=[yo])
                it += 1


def stage_merge(g, l, s, last):
    nc = g.nc
    with Stage(nc) as S:
        wp = [S.sb([128, 4, D], BF16) for _ in range(3)]
        wo = S.sb([128, 8, D], BF16)
        wst = [S.sb([128, 4, D])] * 2
        bmc = S.sb([128, 24])
        S.dma("sp", bmc[:, :], g.b_merge[l, :].rearrange("(c p) -> p c", p=128), w=[bmc], allow_slow_non_contiguous=True)
        srcs = [g.w_pa, g.w_pb, g.w_pc]
        for br in range(3):
            S.dma("sp", wst[br % 2][:, :, :], srcs[br][l].rearrange("(k p) n -> p k n", p=128), w=[wst[br % 2]])
            S.copy("pool", wp[br][:, :, :], wst[br % 2][:, :, :], r=[wst[br % 2]], w=[wp[br]])
        for hf in range(2):
            S.dma("sp", wst[(hf + 1) % 2][:, :, :], g.w_o[l].rearrange("(k p) n -> p k n", p=128)[:, hf * 4:(hf + 1) * 4, :], w=[wst[(hf + 1) % 2]])
            S.copy("pool", wo[:, hf * 4:(hf + 1) * 4, :], wst[(hf + 1) % 2][:, :, :], r=[wst[(hf + 1) % 2]], w=[wo])
        if last:
            fnw = S.sb([128, D])
            S.dma("sp", fnw[:, :], g.fnw[0:1, :].partition_broadcast(128), w=[fnw])
            fst = S.sb([128, 8, 4])
        ysb = [[S.sb([128, 4, 512], BF16) for _ in range(3)]] * 2
        gt = [S.sb([128, 512], BF16) for _ in range(6)]
        tt_ = [S.sb([128, 512]) for _ in range(6)]
        mT = [S.sb([128, 8, 512], BF16) for _ in range(2)]
        xblk = [S.sb([128, D]) for _ in range(2)]
        xo = [S.sb([128, D]) for _ in range(2)]
        tmp = [S.sb([128, 512]) for _ in range(2)]
        junk = S.sb([128, D]) if last else None
        pb = [S.ps([128, 512]) for _ in range(6)]
        po = [S.ps([128, 512]) for _ in range(2)]
        ysrc = [g.YA, g.YB, g.YC]
        e3 = 0
        eo = 0
        for ti, (t0, n) in enumerate(TILES):
            if last and ti == 0:
                continue
            sidx = 2 if ti == 0 else s
            yt = ysb[ti % 2]
            m_t = mT[ti % 2]
            for br in range(3):
                S.dma("sp", yt[br][:, :, 0:n], ysrc[br][:, t0:t0 + n].rearrange("(j p) t -> p j t", p=128), w=[yt[br]])
            for oc in range(8):
                trip = []
                for br in range(3):
                    i6 = e3 % 6
                    e3 += 1
                    ci = br * 8 + oc
                    S.dma("sp", gt[i6][:, 0:n], g.GATE[ci * 128:(ci + 1) * 128, t0:t0 + n], w=[gt[i6]])
                    for j in range(4):
                        S.mm(pb[i6][:, 0:n], wp[br][:, j, oc * 128:(oc + 1) * 128], yt[br][:, j, 0:n], start=(j == 0), stop=(j == 3), r=[wp[br], yt[br]], w=[pb[i6]])
                    S.tt("dve", tt_[i6][:, 0:n], pb[i6][:, 0:n], gt[i6][:, 0:n], ALU.mult, r=[pb[i6], gt[i6]], w=[tt_[i6]])
                    trip.append(tt_[i6])
                S.tt("pool", trip[0][:, 0:n], trip[0][:, 0:n], trip[1][:, 0:n], ALU.add, r=[trip[0], trip[1]], w=[trip[0]])
                S.tt("pool", m_t[:, oc, 0:n], trip[0][:, 0:n], trip[2][:, 0:n], ALU.add, r=[trip[0], trip[2]], w=[m_t])
            for bi in range(n // 128):
                nblk = t0 // 128 + bi
                xb_ = xblk[eo % 2]
                xo_ = xo[eo % 2]
                eo += 1
                S.dma("sp", xb_[:, :], tok_src(g, l, s, nblk), w=[xb_])
                for cg in range(2):
                    p = po[cg]
                    for k in range(8):
                        S.mm(p[:, :], m_t[:, k, bi * 128:(bi + 1) * 128], wo[:, k, cg * 512:(cg + 1) * 512], start=(k == 0), stop=(k == 7), r=[m_t, wo], w=[p])
                    S.tt("dve", tmp[cg][:, :], p[:, :], g.Gb[:, sidx, cg * 512:(cg + 1) * 512], ALU.mult, r=[p, g.Gb], w=[tmp[cg]])
                    S.tt("pool", xo_[:, cg * 512:(cg + 1) * 512], tmp[cg][:, :], xb_[:, cg * 512:(cg + 1) * 512], ALU.add, r=[tmp[cg], xb_], w=[xo_])
                if not last:
                    dst = g.xrc[s, nblk * 128:(nblk + 1) * 128, :] if nblk < 2 else g.xrl[s, (nblk - 2) * 128:(nblk - 1) * 128, :]
                    S.dma("act", dst, xo_[:, :], r=[xo_])
                else:
                    q = eo % 8
                    S.act(junk[:, :], xo_[:, :], AF.Square, accum=fst[:, q, 0:1], r=[xo_], w=[junk, fst])
                    S.ts("dve", fst[:, q, 1:2], fst[:, q, 0:1], 1.0 / D, EPS, ALU.mult, ALU.add, r=[fst], w=[fst])
                    S.act(fst[:, q, 2:3], fst[:, q, 1:2], AF.Ln, r=[fst], w=[fst])
                    S.act(fst[:, q, 3:4], fst[:, q, 2:3], AF.Exp, scale=-0.5, r=[fst], w=[fst])
                    S.stt("dve", xb_[:, :], xo_[:, :], fst[:, q, 3:4], fnw[:, :], ALU.mult, ALU.mult, r=[xo_, fst, fnw], w=[xb_])
                    S.dma("act", g.out[s, (nblk - 2) * 128:(nblk - 1) * 128, :], xb_[:, :], r=[xb_])


def stage_bprep(g, l, s):
    nc = g.nc
    with Stage(nc) as S:
        cw = S.sb([128, 12, 3])
        for k in range(3):
            S.dma("sp", cw[:, :, k], g.conv_dn[l, k, :].rearrange("(c p) -> p c", p=128), w=[cw], allow_slow_non_contiguous=True)
        raw = [S.sb([128, 514]) for _ in range(3)]
        acc = [S.sb([128, 512]) for _ in range(3)]
        sil = [S.sb([128, 512]) for _ in range(8)]
        vout = [S.sb([128, 512], BF16) for _ in range(2)]
        sq = [S.sb([128, 512]) for _ in range(2)]
        lnt = [S.sb([128, 512]) for _ in range(2)]
        qo = [S.sb([128, 512], BF16) for _ in range(2)]
        pn = [S.ps([128, 512]) for _ in range(2)]
        onesb = g.cstb[:, C_ONES:C_ONES + 128]
        sqh = [S.sb([128, 512], BF16) for _ in range(2)]
        sql = [S.sb([128, 512], BF16) for _ in range(2)]
        it = 0
        for (t0, n) in TILES[int(os.environ.get('TLO', '0')):int(os.environ.get('THI', '9'))]:
            seq_lo, seq_hi = (0, NCTX) if t0 == 0 else (NCTX, T)
            for j in range(int(os.environ.get("JLO", "0")), 12):
                RS = int(os.environ.get("RING", "3"))
                rw = raw[it % RS]
                ac_ = acc[it % RS]
                it += 1
                zl = 1 if t0 - 1 < seq_lo else 0
                zr = 1 if t0 + n + 1 > seq_hi else 0
                if zl and not os.environ.get("NOMEMSET"):
                    S.memset("pool", rw[:, 0:1], 0.0, w=[rw])
                if zr and not os.environ.get("NOMEMSET"):
                    S.memset("pool", rw[:, n + 1:n + 2], 0.0, w=[rw])
                S.dma("sp", rw[:, zl:n + 2 - zr], g.P[(CI_Q + j) * 128:(CI_Q + j + 1) * 128, t0 - 1 + zl:t0 + n + 1 - zr], w=[rw])
                S.act(ac_[:, 0:n], rw[:, 1:n + 1], AF.Copy, scale=cw[:, j, 1:2], r=[rw, cw], w=[ac_])
                S.stt("dve", ac_[:, 0:n], rw[:, 0:n], cw[:, j, 0:1], ac_[:, 0:n], ALU.mult, ALU.add, r=[rw, ac_, cw], w=[ac_])
                S.stt("dve", ac_[:, 0:n], rw[:, 2:n + 2], cw[:, j, 2:3], ac_[:, 0:n], ALU.mult, ALU.add, r=[rw, ac_, cw], w=[ac_])
                if j < 8:
                    S.act(sil[j][:, 0:n], ac_[:, 0:n], AF.Silu, r=[ac_], w=[sil[j]])
                else:
                    vo = vout[j % (1 if os.environ.get('RING') == '1' else 2)]
                    S.act(vo[:, 0:n], ac_[:, 0:n], AF.Silu, r=[ac_], w=[vo])
                    S.dma("act", g.VT[(j - 8) * 128:(j - 7) * 128, t0:t0 + n], vo[:, 0:n], r=[vo])
            for j in range(0 if os.environ.get("NONORM") else 8):
                sq_ = sq[j % 2]
                ln_ = lnt[j % 2]
                q_ = qo[j % 2]
                p = pn[j % 2]
                NL = int(os.environ.get("NORMLVL", "9"))
                S.tt("pool", sq_[:, 0:n], sil[j][:, 0:n], sil[j][:, 0:n], ALU.mult, r=[sil[j]], w=[sq_])
                sh_, sl_ = sqh[j % 2], sql[j % 2]
                S.copy("pool", sh_[:, 0:n], sq_[:, 0:n], r=[sq_], w=[sh_])
                S.tt("pool", sl_[:, 0:n], sq_[:, 0:n], sh_[:, 0:n], ALU.subtract, r=[sq_, sh_], w=[sl_])
                if NL >= 2:
                    S.mm(p[:, 0:n], onesb, sh_[:, 0:n], start=True, stop=False, r=[g.cstb, sh_], w=[p])
                    S.mm(p[:, 0:n], onesb, sl_[:, 0:n], start=False, stop=True, r=[g.cstb, sl_], w=[p])
                if NL >= 3:
                    S.act(ln_[:, 0:n], p[:, 0:n], AF.Ln, bias=g.epsc[:, 0:1], r=[p, g.epsc], w=[ln_])
                if NL >= 4:
                    S.act(ln_[:, 0:n], ln_[:, 0:n], AF.Exp, scale=-0.5, r=[ln_], w=[ln_])
                if NL >= 5:
                    S.stt("dve", q_[:, 0:n], sil[j][:, 0:n], (128.0 ** -0.5) if j < 4 else 1.0, ln_[:, 0:n], ALU.mult, ALU.mult, r=[sil[j], ln_], w=[q_])
                dst = g.QT if j < 4 else g.KT
                S.dma("act", dst[(j % 4) * 128:(j % 4 + 1) * 128, t0:t0 + n], q_[:, 0:n], r=[q_])


class Sub:
    def __init__(self, tl):
        self.t = tl.t
        self.b = Buf()
        self.bank = tl.bank

    def __getitem__(self, idx):
        return self.t[idx]


ORD_F = list(range(NBLK))
ORD_B = [1, 0] + list(range(NBLK - 1, 1, -1))


def stage_bscan(g, l, s):
    nc = g.nc
    nsteps = int(os.environ.get("BSTEPS", str(NBLK)))
    with Stage(nc) as S:
        cst = g.cst
        CI = lambda o: cst[:, o:o + 128]
        ident, ones = CI(C_ID), CI(C_ONES)
        identb4 = S.sb([128, 4, 128], BF16)
        for q in range(4):
            S.copy("dve", identb4[:, q, :], ident, r=[cst], w=[identb4])
        gbr = S.sb([128, NBLK, 16])
        S.dma("sp", gbr[:, :, :], g.GB[:, :].rearrange("(n p) c -> p n c", p=128), w=[gbr])
        dtb = S.sb([128, 8])
        alog = S.sb([128, 8])
        S.dma("sp", dtb[:, :], g.dn_dt_bias[l:l + 1, :].partition_broadcast(128), w=[dtb])
        S.dma("sp", alog[:, :], g.dn_a_log[l:l + 1, :].partition_broadcast(128), w=[alog])
        nega = S.sb([128, 8])
        S.act(nega[:, :], alog[:, :], AF.Exp, r=[alog], w=[nega])
        S.ts("dve", nega[:, :], nega[:, :], -1.0, None, ALU.mult, r=[nega], w=[nega])
        t1 = S.sb([128, NBLK, 8])
        G = S.sb([128, NBLK, 8])
        NG = S.sb([128, NBLK, 8])
        BT = S.sb([128, NBLK, 8])
        NB = S.sb([128, NBLK, 8])
        for c in range(8):
            S.act(t1[:, :, c], gbr[:, :, c], AF.Exp, bias=dtb[:, c:c + 1], r=[gbr, dtb], w=[t1])
        S.act(t1[:, :, :], t1[:, :, :], AF.Ln, bias=1.0, r=[t1], w=[t1])
        for c in range(8):
            S.ts("dve", G[:, :, c], t1[:, :, c], nega[:, c:c + 1], None, ALU.mult, r=[t1, nega], w=[G])
        S.ts("dve", NG[:, :, :], G[:, :, :], -1.0, None, ALU.mult, r=[G], w=[NG])
        S.act(BT[:, :, :], gbr[:, :, 8:16], AF.Exp, scale=-1.0, r=[gbr], w=[BT])
        S.ts("dve", BT[:, :, :], BT[:, :, :], 1.0, None, ALU.add, r=[BT], w=[BT])
        S.P.op("dve", lambda e: e.reciprocal(out=BT[:, :, :], in_=BT[:, :, :]), [BT], [BT])
        S.ts("dve", NB[:, :, :], BT[:, :, :], -1.0, None, ALU.mult, r=[BT], w=[NB])
        banks = [S.ps([128, 512]) if bi_ not in (3, 4) else S.ps([128, 1024], BF16) for bi_ in range(8)]
        EX = S.sb([128, NBLK, 24])
        cstb = g.cstb
        CB = lambda o: cstb[:, o:o + 128]
        onesb = CB(C_ONES)
        for n in range(NBLK):
            pb = banks[n // 17]
            o = (n % 17) * 24
            S.mm(pb[:, o:o + 4], CI(C_TRIU), G[:, n, 0:4], r=[cst, G], w=[pb])
            S.mm(pb[:, o + 4:o + 8], CI(C_TRIL), G[:, n, 4:8], r=[cst, G], w=[pb])
            S.mm(pb[:, o + 8:o + 12], CI(C_TRILS), G[:, n, 0:4], r=[cst, G], w=[pb])
            S.mm(pb[:, o + 12:o + 16], CI(C_TRIUS), G[:, n, 4:8], r=[cst, G], w=[pb])
            S.mm(pb[:, o + 16:o + 24], ones, G[:, n, 0:8], r=[cst, G], w=[pb])
        GC = S.sb([128, NBLK, 8])
        GCh = S.sb([128, NBLK, 8], BF16)
        GChf = S.sb([128, NBLK, 8])
        GCl = S.sb([128, NBLK, 8])
        if not os.environ.get("NOGC"):
            for hb in range(2):
                for n_ in range(17):
                    S.copy("dve", GC[:, hb * 17 + n_, :], banks[hb][:, n_ * 24:n_ * 24 + 8], r=[banks[hb]], w=[GC])
            GL = int(os.environ.get("GCLVL", "9"))
            if GL >= 1:
                S.copy("dve", GCh[:, :, :], GC[:, :, :], r=[GC], w=[GCh])
            if GL >= 2:
                S.copy("dve", GChf[:, :, :], GCh[:, :, :], r=[GCh], w=[GChf])
            if GL >= 3:
                S.tt("dve", GCl[:, :, :], GC[:, :, :], GChf[:, :, :], ALU.subtract, r=[GC, GChf], w=[GCl])
        for hb in range(2):
            S.act(EX[:, hb * 17:(hb + 1) * 17, :].rearrange("p n c -> p (n c)"), banks[hb][:, 0:17 * 24], AF.Exp, r=[banks[hb], GC, GCl], w=[EX])
        KCO = S.sb([128, NBLK, 8])
        NEGC = S.sb([128, NBLK, 8])
        S.tt("dve", KCO[:, :, :], EX[:, :, 8:16], BT[:, :, :], ALU.mult, r=[EX, BT], w=[KCO])
        S.ts("dve", NEGC[:, :, :], EX[:, :, 0:8], -1.0, None, ALU.mult, r=[EX], w=[NEGC])
        onc = S.sb([128, 1])
        S.dma("sp", onc[:, :], g.dn_onorm[l, :].rearrange("(p o) -> p o", o=1), w=[onc], allow_slow_non_contiguous=True)

        if os.environ.get("BDBG"):
            for i_, tl_ in enumerate((G, BT, GC, GCl, KCO, NEGC)):
                S.dma("sp", g.dbg[0:128, i_ * 272:(i_ + 1) * 272], tl_[:, :, :].rearrange("p a b -> p (a b)"), r=[tl_])
            S.dma("sp", g.dbg[0:128, 6 * 272:6 * 272 + 816], EX[:, :, :].rearrange("p a b -> p (a b)"), r=[EX])
            return
        qT2 = S.sb([128, 2, T], BF16)
        kT2 = S.sb([128, 2, T], BF16)
        vT2 = S.sb([128, 2, T], BF16)
        otok = S.sb([128, NBLK, 2, 128])
        osub = [[Sub(otok) for _ in range(2)] for _ in range(NBLK)]
        Sf = [S.sb([128, 128]) for _ in range(4)]
        Sb = [S.sb([128, 128], BF16) for _ in range(4)]

        def t4(dt=BF16):
            t = S.sb([128, 4, 128], dt)
            return t, [Sub(t) for _ in range(4)]
        G1, G1s = t4()
        nG1, nG1s = t4()
        ETa, ETas = t4(F32)
        ET, ETs_ = t4(F32)
        ETm, ETms = t4(F32)
        NT, NTs = t4()
        AqT2 = [t4(), t4()]
        kgb2 = [t4(), t4()]
        vtk2 = [t4(), t4()]
        XTf2 = [t4(), t4()]
        rr, rrs = t4()
        yb, ybs = t4()
        Pp = [t4(), t4()]
        PQ = [t4(), t4()]
        PTp = [t4(), t4()]
        XTp = [t4(), t4()]
        XXp = [t4(), t4()]
        Dd, Dds = t4()
        DdT, DdTs = t4()
        C16, C16s = t4()
        C16T, C16Ts = t4()
        C32, C32s = t4()
        C32T, C32Ts = t4()
        C64, C64s = t4()
        M1, M1s = t4()
        M1T, M1Ts = t4()
        GB_ = [[banks[0], banks[1], banks[3], banks[5]], [banks[2], banks[6], banks[4], banks[7]]]
        ROLE = {"KK": (0, 0), "QK": (0, 256), "DT": (1, 0), "kt": (2, 0), "vt": (2, 256), "N0": (2, 512),
                "PP": (3, 0), "PT": (3, 256), "XX": (0, 0), "DTp": (0, 256),
                "ps1": (1, 0), "ps2": (1, 256), "ps3": (1, 0), "ps4": (1, 256), "ps5": (1, 0)}
        rsub = {}

        def R(role, d, hh):
            bi_, off = ROLE[role]
            tl_ = GB_[d][bi_]
            key = (d, bi_, off + hh * 128)
            if key not in rsub:
                rsub[key] = Sub(tl_)
            return tl_[:, off + hh * 128:off + (hh + 1) * 128], rsub[key]

        def RG(role, d):
            bi_, off = ROLE[role]
            tl_ = GB_[d][bi_]
            return tl_[:, off:off + 256], [R(role, d, 0)[1], R(role, d, 1)[1]]

        def g2(t_, d):
            return t_[:, 2 * d:2 * d + 2, :].rearrange("p a b -> p (a b)")
        sl = lambda q: slice(q * 128, (q + 1) * 128)
        zsrc = g.P
        MKg = lambda o: cstb[:, o:o + 256]
        identb2 = identb4[:, 0:2, :].rearrange("p a b -> p (a b)")


        def make_p4(grp, AqT, AqTs, kgb, kgbs, vtk, vtks, XT, XTs):
            def p4a():
                for (d, n, chs) in grp:
                    for (q, hh, c) in chs:
                        kTb = kT2[:, hh, n * 128:(n + 1) * 128]
                        a_, s_ = R("ps1", d, hh)
                        S.mm(a_, kTb, Sb[q][:, :], r=[kT2, Sb[q]], w=[s_])
                        S.stt("dve", rr[:, q, :], a_, NEGC[:, n, c:c + 1], vtk[:, q, :], ALU.mult, ALU.add, r=[s_, NEGC, vtks[q]], w=[rrs[q]])

            def p4b():
                for (d, n, chs) in grp:
                    qs = slice(2 * d, 2 * d + 2)
                    for (q, hh, c) in chs:
                        a_, s_ = R("ps2", d, hh)
                        S.mm(a_, XT[:, q, :], rr[:, q, :], r=[XTs[q], rrs[q]], w=[s_])
                    a_, ss_ = RG("ps2", d)
                    S.copy("act", g2(yb, d), a_, r=ss_, w=ybs[qs])

            def p4c():
                for (d, n, chs) in grp:
                    for (q, hh, c) in chs:
                        qTb = qT2[:, hh, n * 128:(n + 1) * 128]
                        a3, s3 = R("ps3", d, hh)
                        S.mm(a3, qTb, Sb[q][:, :], r=[qT2, Sb[q]], w=[s3])
                        a4, s4 = R("ps4", d, hh)
                        S.mm(a4, AqT[:, q, :], yb[:, q, :], r=[AqTs[q], ybs[q]], w=[s4])
                for (d, n, chs) in grp:
                    for (q, hh, c) in chs:
                        od = otok[:, n, hh, :]
                        ob = osub[n][hh]
                        a3, s3 = R("ps3", d, hh)
                        a4, s4 = R("ps4", d, hh)
                        if (n, hh) not in visited:
                            visited.add((n, hh))
                            S.ts("dve", od, a3, EX[:, n, c:c + 1], None, ALU.mult, r=[s3, EX], w=[ob])
                        else:
                            S.stt("dve", od, a3, EX[:, n, c:c + 1], od, ALU.mult, ALU.add, r=[s3, EX, ob], w=[ob])
                        S.tt("dve", od, a4, od, ALU.add, r=[s4, ob], w=[ob])

            def p4d():
                for (d, n, chs) in grp:
                    for (q, hh, c) in chs:
                        a5, s5 = R("ps5", d, hh)
                        S.mm(a5, kgb[:, q, :], yb[:, q, :], r=[kgbs[q], ybs[q]], w=[s5])
                for (d, n, chs) in grp:
                    for (q, hh, c) in chs:
                        a5, s5 = R("ps5", d, hh)
                        S.stt("dve", Sf[q][:, :], Sf[q][:, :], EX[:, n, 16 + c:17 + c], a5, ALU.mult, ALU.add, r=[Sf[q], EX, s5], w=[Sf[q]])
                        S.copy("pool", Sb[q][:, :], Sf[q][:, :], r=[Sf[q]], w=[Sb[q]])
            return [p4a, p4b, p4c, p4d]

        visited = set()
        for hp in range(2):
            for hh in range(2):
                h = 2 * hp + hh
                S.dma("sp", qT2[:, hh, :], g.QT[h * 128:(h + 1) * 128, :], w=[qT2])
                S.dma("sp", kT2[:, hh, :], g.KT[h * 128:(h + 1) * 128, :], w=[kT2])
                S.dma("sp", vT2[:, hh, :], g.VT[h * 128:(h + 1) * 128, :], w=[vT2])
            for q in range(4):
                S.memset("pool", Sf[q][:, :], 0.0, w=[Sf[q]])
                S.memset("pool", Sb[q][:, :], 0.0, w=[Sb[q]])
            visited.clear()
            if nsteps < NBLK:
                S.memset("pool", otok[:, :, :, :], 0.0, w=[osub[n_][h_] for n_ in range(NBLK) for h_ in range(2)])
            pend = []
            for i in range(nsteps):
                AqT, AqTs = AqT2[i % 2]
                kgb, kgbs = kgb2[i % 2]
                vtk, vtks = vtk2[i % 2]
                XTf, XTfs = XTf2[i % 2]
                grp = []
                for d in range(2):
                    n = (ORD_F if d == 0 else ORD_B)[i]
                    grp.append((d, n, [(d * 2 + hh, hh, d * 4 + 2 * hp + hh) for hh in range(2)]))
                for (d, n, chs) in grp:
                    for (q, hh, c) in chs:
                        kTb = kT2[:, hh, n * 128:(n + 1) * 128]
                        qTb = qT2[:, hh, n * 128:(n + 1) * 128]
                        vTb = vT2[:, hh, n * 128:(n + 1) * 128]
                        a_, s_ = R("kt", d, hh)
                        S.tr(a_, kTb, identb4[:, 0, :], r=[kT2, identb4], w=[s_])
                        a_, s_ = R("vt", d, hh)
                        S.tr(a_, vTb, identb4[:, 0, :], r=[vT2, identb4], w=[s_])
                        a_, s_ = R("KK", d, hh)
                        S.mm(a_, kTb, kTb, r=[kT2], w=[s_])
                        a_, s_ = R("QK", d, hh)
                        S.mm(a_, kTb, qTb, r=[kT2, qT2], w=[s_])
                        S.act(G1[:, q, :], identb4[:, 0, :], AF.Copy, scale=GChf[:, n, c:c + 1], r=[identb4, GChf], w=[G1s[q]])
                        S.act(nG1[:, q, :], identb4[:, 0, :], AF.Copy, scale=GCl[:, n, c:c + 1], r=[identb4, GCl], w=[nG1s[q]])
                        a_, s_ = R("DT", d, hh)
                        S.mm(a_, onesb, G1[:, q, :], start=True, stop=False, r=[cstb, G1s[q]], w=[s_])
                        S.mm(a_, onesb, nG1[:, q, :], start=False, stop=True, r=[cstb, nG1s[q]], w=[s_])
                for (d, n, chs) in grp:
                    neg = CI(C_NEGF) if d == 0 else CI(C_NEGB)
                    for (q, hh, c) in chs:
                        a_, s_ = R("DT", d, hh)
                        S.stt("dve", ETa[:, q, :], a_, GC[:, n, c:c + 1], neg, ALU.subtract, ALU.add, r=[s_, cst, GC], w=[ETas[q]])
                        S.act(ET[:, q, :], ETa[:, q, :], AF.Exp, r=[ETas[q]], w=[ETs_[q]])
                        S.tt("pool", ETm[:, q, :], ET[:, q, :], ident, ALU.subtract, r=[ETs_[q], cst], w=[ETms[q]])
                        a_, s_ = R("KK", d, hh)
                        S.stt("dve", NT[:, q, :], a_, NB[:, n, c:c + 1], ETm[:, q, :], ALU.mult, ALU.mult, r=[s_, NB, ETms[q]], w=[NTs[q]])
                        a_, s_ = R("QK", d, hh)
                        S.stt("dve", AqT[:, q, :], a_, BT[:, n, c:c + 1], ET[:, q, :], ALU.mult, ALU.mult, r=[s_, BT, ETs_[q]], w=[AqTs[q]])
                        a_, s_ = R("kt", d, hh)
                        S.ts("dve", kgb[:, q, :], a_, KCO[:, n, c:c + 1], None, ALU.mult, r=[s_, KCO], w=[kgbs[q]])
                    a_, ss_ = RG("vt", d)
                    S.copy("dve", g2(vtk, d), a_, r=ss_, w=vtks[2 * d:2 * d + 2])
                    for (q, hh, c) in chs:
                        a_, s_ = R("N0", d, hh)
                        S.tr(a_, NT[:, q, :], identb4[:, 0, :], r=[NTs[q], identb4], w=[s_])
                Nn, Nns = Pp[0]
                for (d, n, chs) in grp:
                    a_, ss_ = RG("N0", d)
                    S.copy("act", g2(Nn, d), a_, r=ss_, w=Nns[2 * d:2 * d + 2])
                for (d, n, chs) in grp:
                    qs = slice(2 * d, 2 * d + 2)
                    for (dst, dss, src, sss, mk) in ((Dd, Dds, Nn, Nns, C_MD16), (DdT, DdTs, NT, NTs, C_MD16), (C16, C16s, Nn, Nns, C_MS16),
                                                     (C16T, C16Ts, NT, NTs, C_MS16), (C32, C32s, Nn, Nns, C_MS32), (C32T, C32Ts, NT, NTs, C_MS32),
                                                     (C64, C64s, Nn, Nns, C_MS64)):
                        S.tt("pool", g2(dst, d), g2(src, d), MKg(mk), ALU.mult, r=sss[qs] + [cstb], w=dss[qs])
                    S.tt("dve", g2(XXp[0][0], d), g2(Dd, d), identb2, ALU.add, r=Dds[qs] + [identb4], w=XXp[0][1][qs])
                    S.tt("dve", g2(XTp[0][0], d), g2(DdT, d), identb2, ALU.add, r=DdTs[qs] + [identb4], w=XTp[0][1][qs])
                Xc, Xcs = XXp[0]
                XT, XTs = XTp[0]
                Pc, Pcs, PT, PTs = Dd, Dds, DdT, DdTs
                flip = 0
                for r_ in range(1, 4):
                    Pn, Pns = PQ[r_ % 2]
                    PTn, PTns = PTp[r_ % 2]
                    flip ^= 1
                    Xn, Xns = XXp[flip]
                    XTn, XTns = XTp[flip]
                    for (d, n, chs) in grp:
                        qs = slice(2 * d, 2 * d + 2)
                        for (q, hh, c) in chs:
                            a_, s_ = R("PP", d, hh)
                            S.mm(a_, PT[:, q, :], Pc[:, q, :], r=[PTs[q], Pcs[q]], w=[s_])
                            a_, s_ = R("PT", d, hh)
                            S.mm(a_, Pc[:, q, :], PT[:, q, :], r=[PTs[q], Pcs[q]], w=[s_])
                        a_, ss_ = RG("PP", d)
                        S.copy("act", g2(Pn, d), a_, r=ss_, w=Pns[qs])
                        a_, ss_ = RG("PT", d)
                        S.copy("act", g2(PTn, d), a_, r=ss_, w=PTns[qs])
                    for (d, n, chs) in grp:
                        qs = slice(2 * d, 2 * d + 2)
                        for (q, hh, c) in chs:
                            a_, s_ = R("XX", d, hh)
                            S.mm(a_, PTn[:, q, :], Xc[:, q, :], r=[PTns[q], Xcs[q]], w=[s_])
                            a_, s_ = R("DTp", d, hh)
                            S.mm(a_, Pn[:, q, :], XT[:, q, :], r=[Pns[q], XTs[q]], w=[s_])
                        a_, ss_ = RG("XX", d)
                        S.tt("dve", g2(Xn, d), a_, g2(Xc, d), ALU.add, r=ss_ + Xcs[qs], w=Xns[qs])
                        a_, ss_ = RG("DTp", d)
                        S.tt("dve", g2(XTn, d), a_, g2(XT, d), ALU.add, r=ss_ + XTs[qs], w=XTns[qs])
                    Pc, Pcs, PT, PTs = Pn, Pns, PTn, PTns
                    Xc, Xcs, XT, XTs = Xn, Xns, XTn, XTns
                    if pend:
                        pend.pop(0)()
                for (Cb, Cbs, CbT, CbTs, lastlvl) in ((C16, C16s, C16T, C16Ts, False), (C32, C32s, C32T, C32Ts, False), (C64, C64s, None, None, True)):
                    flip ^= 1
                    Xn, Xns = XXp[flip]
                    XTn, XTns = XTp[flip] if not lastlvl else (XTf, XTfs)
                    for (d, n, chs) in grp:
                        qs = slice(2 * d, 2 * d + 2)
                        for (q, hh, c) in chs:
                            if not lastlvl:
                                a_, s_ = R("PP", d, hh)
                                S.mm(a_, CbT[:, q, :], Xc[:, q, :], r=[CbTs[q], Xcs[q]], w=[s_])
                            a_, s_ = R("PT", d, hh)
                            S.mm(a_, Cb[:, q, :], XT[:, q, :], r=[Cbs[q], XTs[q]], w=[s_])
                        if not lastlvl:
                            a_, ss_ = RG("PP", d)
                            S.copy("act", g2(M1, d), a_, r=ss_, w=M1s[qs])
                        a_, ss_ = RG("PT", d)
                        S.copy("act", g2(M1T, d), a_, r=ss_, w=M1Ts[qs])
                    for (d, n, chs) in grp:
                        qs = slice(2 * d, 2 * d + 2)
                        for (q, hh, c) in chs:
                            if not lastlvl:
                                a_, s_ = R("XX", d, hh)
                                S.mm(a_, XT[:, q, :], M1[:, q, :], r=[XTs[q], M1s[q]], w=[s_])
                            a_, s_ = R("DTp", d, hh)
                            S.mm(a_, Xc[:, q, :], M1T[:, q, :], r=[Xcs[q], M1Ts[q]], w=[s_])
                        if not lastlvl:
                            a_, ss_ = RG("XX", d)
                            S.tt("dve", g2(Xn, d), a_, g2(Xc, d), ALU.add, r=ss_ + Xcs[qs], w=Xns[qs])
                        a_, ss_ = RG("DTp", d)
                        S.tt("dve", g2(XTn, d), a_, g2(XT, d), ALU.add, r=ss_ + XTs[qs], w=XTns[qs])
                    Xc, Xcs, XT, XTs = Xn, Xns, XTn, XTns
                    if pend:
                        pend.pop(0)()
                while pend:
                    pend.pop(0)()
                pend = make_p4(grp, AqT, AqTs, kgb, kgbs, vtk, vtks, XTf, XTfs)
            while pend:
                pend.pop(0)()
            st_ = S.sb([128, NBLK, 2, 4])
            junk = S.sb([128, 128])
            for n in range(NBLK):
                for hh in range(2):
                    S.act(junk[:, :], otok[:, n, hh, :], AF.Square, accum=st_[:, n, hh, 0:1], r=[osub[n][hh]], w=[junk, st_])
            S.ts("dve", st_[:, :, :, 1], st_[:, :, :, 0], 1.0 / 128, EPS, ALU.mult, ALU.add, r=[st_], w=[st_])
            S.act(st_[:, :, :, 2], st_[:, :, :, 1], AF.Ln, r=[st_], w=[st_])
            S.act(st_[:, :, :, 3], st_[:, :, :, 2], AF.Exp, scale=-0.5, r=[st_], w=[st_])
            zt = [S.sb([128, 512]) for _ in range(2)]
            yt = [S.sb([128, 512]) for _ in range(2)]
            yo = [S.sb([128, 512], BF16) for _ in range(2)]
            ev = 0
            for hh in range(2):
                h = 2 * hp + hh
                for (t0, n_) in TILES:
                    nb_ = n_ // 128
                    bk = banks[(ev % 2) * 2]
                    for bi in range(nb_):
                        n = t0 // 128 + bi
                        S.act(otok[:, n, hh, :], otok[:, n, hh, :], AF.Copy, scale=st_[:, n, hh, 3:4], r=[osub[n][hh], st_], w=[osub[n][hh]])
                        S.tr(bk[:, bi * 128:(bi + 1) * 128], otok[:, n, hh, :], ident, r=[osub[n][hh], cst], w=[bk])
                    z_, y_, o_ = zt[ev % 2], yt[ev % 2], yo[ev % 2]
                    ev += 1
                    S.dma("sp", z_[:, 0:n_], zsrc[(CI_DZ + h) * 128:(CI_DZ + h + 1) * 128, t0:t0 + n_], w=[z_])
                    S.act(y_[:, 0:n_], bk[:, 0:n_], AF.Copy, scale=onc[:, 0:1], r=[bk, onc], w=[y_])
                    S.act(z_[:, 0:n_], z_[:, 0:n_], AF.Silu, r=[z_], w=[z_])
                    S.tt("dve", o_[:, 0:n_], y_[:, 0:n_], z_[:, 0:n_], ALU.mult, r=[y_, z_], w=[o_])
                    S.dma("act", g.YB[h * 128:(h + 1) * 128, t0:t0 + n_], o_[:, 0:n_], r=[o_])


def cm_view(ap):
    return ap.rearrange("p (r c) -> p c r", c=64)


def stage_c(g, l, s):
    nc = g.nc
    nsteps = int(os.environ.get("CSTEPS", str(NBLK)))
    with Stage(nc) as S:
        cst, cstb = g.cst, g.cstb
        CI = lambda o: cst[:, o:o + 128]
        CB = lambda o: cstb[:, o:o + 128]
        ident, identb = CI(C_ID), CB(C_ID)
        stag = S.sb([128, T])
        qcm = S.sb([128, T])
        kcm = S.sb([128, T])
        vcm = S.sb([128, 2, T], BF16)
        rcm = S.sb([64, T], BF16)
        otok = S.sb([128, NBLK, 2, 128])
        osub = [[Sub(otok) for _ in range(2)] for _ in range(NBLK)]
        ycm = S.sb([128, T])
        wr32 = S.sb([64, 256])
        wrb = S.sb([64, 256], BF16)
        onc = S.sb([128, 1])
        S.dma("sp", onc[:, :], g.gla_onorm[l, :].rearrange("(p o) -> p o", o=1), w=[onc], allow_slow_non_contiguous=True)
        S.memset("pool", wr32[:, :], 0.0, w=[wr32])
        for d in range(2):
            S.dma("sp", wr32[32 * d:32 * d + 16, :], g.gla_w_r2[l, d, :, :], w=[wr32])
            S.dma("sp", wr32[32 * d + 16:32 * d + 17, :], g.gla_b_r[l, d:d + 1, :], w=[wr32])
        S.copy("dve", wrb[:, :], wr32[:, :], r=[wr32], w=[wrb])
        S.dma("sp", stag[0:48, :], g.P[CI_GR * 128:CI_GR * 128 + 48, :], w=[stag])
        S.memset("pool", rcm[:, :], 1.0, w=[rcm])
        for d in range(2):
            S.copy("dve", rcm[32 * d:32 * d + 16, 0:NCTX], stag[32 * d:32 * d + 16, 0:NCTX], r=[stag], w=[rcm])
            S.copy("dve", rcm[32 * d:32 * d + 16, NCTX:T].rearrange("p (c r) -> p c r", r=64), cm_view(stag[32 * d:32 * d + 16, NCTX:T]), r=[stag], w=[rcm])
        Sf = [S.sb([128, 128]) for _ in range(2)]
        Sb = [S.sb([128, 128], BF16) for _ in range(2)]
        bL = [S.ps([128, 512]) for _ in range(2)]
        bV = [S.ps([128, 1024], BF16) for _ in range(2)]
        bA = [S.ps([128, 512]) for _ in range(2)]
        bK = S.ps([128, 512])
        bKs = [Sub(bK) for _ in range(2)]

        def pd(shape, dt=F32):
            return [S.sb(shape, dt) for _ in range(2)]
        et, lt = pd([128, 128]), pd([128, 128])
        lh, lhf, ll = pd([128, 128], BF16), pd([128, 128]), pd([128, 128], BF16)
        eq, ek, ekg = pd([128, 128]), pd([128, 128]), pd([128, 128])
        qg, qgpad, kinvpad, kgpad = pd([128, 128], BF16), pd([128, 2, 128], BF16), pd([128, 2, 128], BF16), pd([128, 2, 128], BF16)
        vtok = pd([128, 2, 128], BF16)
        atm = pd([128, 2, 128], BF16)
        for d in range(2):
            S.memset("pool", qgpad[d][:, :, :], 0.0, w=[qgpad[d]])
            S.memset("pool", kinvpad[d][:, :, :], 0.0, w=[kinvpad[d]])
            S.memset("pool", kgpad[d][:, :, :], 0.0, w=[kgpad[d]])
        ISC = 1.0 / 16.0

        for hp in range(2):
            for (dst, ci) in ((qcm, CI_GQ + hp), (kcm, CI_GK + hp)):
                S.dma("sp", stag[:, :], g.P[ci * 128:(ci + 1) * 128, :], w=[stag])
                S.copy("dve", dst[:, 0:NCTX], stag[:, 0:NCTX], r=[stag], w=[dst])
                S.copy("dve", dst[:, NCTX:T].rearrange("p (c r) -> p c r", r=64), cm_view(stag[:, NCTX:T]), r=[stag], w=[dst])
            for hh in range(2):
                ci = CI_GV + 2 * hp + hh
                S.dma("sp", stag[:, :], g.P[ci * 128:(ci + 1) * 128, :], w=[stag])
                S.copy("pool", vcm[:, hh, 0:NCTX], stag[:, 0:NCTX], r=[stag], w=[vcm])
                S.copy("pool", vcm[:, hh, NCTX:T].rearrange("p (c r) -> p c r", r=64), cm_view(stag[:, NCTX:T]), r=[stag], w=[vcm])
            for d in range(2):
                S.memset("pool", Sf[d][:, :], 0.0, w=[Sf[d]])
                S.memset("pool", Sb[d][:, :], 0.0, w=[Sb[d]])
            if nsteps < NBLK:
                S.memset("pool", otok[:, :, :, :], 0.0, w=[osub[n_][h_] for n_ in range(NBLK) for h_ in range(2)])
            visited = set()
            for i in range(nsteps):
                blks = [(0, ORD_F[i]), (1, ORD_B[i])]
                for (d, n) in blks:
                    bs = slice(n * 128, (n + 1) * 128)
                    S.mm(bL[d][:, 0:128], rcm[32 * d:32 * d + 17, bs], wrb[32 * d:32 * d + 17, hp * 128:(hp + 1) * 128], r=[rcm, wrb], w=[bL[d]])
                    S.tr(bL[d][:, 128:256], kcm[:, bs], ident, r=[kcm, cst], w=[bL[d]])
                    for hh in range(2):
                        S.tr(bV[d][:, hh * 128:(hh + 1) * 128], vcm[:, hh, bs], identb, r=[vcm, cstb], w=[bV[d]])
                for (d, n) in blks:
                    S.act(et[d][:, :], bL[d][:, 0:128], AF.Exp, scale=-1.0, r=[bL[d]], w=[et[d]])
                    S.act(lt[d][:, :], et[d][:, :], AF.Ln, bias=1.0, r=[et[d]], w=[lt[d]])
                    S.copy("dve", lh[d][:, :], lt[d][:, :], r=[lt[d]], w=[lh[d]])
                    S.copy("dve", lhf[d][:, :], lh[d][:, :], r=[lh[d]], w=[lhf[d]])
                    S.tt("dve", ll[d][:, :], lt[d][:, :], lhf[d][:, :], ALU.subtract, r=[lt[d], lhf[d]], w=[ll[d]])
                    S.copy("act", vtok[d][:, :, :].rearrange("p a b -> p (a b)"), bV[d][:, 0:256], r=[bV[d]], w=[vtok[d]])
                for (d, n) in blks:
                    tri = CB(C_TRIU) if d == 0 else CB(C_TRIL)
                    tris = CB(C_TRILS) if d == 0 else CB(C_TRIUS)
                    S.mm(bL[d][:, 384:512], lh[d][:, :], tri, start=True, stop=False, r=[lh[d], cstb], w=[bL[d]])
                    S.mm(bL[d][:, 384:512], ll[d][:, :], tri, start=False, stop=True, r=[ll[d], cstb], w=[bL[d]])
                    S.mm(bL[d][:, 256:384], tris, lh[d][:, :], start=True, stop=False, r=[lh[d], cstb], w=[bL[d]])
                    S.mm(bL[d][:, 256:384], tris, ll[d][:, :], start=False, stop=True, r=[ll[d], cstb], w=[bL[d]])
                for (d, n) in blks:
                    bs = slice(n * 128, (n + 1) * 128)
                    S.act(eq[d][:, :], bL[d][:, 384:512], AF.Exp, scale=-ISC, r=[bL[d]], w=[eq[d]])
                    S.act(ek[d][:, :], bL[d][:, 384:512], AF.Exp, scale=ISC, r=[bL[d]], w=[ek[d]])
                    S.act(ekg[d][:, :], bL[d][:, 256:384], AF.Exp, scale=-ISC, r=[bL[d]], w=[ekg[d]])
                    S.stt("dve", qg[d][:, :], qcm[:, bs], 0.125, eq[d][:, :], ALU.mult, ALU.mult, r=[qcm, eq[d]], w=[qg[d]])
                    for hh in range(2):
                        ps_ = slice(64 * hh, 64 * hh + 64)
                        S.copy("pool", qgpad[d][ps_, hh, :], qg[d][ps_, :], r=[qg[d]], w=[qgpad[d]])
                        S.tt("pool", kinvpad[d][ps_, hh, :], kcm[ps_, bs], ek[d][ps_, :], ALU.mult, r=[kcm, ek[d]], w=[kinvpad[d]])
                        S.tt("dve", kgpad[d][:, hh, 64 * hh:64 * hh + 64], bL[d][:, 128 + 64 * hh:128 + 64 * hh + 64], ekg[d][:, 64 * hh:64 * hh + 64], ALU.mult,
                             r=[bL[d], ekg[d]], w=[kgpad[d]])
                for (d, n) in blks:
                    for hh in range(2):
                        S.mm(bA[d][:, hh * 128:(hh + 1) * 128], kinvpad[d][:, hh, :], qg[d][:, :], r=[kinvpad[d], qg[d]], w=[bA[d]])
                    msk = CI(C_TRIU) if d == 0 else CI(C_TRIL)
                    for hh in range(2):
                        S.tt("dve", atm[d][:, hh, :], bA[d][:, hh * 128:(hh + 1) * 128], msk, ALU.mult, r=[bA[d], cst], w=[atm[d]])
                    for hh in range(2):
                        S.mm(bA[d][:, 256 + hh * 128:256 + (hh + 1) * 128], qgpad[d][:, hh, :], Sb[d][:, :], start=True, stop=False, r=[qgpad[d], Sb[d]], w=[bA[d]])
                        S.mm(bA[d][:, 256 + hh * 128:256 + (hh + 1) * 128], atm[d][:, hh, :], vtok[d][:, hh, :], start=False, stop=True, r=[atm[d], vtok[d]], w=[bA[d]])
                    for hh in range(2):
                        S.mm(bK[:, d * 128:(d + 1) * 128], kgpad[d][:, hh, :], vtok[d][:, hh, :], start=(hh == 0), stop=(hh == 1), r=[kgpad[d], vtok[d]], w=[bKs[d]])
                    for hh in range(2):
                        od = otok[:, n, hh, :]
                        ob = osub[n][hh]
                        src = bA[d][:, 256 + hh * 128:256 + (hh + 1) * 128]
                        if (n, hh) not in visited:
                            visited.add((n, hh))
                            S.copy("act", od, src, r=[bA[d]], w=[ob])
                        else:
                            S.tt("dve", od, src, od, ALU.add, r=[bA[d], ob], w=[ob])
                    gl = eq[d][:, 127:128] if d == 0 else eq[d][:, 0:1]
                    S.stt("dve", Sf[d][:, :], Sf[d][:, :], gl, bK[:, d * 128:(d + 1) * 128], ALU.mult, ALU.add, r=[Sf[d], eq[d], bKs[d]], w=[Sf[d]])
                    S.copy("pool", Sb[d][:, :], Sf[d][:, :], r=[Sf[d]], w=[Sb[d]])
            st_ = S.sb([128, NBLK, 2, 4])
            junk = S.sb([128, 128])
            for n in range(NBLK):
                for hh in range(2):
                    S.act(junk[:, :], otok[:, n, hh, :], AF.Square, accum=st_[:, n, hh, 0:1], r=[osub[n][hh]], w=[junk, st_])
            S.ts("dve", st_[:, :, :, 1], st_[:, :, :, 0], 1.0 / 128, EPS, ALU.mult, ALU.add, r=[st_], w=[st_])
            S.act(st_[:, :, :, 2], st_[:, :, :, 1], AF.Ln, r=[st_], w=[st_])
            S.act(st_[:, :, :, 3], st_[:, :, :, 2], AF.Exp, scale=-0.5, r=[st_], w=[st_])
            yo = S.sb([128, T], BF16)
            for hh in range(2):
                h = 2 * hp + hh
                for g4 in range(0, NBLK, 4):
                    nb_ = min(4, NBLK - g4)
                    bk = bL[(g4 // 4) % 2]
                    for bi in range(nb_):
                        n = g4 + bi
                        S.act(otok[:, n, hh, :], otok[:, n, hh, :], AF.Copy, scale=st_[:, n, hh, 3:4], r=[osub[n][hh], st_], w=[osub[n][hh]])
                        S.tr(bk[:, bi * 128:(bi + 1) * 128], otok[:, n, hh, :], ident, r=[osub[n][hh], cst], w=[bk])
                    S.act(ycm[:, g4 * 128:(g4 + nb_) * 128], bk[:, 0:nb_ * 128], AF.Copy, scale=onc[:, 0:1], r=[bk, onc], w=[ycm])
                S.dma("sp", stag[:, :], g.P[(CI_GZ + h) * 128:(CI_GZ + h + 1) * 128, :], w=[stag])
                S.act(stag[:, :], stag[:, :], AF.Silu, r=[stag], w=[stag])
                S.tt("dve", yo[:, 0:NCTX], ycm[:, 0:NCTX], stag[:, 0:NCTX], ALU.mult, r=[ycm, stag], w=[yo])
                S.tt("dve", yo[:, NCTX:T].rearrange("p (r c) -> p r c", c=64), ycm[:, NCTX:T].rearrange("p (c r) -> p r c", r=64),
                     stag[:, NCTX:T].rearrange("p (r c) -> p r c", c=64), ALU.mult, r=[ycm, stag], w=[yo])
                S.dma("act", g.YC[h * 128:(h + 1) * 128, :], yo[:, :], r=[yo])


def build_nc(nlayer=4, nseq=2, debug=None, dbg_what=None):
    nc = bass.Bass("TRN2", target_bir_lowering=False)
    g = Ctx()
    g.nc = nc
    dt = lambda name, shape, kind="ExternalInput", dtype=F32: nc.dram_tensor(name, list(shape), dtype, kind=kind).ap()
    g.x_in = dt("x", [2, NLAT, D])
    g.ctx_in = dt("ctx", [2, NCTX, D])
    g.c_in = dt("c", [2, D])
    g.cctx_in = dt("c_ctx", [1, D])
    g.w_mod = dt("w_mod", [4, D, 3 * D])
    g.b_mod = dt("b_mod", [4, 3 * D])
    g.norm_w = dt("norm_w", [4, D])
    g.w_in = dt("w_in", [4, D, INW])
    g.b_merge = dt("b_merge", [4, 3 * D])
    g.conv_a = dt("conv_a", [4, 3, 512])
    g.conv_dn = dt("conv_dn", [4, 3, 1536])
    g.dn_a_log = dt("dn_a_log", [4, 8])
    g.dn_dt_bias = dt("dn_dt_bias", [4, 8])
    g.dn_onorm = dt("dn_onorm", [4, 128])
    g.gla_w_r2 = dt("gla_w_r2", [4, 2, 16, 256])
    g.gla_b_r = dt("gla_b_r", [4, 2, 256])
    g.gla_onorm = dt("gla_onorm", [4, 128])
    g.w_pa = dt("w_pa", [4, 512, D])
    g.w_pb = dt("w_pb", [4, 512, D])
    g.w_pc = dt("w_pc", [4, 512, D])
    g.w_o = dt("w_o", [4, D, D])
    g.fnw = dt("final_norm_w", [1, D])
    g.cst_in = dt("cst", [128, NCST])
    g.out = dt("out", [2, NLAT, D], kind="ExternalOutput")
    import os
    g.P = nc.dram_tensor("P_scr", [(1 if os.environ.get("SMALLP") else NCH) * 128, T], F32).ap()
    g.GB = nc.dram_tensor("GB_scr", [T, 16], F32).ap()
    g.xrl = nc.dram_tensor("xrl_scr", [2, NLAT, D], F32).ap()
    g.xrc = nc.dram_tensor("xrc_scr", [2, NCTX, D], F32).ap()
    g.GATE = nc.dram_tensor("GATE_scr", [3072, T], BF16).ap()
    g.YA = nc.dram_tensor("YA_scr", [512, T], BF16).ap()
    g.YB = nc.dram_tensor("YB_scr", [512, T], BF16).ap()
    g.YC = nc.dram_tensor("YC_scr", [512, T], BF16).ap()
    g.QT = nc.dram_tensor("QT_scr", [512, T], BF16).ap()
    g.KT = nc.dram_tensor("KT_scr", [512, T], BF16).ap()
    g.VT = nc.dram_tensor("VT_scr", [512, T], BF16).ap()
    if debug is not None:
        g.dbg = dt("dbg", debug, kind="ExternalOutput", dtype=(BF16 if ((dbg_what in ("a", "bprep", "b", "c") or os.environ.get("DBGBF16")) and not os.environ.get("DBGF32")) else F32))

    with contextlib.ExitStack() as top:
        def psb(name, shape, dtype=F32):
            return TL(top.enter_context(nc.sbuf_tensor(name, list(shape), dtype)))
        g.cst = psb("cst_sb", [128, NCST])
        g.modcol = psb("modcol", [128, 24, 3])
        g.A1 = psb("A1", [128, 8, 3])
        g.Gb = psb("Gb", [128, 3, D])
        g.epsc = psb("epsc", [128, 2])
        g.cstb = psb("cstb_sb", [128, NCST], BF16)
        with Stage(nc) as S:
            S.dma("sp", g.cst[:, :], g.cst_in, w=[g.cst])
            S.memset("pool", g.epsc[:, :], EPS, w=[g.epsc])
            S.copy("dve", g.cstb[:, :], g.cst[:, :], r=[g.cst], w=[g.cstb])
        def dump2d(src, rows, dtype_rows=128):
            with Stage(nc) as S:
                for r0 in range(0, rows, 128):
                    S.dma("sp", g.dbg[r0:r0 + 128, :], src[r0:r0 + 128, :])
        for l in range(nlayer):
            stage_mod(g, l)
            if dbg_what == "mod":
                with Stage(nc) as S:
                    S.dma("sp", g.dbg[:, 0:72], g.modcol[:, :, :].rearrange("p a b -> p (a b)"), r=[g.modcol])
                    S.dma("sp", g.dbg[:, 72:96], g.A1[:, :, :].rearrange("p a b -> p (a b)"), r=[g.A1])
                    S.dma("sp", g.dbg[:, 96:96 + 3072], g.Gb[:, :, :].rearrange("p a b -> p (a b)"), r=[g.Gb])
                return nc
            for s in range(nseq):
                stage_proj(g, l, s)
                if dbg_what == "proj":
                    dump2d(g.P, NCH * 128)
                    return nc
                stage_a(g, l, s)
                if dbg_what == "hook":
                    return nc
                if dbg_what == "proj2":
                    dump2d(g.P, NCH * 128)
                    return nc
                if dbg_what == "a":
                    dump2d(g.YA, 512)
                    return nc
                stage_bprep(g, l, s)
                if dbg_what == "bprep":
                    dump2d(g.QT, 512)
                    with Stage(nc) as S:
                        for r0 in range(0, 512, 128):
                            S.dma("sp", g.dbg[512 + r0:512 + r0 + 128, :], g.KT[r0:r0 + 128, :])
                            S.dma("sp", g.dbg[1024 + r0:1024 + r0 + 128, :], g.VT[r0:r0 + 128, :])
                    return nc
                stage_bscan(g, l, s)
                if os.environ.get("BDBG"):
                    return nc
                if dbg_what == "b":
                    dump2d(g.YB, 512)
                    return nc
                stage_c(g, l, s)
                if dbg_what == "c":
                    dump2d(g.YC, 512)
                    return nc
                stage_merge(g, l, s, last=(l == nlayer - 1) and not os.environ.get("MERGE_NOTLAST"))
                if dbg_what == "merge":
                    with Stage(nc) as S:
                        for r0 in range(0, NLAT, 128):
                            S.dma("sp", g.dbg[r0:r0 + 128, :], g.xrl[s, r0:r0 + 128, :])
                        for r0 in range(0, NCTX, 128):
                            S.dma("sp", g.dbg[NLAT + r0:NLAT + r0 + 128, :], g.xrc[s, r0:r0 + 128, :])
                    return nc
    return nc


_NC_CACHE = {}


def kernel(**inputs):
    n = 8
    if "nc" not in _NC_CACHE:
        _NC_CACHE["nc"] = build_nc()
    nc = _NC_CACHE["nc"]
    cst = make_consts()
    f = lambda k: np.ascontiguousarray(np.asarray(inputs[k], dtype=np.float32))
    shared = {
        "c_ctx": f("c_ctx").reshape(1, D), "w_mod": f("w_mod"), "b_mod": f("b_mod"), "norm_w": f("norm_w"),
        "w_in": f("w_in"), "b_merge": f("b_merge"), "conv_a": f("conv_a"), "conv_dn": f("conv_dn"),
        "dn_a_log": f("dn_a_log").reshape(4, 8), "dn_dt_bias": f("dn_dt_bias").reshape(4, 8),
        "dn_onorm": f("dn_onorm"), "gla_w_r2": f("gla_w_r2"), "gla_b_r": f("gla_b_r"), "gla_onorm": f("gla_onorm"),
        "w_pa": f("w_pa"), "w_pb": f("w_pb"), "w_pc": f("w_pc"), "w_o": f("w_o"),
        "final_norm_w": f("final_norm_w").reshape(1, D), "cst": cst,
    }
    x = f("x")
    c = f("c")
    ctx = f("ctx")
    in_maps = []
    for i in range(n):
        m = dict(shared)
        m["x"] = x[2 * i:2 * i + 2]
        m["c"] = c[2 * i:2 * i + 2]
        m["ctx"] = ctx[2 * i:2 * i + 2]
        in_maps.append(m)
    res = run_bass_kernel_spmd(nc, in_maps, core_ids=list(range(n)))
    return np.concatenate([r["out"] for r in res.results], axis=0)
```
